# Optimizing a Trainium2 kernel written in Bass

```python
import math
import jax
import jax.numpy as jnp
from jax import lax
import numpy as np

D_MODEL = 1024
BATCH = 16
SEQ = 2048
DEPTH = 1

PLE_DIM = 256
SSM_D_INNER = 1024
SSM_HEAD_DIM = 64
SSM_HEADS = SSM_D_INNER // SSM_HEAD_DIM
SSM_GROUPS = 4
SSM_STATE = 128
SSM_CONV = 4
SSM_CHUNK = 128
SSM_XBC_DIM = SSM_D_INNER + 2 * SSM_GROUPS * SSM_STATE
LSTM_HEADS = 8
LSTM_QK_DIM = 64
LSTM_V_DIM = 128
LSTM_D_QK = LSTM_HEADS * LSTM_QK_DIM
LSTM_D_V = LSTM_HEADS * LSTM_V_DIM
LSTM_CHUNK = 128
MOE_GROUPS = 8
MOE_EXPERTS_PER_GROUP = 8
MOE_EXPERTS = MOE_GROUPS * MOE_EXPERTS_PER_GROUP
MOE_TOP_K = 2
MOE_D_FF = 512
MOE_BLOCK = 256
NORM_EPS = 1e-5
DEEPNORM_ALPHA = (2.0 * DEPTH) ** 0.25
DEEPNORM_BETA = (8.0 * DEPTH) ** -0.25
IN_PROJ_SIZES = (SSM_D_INNER, SSM_XBC_DIM, SSM_HEADS,
                 LSTM_D_QK, LSTM_D_QK, LSTM_D_V, LSTM_D_V, LSTM_HEADS, LSTM_HEADS,
                 D_MODEL, D_MODEL)
IN_PROJ_DIM = sum(IN_PROJ_SIZES)
IN_PROJ_SPLITS = tuple(int(v) for v in np.cumsum(IN_PROJ_SIZES)[:-1])

kernel_name = 'hybrid_ssd_mlstm_hmoe_deepnorm_block'


def layer_norm(x, g, b):
    xf = x.astype(jnp.float32)
    mu = jnp.mean(xf, -1, keepdims=True)
    xc = xf - mu
    var = jnp.mean(xc * xc, -1, keepdims=True)
    y = xc * lax.rsqrt(var + NORM_EPS) * g.astype(jnp.float32) + b.astype(jnp.float32)
    return y.astype(x.dtype)


def group_rms_norm(x, w, n_groups):
    shp = x.shape
    xg = x.reshape(shp[:-1] + (n_groups, shp[-1] // n_groups))
    xg = xg * lax.rsqrt(jnp.mean(xg * xg, -1, keepdims=True) + NORM_EPS)
    return xg.reshape(shp) * w.astype(jnp.float32)


def causal_depthwise_conv(u, w, b):
    y = lax.conv_general_dilated(u, w[:, None, :].astype(u.dtype), window_strides=(1,),
                                 padding=((SSM_CONV - 1, 0),),
                                 dimension_numbers=('NWC', 'WIO', 'NWC'),
                                 feature_group_count=u.shape[-1])
    return y + b.astype(u.dtype)


def ssd_chunked(xh, dt, a, bm, cm):
    b, s, h, p = xh.shape
    g, n = bm.shape[2], bm.shape[3]
    j = h // g
    L = SSM_CHUNK
    c = s // L
    xdt = (xh * dt[..., None]).reshape(b, c, L, g, j, p)
    a_cs = jnp.cumsum(jnp.moveaxis((dt * a).reshape(b, c, L, g, j), 2, -1), axis=-1)
    bc = bm.reshape(b, c, L, g, n)
    cc = cm.reshape(b, c, L, g, n)
    causal = jnp.tril(jnp.ones((L, L), bool))
    decay = jnp.exp(jnp.where(causal, a_cs[..., :, None] - a_cs[..., None, :], -jnp.inf))
    cb = jnp.einsum('bclgn,bcsgn->bcgls', cc, bc)
    y_diag = jnp.einsum('bcgjls,bcsgjp->bclgjp', cb[:, :, :, None] * decay, xdt)
    decay_to_end = jnp.exp(a_cs[..., -1:] - a_cs)
    chunk_states = jnp.einsum('bcsgn,bcgjs,bcsgjp->bcgjpn', bc, decay_to_end, xdt)
    chunk_decay = jnp.exp(a_cs[..., -1])

    def chunk_step(state, inp):
        st, dec = inp
        return state * dec[..., None, None] + st, state

    init = jnp.zeros((b, g, j, p, n), xh.dtype)
    _, prev = lax.scan(chunk_step, init, (jnp.moveaxis(chunk_states, 1, 0), jnp.moveaxis(chunk_decay, 1, 0)))
    prev = jnp.moveaxis(prev, 0, 1)
    y_off = jnp.einsum('bclgn,bcgjpn,bcgjl->bclgjp', cc, prev, jnp.exp(a_cs))
    return (y_diag + y_off).reshape(b, s, h, p)


def mamba2_mixer(z, xbc, dt_raw, conv_w, conv_b, dt_bias, a_log, d_skip, norm_w):
    f32 = jnp.float32
    b, s, _ = z.shape
    xbc = jax.nn.silu(causal_depthwise_conv(xbc, conv_w, conv_b)).astype(f32)
    xs, bm, cm = jnp.split(xbc, (SSM_D_INNER, SSM_D_INNER + SSM_GROUPS * SSM_STATE), axis=-1)
    xh = xs.reshape(b, s, SSM_HEADS, SSM_HEAD_DIM)
    dt = jax.nn.softplus(dt_raw.astype(f32) + dt_bias.astype(f32))
    a = -jnp.exp(a_log.astype(f32))
    y = ssd_chunked(xh, dt, a, bm.reshape(b, s, SSM_GROUPS, SSM_STATE), cm.reshape(b, s, SSM_GROUPS, SSM_STATE))
    y = y + d_skip.astype(f32)[:, None] * xh
    y = y.reshape(b, s, SSM_D_INNER) * jax.nn.silu(z.astype(f32))
    return group_rms_norm(y, norm_w, SSM_GROUPS).astype(z.dtype)


def mlstm_mixer(q, k, v, o_raw, i_raw, f_raw, i_bias, f_bias, norm_w):
    f32 = jnp.float32
    b, s, _ = q.shape
    H, L = LSTM_HEADS, LSTM_CHUNK
    c = s // L
    q = q.astype(f32).reshape(b, c, L, H, LSTM_QK_DIM) * (LSTM_QK_DIM ** -0.5)
    k = k.astype(f32).reshape(b, c, L, H, LSTM_QK_DIM)
    v = v.astype(f32).reshape(b, c, L, H, LSTM_V_DIM)
    log_i = jnp.swapaxes((i_raw.astype(f32) + i_bias.astype(f32)).reshape(b, c, L, H), 2, 3)
    log_f = jnp.swapaxes(jax.nn.log_sigmoid(f_raw.astype(f32) + f_bias.astype(f32)).reshape(b, c, L, H), 2, 3)
    fcum = jnp.cumsum(log_f, -1)
    causal = jnp.tril(jnp.ones((L, L), bool))
    log_d = jnp.where(causal, fcum[..., :, None] - fcum[..., None, :] + log_i[..., None, :], -jnp.inf)
    a_end = fcum[..., -1:] - fcum + log_i
    m_loc = jnp.max(a_end, -1)
    w_end = jnp.exp(a_end - m_loc[..., None])
    c_loc = jnp.einsum('bchs,bcshk,bcshv->bchkv', w_end, k, v)
    n_loc = jnp.einsum('bchs,bcshk->bchk', w_end, k)
    f_tot = fcum[..., -1]

    def chunk_step(carry, inp):
        c_st, n_st, m_st = carry
        c_l, n_l, m_l, f_l = inp
        m_new = jnp.maximum(f_l + m_st, m_l)
        s_prev = jnp.exp(f_l + m_st - m_new)
        s_loc = jnp.exp(m_l - m_new)
        c_new = s_prev[..., None, None] * c_st + s_loc[..., None, None] * c_l
        n_new = s_prev[..., None] * n_st + s_loc[..., None] * n_l
        return (c_new, n_new, m_new), (c_st, n_st, m_st)

    init = (jnp.zeros((b, H, LSTM_QK_DIM, LSTM_V_DIM), f32),
            jnp.zeros((b, H, LSTM_QK_DIM), f32),
            jnp.full((b, H), -jnp.inf, f32))
    _, (c_prev, n_prev, m_prev) = lax.scan(
        chunk_step, init,
        (jnp.moveaxis(c_loc, 1, 0), jnp.moveaxis(n_loc, 1, 0), jnp.moveaxis(m_loc, 1, 0), jnp.moveaxis(f_tot, 1, 0)))
    c_prev = jnp.moveaxis(c_prev, 0, 1)
    n_prev = jnp.moveaxis(n_prev, 0, 1)
    m_prev = jnp.moveaxis(m_prev, 0, 1)
    inter_log = fcum + m_prev[..., None]
    m_t = jnp.maximum(inter_log, jnp.max(log_d, -1))
    scores = jnp.einsum('bclhk,bcshk->bchls', q, k) * jnp.exp(log_d - m_t[..., None])
    inter_w = jnp.exp(inter_log - m_t)
    num = (jnp.einsum('bchls,bcshv->bclhv', scores, v)
           + jnp.einsum('bchl,bclhk,bchkv->bclhv', inter_w, q, c_prev))
    den = jnp.sum(scores, -1) + inter_w * jnp.einsum('bclhk,bchk->bchl', q, n_prev)
    denom = jnp.maximum(jnp.abs(den), jnp.exp(-m_t))
    h = (num / jnp.swapaxes(denom, 2, 3)[..., None]).reshape(b, s, LSTM_D_V)
    h = jax.nn.sigmoid(o_raw.astype(f32)) * group_rms_norm(h, norm_w, H)
    return h.astype(o_raw.dtype)


def hierarchical_moe(x, w_group, b_group, w_expert, b_expert, w_gate, w_up, w_down):
    f32 = jnp.float32
    b, s, d = x.shape
    t = x.reshape(b * s, d)
    n_tok = t.shape[0]
    grp_prob = jax.nn.softmax((t @ w_group).astype(f32) + b_group.astype(f32), -1)
    grp_p, grp_idx = lax.top_k(grp_prob, 1)
    exp_logits = ((t @ w_expert).astype(f32) + b_expert.astype(f32)).reshape(n_tok, MOE_GROUPS, MOE_EXPERTS_PER_GROUP)
    in_grp = jnp.take_along_axis(exp_logits, grp_idx[:, :, None], axis=1)[:, 0]
    top_logit, top_local = lax.top_k(in_grp, MOE_TOP_K)
    gate = jax.nn.softmax(top_logit, -1) * grp_p
    flat_e = (grp_idx * MOE_EXPERTS_PER_GROUP + top_local).reshape(-1)
    n_rows = flat_e.shape[0]
    order = jnp.argsort(flat_e)
    sorted_e = flat_e[order]
    sorted_tok = (order // MOE_TOP_K).astype(jnp.int32)
    sorted_gate = gate.reshape(-1)[order]
    counts = jnp.bincount(flat_e, length=MOE_EXPERTS)
    starts = jnp.cumsum(counts) - counts
    blocks = (counts + MOE_BLOCK - 1) // MOE_BLOCK
    block_end = jnp.cumsum(blocks)
    block_start = block_end - blocks
    slot = block_start[sorted_e] * MOE_BLOCK + jnp.arange(n_rows) - starts[sorted_e]
    n_blocks = -(-n_rows // MOE_BLOCK) + MOE_EXPERTS
    pad_tok = jnp.full((n_blocks * MOE_BLOCK,), n_tok, jnp.int32).at[slot].set(sorted_tok)
    pad_gate = jnp.zeros((n_blocks * MOE_BLOCK,), x.dtype).at[slot].set(sorted_gate.astype(x.dtype))
    block_e = jnp.minimum(jnp.searchsorted(block_end, jnp.arange(n_blocks), side='right'), MOE_EXPERTS - 1)
    t_pad = jnp.concatenate([t, jnp.zeros((1, d), t.dtype)], 0)
    x_blocks = t_pad[pad_tok].reshape(n_blocks, MOE_BLOCK, d)

    def expert_block(args):
        xb, e = args
        hid = jax.nn.silu(xb @ w_gate[e]) * (xb @ w_up[e])
        return hid @ w_down[e]

    y_blocks = lax.map(expert_block, (x_blocks, block_e)).reshape(n_blocks * MOE_BLOCK, d)
    out = jnp.zeros((n_tok + 1, d), x.dtype).at[pad_tok].add(y_blocks * pad_gate[:, None])
    return out[:n_tok].reshape(b, s, d)


def setup_inputs(seed: int = 0) -> dict:
    key = jax.random.key(seed)
    ks = jax.random.split(key, 32)
    f32 = jnp.float32

    def nrm(k, shape, scale):
        return jax.random.normal(k, shape, f32) * scale

    L = DEPTH
    x = nrm(ks[0], (BATCH, SEQ, D_MODEL), 1.0)
    p = nrm(ks[1], (DEPTH, BATCH, SEQ, PLE_DIM), 1.0)
    w_in = nrm(ks[2], (L, D_MODEL, IN_PROJ_DIM), D_MODEL ** -0.5)
    ssm_conv_w = nrm(ks[3], (L, SSM_CONV, SSM_XBC_DIM), SSM_CONV ** -0.5)
    ssm_conv_b = nrm(ks[4], (L, SSM_XBC_DIM), 0.02)
    dt0 = jnp.exp(jax.random.uniform(ks[5], (L, SSM_HEADS), f32, math.log(1e-3), math.log(1e-1)))
    ssm_dt_bias = dt0 + jnp.log(-jnp.expm1(-dt0))
    ssm_a_log = jnp.log(jax.random.uniform(ks[6], (L, SSM_HEADS), f32, 1.0, 16.0))
    ssm_d = 1.0 + nrm(ks[7], (L, SSM_HEADS), 0.1)
    ssm_norm_w = 1.0 + nrm(ks[8], (L, SSM_D_INNER), 0.02)
    lstm_i_bias = nrm(ks[9], (L, LSTM_HEADS), 0.1)
    lstm_f_bias = jnp.linspace(3.0, 6.0, LSTM_HEADS, dtype=f32)[None, :] + nrm(ks[10], (L, LSTM_HEADS), 0.1)
    lstm_norm_w = 1.0 + nrm(ks[11], (L, LSTM_D_V), 0.02)
    w_branch_ssm = nrm(ks[12], (L, SSM_D_INNER, D_MODEL), SSM_D_INNER ** -0.5 * DEEPNORM_BETA)
    w_branch_lstm = nrm(ks[13], (L, LSTM_D_V, D_MODEL), LSTM_D_V ** -0.5 * DEEPNORM_BETA)
    w_out = nrm(ks[14], (L, D_MODEL, D_MODEL), D_MODEL ** -0.5 * DEEPNORM_BETA)
    ln1_g = 1.0 + nrm(ks[15], (L, D_MODEL), 0.02)
    ln1_b = nrm(ks[16], (L, D_MODEL), 0.02)
    moe_w_group = nrm(ks[17], (L, D_MODEL, MOE_GROUPS), D_MODEL ** -0.5)
    moe_b_group = nrm(ks[18], (L, MOE_GROUPS), 0.01)
    moe_w_expert = nrm(ks[19], (L, D_MODEL, MOE_EXPERTS), D_MODEL ** -0.5)
    moe_b_expert = nrm(ks[20], (L, MOE_EXPERTS), 0.01)
    moe_w_gate = nrm(ks[21], (L, MOE_EXPERTS, D_MODEL, MOE_D_FF), D_MODEL ** -0.5 * DEEPNORM_BETA)
    moe_w_up = nrm(ks[22], (L, MOE_EXPERTS, D_MODEL, MOE_D_FF), D_MODEL ** -0.5 * DEEPNORM_BETA)
    moe_w_down = nrm(ks[23], (L, MOE_EXPERTS, MOE_D_FF, D_MODEL), MOE_D_FF ** -0.5 * DEEPNORM_BETA)
    ln2_g = 1.0 + nrm(ks[24], (L, D_MODEL), 0.02)
    ln2_b = nrm(ks[25], (L, D_MODEL), 0.02)
    ple_w_proj = nrm(ks[26], (L, PLE_DIM, D_MODEL), PLE_DIM ** -0.5)
    ple_w_gate = nrm(ks[27], (L, D_MODEL, D_MODEL), D_MODEL ** -0.5)
    return {'x': x, 'p': p, 'w_in': w_in,
            'ssm_conv_w': ssm_conv_w, 'ssm_conv_b': ssm_conv_b, 'ssm_dt_bias': ssm_dt_bias,
            'ssm_a_log': ssm_a_log, 'ssm_d': ssm_d, 'ssm_norm_w': ssm_norm_w,
            'lstm_i_bias': lstm_i_bias, 'lstm_f_bias': lstm_f_bias, 'lstm_norm_w': lstm_norm_w,
            'w_branch_ssm': w_branch_ssm, 'w_branch_lstm': w_branch_lstm, 'w_out': w_out,
            'ln1_g': ln1_g, 'ln1_b': ln1_b,
            'moe_w_group': moe_w_group, 'moe_b_group': moe_b_group,
            'moe_w_expert': moe_w_expert, 'moe_b_expert': moe_b_expert,
            'moe_w_gate': moe_w_gate, 'moe_w_up': moe_w_up, 'moe_w_down': moe_w_down,
            'ln2_g': ln2_g, 'ln2_b': ln2_b,
            'ple_w_proj': ple_w_proj, 'ple_w_gate': ple_w_gate}


def reference(x, p, w_in, ssm_conv_w, ssm_conv_b, ssm_dt_bias, ssm_a_log, ssm_d, ssm_norm_w,
              lstm_i_bias, lstm_f_bias, lstm_norm_w, w_branch_ssm, w_branch_lstm, w_out,
              ln1_g, ln1_b, moe_w_group, moe_b_group, moe_w_expert, moe_b_expert,
              moe_w_gate, moe_w_up, moe_w_down, ln2_g, ln2_b, ple_w_proj, ple_w_gate):
    for i in range(DEPTH):
        proj = x @ w_in[i]
        (z, xbc, dt_raw, q, k, v, o_raw, i_raw, f_raw, g_ssm, g_lstm) = jnp.split(proj, IN_PROJ_SPLITS, axis=-1)
        y_ssm = mamba2_mixer(z, xbc, dt_raw, ssm_conv_w[i], ssm_conv_b[i], ssm_dt_bias[i],
                             ssm_a_log[i], ssm_d[i], ssm_norm_w[i])
        y_lstm = mlstm_mixer(q, k, v, o_raw, i_raw, f_raw, lstm_i_bias[i], lstm_f_bias[i], lstm_norm_w[i])
        merged = (jax.nn.sigmoid(g_ssm) * (y_ssm @ w_branch_ssm[i])
                  + jax.nn.sigmoid(g_lstm) * (y_lstm @ w_branch_lstm[i]))
        x = layer_norm(DEEPNORM_ALPHA * x + merged @ w_out[i], ln1_g[i], ln1_b[i])
        moe = hierarchical_moe(x, moe_w_group[i], moe_b_group[i], moe_w_expert[i], moe_b_expert[i],
                               moe_w_gate[i], moe_w_up[i], moe_w_down[i])
        x = layer_norm(DEEPNORM_ALPHA * x + moe, ln2_g[i], ln2_b[i])
        x = x + jax.nn.sigmoid(x @ ple_w_gate[i]) * (p[i] @ ple_w_proj[i])
    return x
```

```python
import numpy as np
from contextlib import ExitStack
import concourse.bass as bass
import concourse.mybir as mybir
from concourse.bass_utils import run_bass_kernel_spmd


F32 = mybir.dt.float32
BF16 = mybir.dt.bfloat16
I32 = mybir.dt.int32
AF = mybir.ActivationFunctionType
ALU = mybir.AluOpType
AX = mybir.AxisListType

EPOCH = 30000


class _Rec:
    def __init__(self):
        self.call = None

    def __getattr__(self, name):
        def f(*a, **k):
            assert self.call is None
            self.call = (name, a, k)
            return None
        return f


def _record(fn):
    r = _Rec()
    fn(r)
    assert r.call is not None
    return r.call


class Prog:
    def __init__(self, nc, es):
        self.nc = nc
        self.es = es
        self.eng = {"pe": nc.tensor, "dve": nc.vector, "act": nc.scalar,
                    "pool": nc.gpsimd, "sp": nc.sync}
        self.cnt = {e: 0 for e in self.eng}
        self.esem = {}
        self.nsem = 0
        for e in self.eng:
            self.esem[e] = self._newsem("c_" + e)
        self.waited = {e: {} for e in self.eng}
        self.res = {}
        self.dsem = {}
        self.nops = 0
        self.q = {e: [] for e in self.eng}
        self.pend = {e: False for e in self.eng}
        self.nwaits = 0

    def _newsem(self, name):
        self.nsem += 1
        return self.es.enter_context(self.nc.semaphore(name + "_%d" % self.nsem))

    def _wait(self, e, toks):
        w = self.waited[e]
        need = {}
        for t in toks:
            if t is None:
                continue
            sem, val, src = t
            if src == e and e == "pe":
                continue
            k = id(sem)
            if w.get(k, 0) >= val:
                continue
            if k not in need or need[k][1] < val:
                need[k] = (sem, val)
        for k, (sem, val) in need.items():
            self.q[e].append(("w", sem, val))
            w[k] = val
            self.nwaits += 1

    def _deps(self, reads, writes):
        toks = []
        for r in reads:
            st = self.res.get(r)
            if st is not None:
                toks.append(st[0])
        for wkey in writes:
            st = self.res.get(wkey)
            if st is not None:
                toks.append(st[0])
                toks.extend(st[1].values())
        return toks

    def _commit(self, tok, reads, writes):
        for r in reads:
            st = self.res.setdefault(r, [None, {}])
            k = tok[1] if tok[0] == "dma" else id(tok[0])
            old = st[1].get(k)
            if old is None or tok[0] == "dma" or old[1] < tok[1]:
                st[1][k] = tok
        for wkey in writes:
            self.res[wkey] = [tok, {}]

    skip = False

    def op(self, e, fn, reads=(), writes=(), inc=True):
        if self.skip:
            return None
        toks = self._deps(reads, writes)
        self._wait(e, toks)
        self.nops += 1
        if inc:
            if self.cnt[e] >= EPOCH and not self.pend[e]:
                self.esem[e] = self._newsem("c_" + e)
                self.cnt[e] = 0
            self.cnt[e] += 1
            self.q[e].append(("o", _record(fn), self.esem[e], 1))
            tok = (self.esem[e], self.cnt[e], e)
            self.pend[e] = False
        else:
            self.q[e].append(("o", _record(fn), None, 0))
            self.pend[e] = True
            tok = (self.esem[e], self.cnt[e] + 1, e)
        self._commit(tok, reads, writes)
        return None

    def dma(self, q, fn, semkey, reads=(), writes=()):
        if self.skip:
            return None
        toks = self._deps(reads, writes)
        self._wait(q, toks)
        if semkey not in self.dsem:
            self.dsem[semkey] = [self._newsem("d"), 0]
        ds = self.dsem[semkey]
        ds[1] += 16
        self.q[q].append(("o", _record(fn), ds[0], 16))
        self.nops += 1
        tok = ("dma", semkey)
        self._commit(tok, reads, writes)
        return None

    def barrier(self):
        self.skip = False
        toks = []
        for e in self.eng:
            if self.pend[e]:
                self.op(e, lambda q: q.nop(), inc=True)
            if self.cnt[e] > 0:
                toks.append((self.esem[e], self.cnt[e], None))
        for k, (sem, cnt) in self.dsem.items():
            if cnt > 0:
                toks.append((sem, cnt, None))
        for e in self.eng:
            self._wait(e, toks)
        self.res = {}
        self.flush()

    def flush(self):
        if not any(self.q.values()):
            return
        qs = self.q
        self.q = {e: [] for e in self.eng}
        deco = {"pe": "tensor", "dve": "vector", "act": "scalar", "pool": "gpsimd", "sp": "sync"}

        def replay(lst):
            def f(eng):
                for it in lst:
                    if it[0] == "w":
                        eng.wait_ge(it[1], it[2])
                    else:
                        name, a, k = it[1]
                        ins = getattr(eng, name)(*a, **k)
                        if it[2] is not None:
                            ins.then_inc(it[2], it[3])
            return f
        with self.nc.Block() as block:
            for e, lst in qs.items():
                if lst:
                    getattr(block, deco[e])(replay(lst))

    def finish(self, e="sp"):
        for k, (sem, cnt) in self.dsem.items():
            if cnt > 0:
                self.q[e].append(("w", sem, cnt))
        self.flush()


_orig_wait = Prog._wait


def _wait2(self, e, toks):
    out = []
    for t in toks:
        if t is None:
            continue
        if t[0] == "dma":
            ds = self.dsem[t[1]]
            out.append((ds[0], ds[1], None))
        else:
            out.append(t)
    _orig_wait(self, e, out)


Prog._wait = _wait2


NEG = -1.0e30
EPS = 1e-5


class Ctx:
    stop = None


class Stop(Exception):
    pass


def chk(C, name):
    if C.stop == name or C.stop == "%s@%d" % (name, getattr(C, "cur", -1)):
        C.P.skip = True


def mmg(P, out, pairs, reads, writes):
    n = len(pairs)
    for i, (l, r) in enumerate(pairs):
        P.op("pe", lambda e, l=l, r=r, i=i: e.matmul(out, lhsT=l, rhs=r, start=(i == 0), stop=(i == n - 1)),
             reads=reads, writes=writes, inc=(i == n - 1))


def setup_consts(C):
    P, nc, es = C.P, C.nc, C.es
    sb = C.sb
    C.cst = sb("cst", [128, 6, 128], F32)
    P.dma("sp", lambda q: q.dma_start(out=C.cst[:].rearrange("p a b -> p (a b)"), in_=C.d["consts"][:, :]), "cst", writes=["cst"])
    C.ident = C.cst[:, 0, :]
    C.U = C.cst[:, 1, :]
    C.ones = C.cst[:, 2, :]
    C.nmT = C.cst[:, 3, :]
    C.nm = C.cst[:, 4, :]
    C.cstb = sb("cstb", [128, 6, 128], BF16)
    P.op("dve", lambda e: e.tensor_copy(out=C.cstb[:], in_=C.cst[:]), reads=["cst"], writes=["cstb"])
    C.identb = C.cstb[:, 0, :]
    C.Ustrb = C.cstb[:, 5, :]
    C.onesb = C.cstb[:, 2, :]


def phase0(C):
    P, nc = C.P, C.nc
    with ExitStack() as es:
        sb = lambda n, s, d: es.enter_context(nc.sbuf_tensor(n, s, d))
        xf = [sb("p0_xf%d" % i, [128, 1024], F32) for i in range(4)]
        xb = [sb("p0_xb%d" % i, [128, 1024], BF16) for i in range(4)]
        xt = [sb("p0_xt%d" % i, [128, 8, 128], BF16) for i in range(4)]
        x = C.d["x"]
        if getattr(C, "zero_xs", None):
            C.zero_xs(C, es)
        for c in range(C.TOK // 128):
            s = c % 4
            t0 = c * 128
            P.dma("sp", lambda q: q.dma_start(out=xf[s][:], in_=x[t0:t0 + 128, :]), "p0xf%d" % s, writes=["p0xf%d" % s])
            P.op("act", lambda e: e.activation(out=xb[s][:], in_=xf[s][:], func=AF.Copy), reads=["p0xf%d" % s], writes=["p0xb%d" % s])
            pb = (C.PC if s < 2 else C.PD)[:].bitcast(BF16)[:, (s % 2) * 1024:(s % 2 + 1) * 1024]
            bk = ("PC%d" if s < 2 else "PD%d") % (s % 2)
            for kc in range(8):
                P.op("pe", lambda e, kc=kc: e.transpose(out=pb[:, kc * 128:(kc + 1) * 128], in_=xb[s][:, kc * 128:(kc + 1) * 128], identity=C.identb),
                     reads=["p0xb%d" % s, "cstb"], writes=[bk], inc=(kc == 7))
            P.op("dve" if s % 2 == 0 else "act", lambda e: (e.tensor_copy(out=xt[s][:].rearrange("p a b -> p (a b)"), in_=pb) if s % 2 == 0 else e.activation(out=xt[s][:].rearrange("p a b -> p (a b)"), in_=pb, func=AF.Copy)),
                 reads=[bk], writes=["p0xt%d" % s])
            P.dma("sp", lambda q: q.dma_start(out=C.d["xT"][:, :, t0:t0 + 128], in_=xt[s][:]), "p0xt%d" % s, reads=["p0xt%d" % s], writes=["xT_dram"])
        P.barrier()


def ssd_pass(C, b):
    P, nc, d = C.P, C.nc, C.d
    S = C.S
    with ExitStack() as es:
        sb = lambda n, s, dt: es.enter_context(nc.sbuf_tensor("ssd%d_" % b + n, s, dt))
        Wz = sb("Wz", [128, 8, 1024], BF16)
        Wx = sb("Wx", [128, 8, 2048], BF16)
        Wdt = sb("Wdt", [128, 8, 16], BF16)
        win = d["w_in"].rearrange("(kc p) c -> p kc c", p=128)
        P.dma("pool", lambda q: q.dma_start(out=Wx[:, :, 0:1024], in_=win[:, :, 1024:2048]), "Wx", writes=["Wx"])
        P.dma("pool", lambda q: q.dma_start(out=Wx[:, :, 1024:2048], in_=win[:, :, 2048:3072]), "Wx", writes=["Wx"])
        P.dma("pool", lambda q: q.dma_start(out=Wz[:], in_=win[:, :, 0:1024]), "Wz", writes=["Wz"])
        P.dma("pool", lambda q: q.dma_start(out=Wdt[:], in_=win[:, :, 3072:3088]), "Wdt", writes=["Wdt"])
        cw = sb("cw", [128, 16, 4], F32)
        cb = sb("cb", [128, 16], F32)
        P.dma("sp", lambda q: q.dma_start(out=cw[:].rearrange("p a b -> p (a b)"), in_=d["convw_fm"][:, :]), "cw", writes=["cw"])
        P.dma("sp", lambda q: q.dma_start(out=cb[:], in_=d["convb_fm"][:, :]), "cb", writes=["cb"])
        rp = sb("rp", [128, 48 + 1024], F32)
        P.dma("sp", lambda q: q.dma_start(out=rp[:], in_=d["rep_ssd"][:, :]), "rp", writes=["rp"])
        dtb, alog, Drep, normw = rp[:, 0:16], rp[:, 16:32], rp[:, 32:48], rp[:, 48:1072]
        arep = sb("arep", [128, 16], F32)
        P.op("act", lambda e: e.activation(out=arep[:], in_=alog, func=AF.Exp), reads=["rp"], writes=["arep"])
        P.op("dve", lambda e: e.tensor_scalar(out=arep[:], in0=arep[:], scalar1=-1.0, scalar2=None, op0=ALU.mult), reads=["arep"], writes=["arep"])
        diagw = sb("diagw", [128, 16, 4, 128], BF16)
        for blk in range(16):
            for k in range(4):
                P.op("dve" if (blk + k) % 2 else "pool", lambda e, blk=blk, k=k: e.tensor_scalar(out=diagw[:, blk, k, :], in0=C.ident, scalar1=cw[:, blk, k:k + 1], scalar2=None, op0=ALU.mult),
                     reads=["cst", "cw"], writes=[("diagw", blk, k)])
        chk(C, "setup")
        ubuf = sb("ubuf", [128, 16, 516], BF16)
        xbcT = sb("xbcT", [128, 16, 512], BF16)
        xTs = [sb("xTs%d" % i, [128, 8, 512], BF16) for i in range(2)]
        x_tok = sb("x_tok", [128, 4, 1024], BF16)
        B_tok = sb("B_tok", [128, 4, 512], BF16)
        szb = sb("szb", [128, 4, 1024], F32)
        rhs_cs2 = [sb("rhs_cs%d" % i, [128, 8, 128], F32) for i in range(2)]
        dtmp2 = [sb("dtmp%d" % i, [128, 8, 128], F32) for i in range(2)]
        MT2 = [sb("MT%d" % i, [128, 8, 128], BF16) for i in range(2)]
        xdt = sb("xdt", [128, 1024], BF16)
        xdtd = sb("xdtd", [128, 1024], BF16)
        yA = sb("yA", [128, 1024], F32)
        yB = sb("yB", [128, 1024], F32)
        ybf = sb("ybf", [128, 1024], BF16)
        St = sb("St", [128, 1024], F32)
        Sbf = sb("Sbf", [128, 1024], BF16)
        sm = sb("sm", [128, 160], F32)
        ss = sb("ss", [128, 8], F32)
        yTst = [sb("yTst%d" % i, [128, 8, 128], BF16) for i in range(2)]

        PA, PB, PC, PD = C.PA, C.PB, C.PC, C.PD
        PCb = PC[:].bitcast(BF16)
        P.op("pool", lambda e: e.memset(St[:], 0.0), writes=["St"])
        P.op("pool", lambda e: e.memset(Sbf[:], 0.0), writes=["Sbf"])
        P.op("pool", lambda e: e.memset(ubuf[:, :, 0:4], 0.0), writes=[("ubuf", i) for i in range(16)])

        nsc = S // 512
        for sc in range(nsc):
            tok0 = b * S + sc * 512
            xs = xTs[sc % 2]
            xk = "xTs%d" % (sc % 2)
            P.dma("sp", lambda q: q.dma_start(out=xs[:], in_=d["xT"][:, :, tok0:tok0 + 512]), xk, reads=["xT_dram"], writes=[xk])
            if sc > 0:
                P.op("pool", lambda e: e.tensor_copy(out=ubuf[:, :, 1:4], in_=ubuf[:, :, 513:516]),
                     reads=[("ubuf", i) for i in range(16)], writes=[("ubuf", i) for i in range(16)])
            for blk in range(16):
                bank = PD[:, (blk % 2) * 512:(blk % 2 + 1) * 512]
                bk = "PD%d" % (blk % 2)
                mmg(P, bank, [(Wx[:, kc, blk * 128:(blk + 1) * 128], xs[:, kc, :]) for kc in range(8)], reads=["Wx", xk], writes=[bk])
                P.op("act" if blk % 2 else "dve", lambda e, blk=blk, bank=bank: (e.activation(out=ubuf[:, blk, 4:516], in_=bank, func=AF.Copy) if blk % 2 else e.tensor_copy(out=ubuf[:, blk, 4:516], in_=bank)),
                     reads=[bk], writes=[("ubuf", blk)])
            chk(C, "uT")
            for blk in range(16):
                bank = PD[:, (blk % 2) * 512:(blk % 2 + 1) * 512]
                bk = "PD%d" % (blk % 2)
                mmg(P, bank, [(diagw[:, blk, k, :], ubuf[:, blk, 1 + k:1 + k + 512]) for k in range(4)],
                    reads=[("ubuf", blk)] + [("diagw", blk, k) for k in range(4)], writes=[bk])
                P.op("act", lambda e, blk=blk, bank=bank: e.activation(out=xbcT[:, blk, :], in_=bank, func=AF.Silu, bias=cb[:, blk:blk + 1]),
                     reads=[bk, "cb"], writes=[("xbcT", blk)])
            chk(C, "conv")
            for c4 in range(4):
                cs = slice(c4 * 128, (c4 + 1) * 128)
                for blk in range(8):
                    P.op("pe", lambda e, blk=blk: e.transpose(out=PCb[:, blk * 128:(blk + 1) * 128], in_=xbcT[:, blk, cs], identity=C.identb),
                         reads=[("xbcT", blk), "cstb"], writes=["PC0"], inc=(blk == 7))
                P.op("dve", lambda e: e.tensor_copy(out=x_tok[:, c4, :], in_=PCb[:, 0:1024]), reads=["PC0"], writes=[("x_tok", c4)])
                for g in range(4):
                    P.op("pe", lambda e, g=g: e.transpose(out=PCb[:, 1024 + g * 128:1024 + (g + 1) * 128], in_=xbcT[:, 8 + g, cs], identity=C.identb),
                         reads=[("xbcT", 8 + g), "cstb"], writes=["PC1"], inc=(g == 3))
                P.op("act", lambda e: e.activation(out=B_tok[:, c4, :], in_=PCb[:, 1024:1536], func=AF.Copy), reads=["PC1"], writes=[("B_tok", c4)])

            for c4 in range(4):
                cs = slice(c4 * 128, (c4 + 1) * 128)
                for nb in range(2):
                    mmg(P, PA[:, nb * 512:(nb + 1) * 512], [(xs[:, kc, cs], Wz[:, kc, nb * 512:(nb + 1) * 512]) for kc in range(8)], reads=[xk, "Wz"], writes=["PA%d" % nb])
                P.op("act", lambda e: e.activation(out=szb[:, c4, :], in_=PA[:], func=AF.Silu), reads=["PA0", "PA1"], writes=[("sz", c4)])
            chk(C, "tok")
            for c4r in range(4):
                c4 = c4r
                cs = slice(c4 * 128, (c4 + 1) * 128)
                tk = tok0 + c4 * 128
                C.cur = sc * 4 + c4r
                sz = szb[:, c4, :]
                chk(C, "z")
                mmg(P, PC[:, 512:528], [(xs[:, kc, cs], Wdt[:, kc, :]) for kc in range(8)], reads=[xk, "Wdt"], writes=["PC1"])
                P.op("dve", lambda e: e.tensor_tensor(out=sm[:, 0:16], in0=PC[:, 512:528], in1=dtb, op=ALU.add), reads=["PC1", "rp"], writes=["sm0"])
                P.op("act", lambda e: e.activation(out=sm[:, 16:32], in_=sm[:, 0:16], func=AF.Exp), reads=["sm0"], writes=["sm1"])
                P.op("act", lambda e: e.activation(out=sm[:, 32:48], in_=sm[:, 16:32], func=AF.Ln, bias=1.0), reads=["sm1"], writes=["dt"])
                dt = sm[:, 32:48]
                P.op("dve", lambda e: e.tensor_tensor(out=sm[:, 48:64], in0=dt, in1=arep[:], op=ALU.mult), reads=["dt", "arep"], writes=["dta"])
                dta = sm[:, 48:64]
                P.op("pe", lambda e: e.matmul(PC[:, 528:544], lhsT=C.U, rhs=dta, start=True, stop=True), reads=["cst", "dta"], writes=["PC1"])
                P.op("act", lambda e: e.activation(out=sm[:, 64:80], in_=PC[:, 528:544], func=AF.Exp), reads=["PC1"], writes=["eacs"])
                P.op("dve", lambda e: e.tensor_scalar(out=sm[:, 80:96], in0=PC[:, 528:544], scalar1=-1.0, scalar2=None, op0=ALU.mult), reads=["PC1"], writes=["nacs"])
                eacs, nacs = sm[:, 64:80], sm[:, 80:96]
                chk(C, "dt")
                P.op("dve", lambda e: e.tensor_tensor(out=xdt[:].rearrange("p (h q) -> p h q", h=16), in0=x_tok[:, c4, :].rearrange("p (h q) -> p h q", h=16),
                                                       in1=dt.unsqueeze(2).broadcast_to([128, 16, 64]), op=ALU.mult), reads=[("x_tok", c4), "dt"], writes=["xdt"])
                stages = [[], []]
                for hh in range(2):
                    hs = slice(hh * 8, hh * 8 + 8)
                    rhs_cs, dtmp, MT = rhs_cs2[hh], dtmp2[hh], MT2[hh]
                    PH, nph = (PB, "PB") if hh == 0 else (PA, "PA")
                    k_rc, k_dt, k_mt = "rhs_cs%d" % hh, "dtmp%d" % hh, "MT%d" % hh
                    ST = stages[hh].append
                    def _stage(hh=hh, hs=hs, rhs_cs=rhs_cs, dtmp=dtmp, MT=MT, PH=PH, nph=nph, k_rc=k_rc, k_dt=k_dt, k_mt=k_mt, **kw):
                        PB3 = PH[:].rearrange("p (a b) -> p a b", a=8); kph = [nph + "0", nph + "1"]
                        cbt = PC[:, hh * 256:hh * 256 + 256].rearrange("p (g l) -> p g l", g=2).unsqueeze(2).broadcast_to([128, 2, 4, 128])
                        P.op("dve", lambda e: e.tensor_tensor(out=rhs_cs[:], in0=C.U.unsqueeze(1).broadcast_to([128, 8, 128]),
                                                               in1=dta[:, hs].unsqueeze(2).broadcast_to([128, 8, 128]), op=ALU.mult), reads=["cst", "dta"], writes=[k_rc])
                    ST(_stage)
                    def _stage(hh=hh, hs=hs, rhs_cs=rhs_cs, dtmp=dtmp, MT=MT, PH=PH, nph=nph, k_rc=k_rc, k_dt=k_dt, k_mt=k_mt, **kw):
                        PB3 = PH[:].rearrange("p (a b) -> p a b", a=8); kph = [nph + "0", nph + "1"]
                        cbt = PC[:, hh * 256:hh * 256 + 256].rearrange("p (g l) -> p g l", g=2).unsqueeze(2).broadcast_to([128, 2, 4, 128])
                        for q4 in range(2):
                            P.op("pe", lambda e, q4=q4: e.matmul(PH[:, q4 * 512:(q4 + 1) * 512], lhsT=C.ones, rhs=rhs_cs[:, q4 * 4:(q4 + 1) * 4, :].rearrange("p a b -> p (a b)"), start=True, stop=True),
                                 reads=["cst", k_rc], writes=[nph + "%d" % q4])
                    ST(_stage)
                    PB3 = PH[:].rearrange("p (a b) -> p a b", a=8)
                    kph = [nph + "0", nph + "1"]
                    def _stage(hh=hh, hs=hs, rhs_cs=rhs_cs, dtmp=dtmp, MT=MT, PH=PH, nph=nph, k_rc=k_rc, k_dt=k_dt, k_mt=k_mt, **kw):
                        PB3 = PH[:].rearrange("p (a b) -> p a b", a=8); kph = [nph + "0", nph + "1"]
                        cbt = PC[:, hh * 256:hh * 256 + 256].rearrange("p (g l) -> p g l", g=2).unsqueeze(2).broadcast_to([128, 2, 4, 128])
                        P.op("dve", lambda e: e.tensor_tensor(out=dtmp[:], in0=PB3, in1=C.nmT.unsqueeze(1).broadcast_to([128, 8, 128]), op=ALU.add), reads=kph + ["cst"], writes=[k_dt])
                    ST(_stage)
                    def _stage(hh=hh, hs=hs, rhs_cs=rhs_cs, dtmp=dtmp, MT=MT, PH=PH, nph=nph, k_rc=k_rc, k_dt=k_dt, k_mt=k_mt, **kw):
                        PB3 = PH[:].rearrange("p (a b) -> p a b", a=8); kph = [nph + "0", nph + "1"]
                        cbt = PC[:, hh * 256:hh * 256 + 256].rearrange("p (g l) -> p g l", g=2).unsqueeze(2).broadcast_to([128, 2, 4, 128])
                    ST(_stage)
                    def _stage(hh=hh, hs=hs, rhs_cs=rhs_cs, dtmp=dtmp, MT=MT, PH=PH, nph=nph, k_rc=k_rc, k_dt=k_dt, k_mt=k_mt, **kw):
                        PB3 = PH[:].rearrange("p (a b) -> p a b", a=8); kph = [nph + "0", nph + "1"]
                        cbt = PC[:, hh * 256:hh * 256 + 256].rearrange("p (g l) -> p g l", g=2).unsqueeze(2).broadcast_to([128, 2, 4, 128])
                        P.op("dve", lambda e: e.tensor_tensor(out=sm[:, 96 + hh * 8:96 + hh * 8 + 8], in0=PB3[:, :, 127], in1=nacs[:, hs], op=ALU.add), reads=kph + ["nacs"], writes=[("dtea", hh)])
                    ST(_stage)
                    def _stage(hh=hh, hs=hs, rhs_cs=rhs_cs, dtmp=dtmp, MT=MT, PH=PH, nph=nph, k_rc=k_rc, k_dt=k_dt, k_mt=k_mt, **kw):
                        PB3 = PH[:].rearrange("p (a b) -> p a b", a=8); kph = [nph + "0", nph + "1"]
                        cbt = PC[:, hh * 256:hh * 256 + 256].rearrange("p (g l) -> p g l", g=2).unsqueeze(2).broadcast_to([128, 2, 4, 128])
                        P.op("act", lambda e: e.activation(out=sm[:, 112 + hh * 8:112 + hh * 8 + 8], in_=PB3[:, :, 127], func=AF.Exp), reads=kph, writes=[("cd", hh)])
                    ST(_stage)
                    def _stage(hh=hh, hs=hs, rhs_cs=rhs_cs, dtmp=dtmp, MT=MT, PH=PH, nph=nph, k_rc=k_rc, k_dt=k_dt, k_mt=k_mt, **kw):
                        PB3 = PH[:].rearrange("p (a b) -> p a b", a=8); kph = [nph + "0", nph + "1"]
                        cbt = PC[:, hh * 256:hh * 256 + 256].rearrange("p (g l) -> p g l", g=2).unsqueeze(2).broadcast_to([128, 2, 4, 128])
                        P.op("dve", lambda e: e.tensor_tensor(out=dtmp[:], in0=dtmp[:], in1=nacs[:, hs].unsqueeze(2).broadcast_to([128, 8, 128]), op=ALU.add), reads=[k_dt, "nacs"], writes=[k_dt])
                    ST(_stage)
                    def _stage(hh=hh, hs=hs, rhs_cs=rhs_cs, dtmp=dtmp, MT=MT, PH=PH, nph=nph, k_rc=k_rc, k_dt=k_dt, k_mt=k_mt, **kw):
                        PB3 = PH[:].rearrange("p (a b) -> p a b", a=8); kph = [nph + "0", nph + "1"]
                        cbt = PC[:, hh * 256:hh * 256 + 256].rearrange("p (g l) -> p g l", g=2).unsqueeze(2).broadcast_to([128, 2, 4, 128])
                        P.op("act", lambda e: e.activation(out=dtmp[:], in_=dtmp[:], func=AF.Exp), reads=[k_dt], writes=[k_dt])
                    ST(_stage)
                    def _stage(hh=hh, hs=hs, rhs_cs=rhs_cs, dtmp=dtmp, MT=MT, PH=PH, nph=nph, k_rc=k_rc, k_dt=k_dt, k_mt=k_mt, **kw):
                        PB3 = PH[:].rearrange("p (a b) -> p a b", a=8); kph = [nph + "0", nph + "1"]
                        cbt = PC[:, hh * 256:hh * 256 + 256].rearrange("p (g l) -> p g l", g=2).unsqueeze(2).broadcast_to([128, 2, 4, 128])
                    ST(_stage)
                    def _stage(hh=hh, hs=hs, rhs_cs=rhs_cs, dtmp=dtmp, MT=MT, PH=PH, nph=nph, k_rc=k_rc, k_dt=k_dt, k_mt=k_mt, **kw):
                        PB3 = PH[:].rearrange("p (a b) -> p a b", a=8); kph = [nph + "0", nph + "1"]
                        cbt = PC[:, hh * 256:hh * 256 + 256].rearrange("p (g l) -> p g l", g=2).unsqueeze(2).broadcast_to([128, 2, 4, 128])
                        for gi in range(2):
                            g = hh * 2 + gi
                            P.op("pe", lambda e, g=g, gi=gi: e.matmul(PC[:, hh * 256 + gi * 128:hh * 256 + (gi + 1) * 128], lhsT=xbcT[:, 8 + g, cs], rhs=xbcT[:, 12 + g, cs], start=True, stop=True),
                                 reads=[("xbcT", 8 + g), ("xbcT", 12 + g)], writes=["PC0"])
                    ST(_stage)
                    cbt = PC[:, hh * 256:hh * 256 + 256].rearrange("p (g l) -> p g l", g=2).unsqueeze(2).broadcast_to([128, 2, 4, 128])
                    def _stage(hh=hh, hs=hs, rhs_cs=rhs_cs, dtmp=dtmp, MT=MT, PH=PH, nph=nph, k_rc=k_rc, k_dt=k_dt, k_mt=k_mt, **kw):
                        PB3 = PH[:].rearrange("p (a b) -> p a b", a=8); kph = [nph + "0", nph + "1"]
                        cbt = PC[:, hh * 256:hh * 256 + 256].rearrange("p (g l) -> p g l", g=2).unsqueeze(2).broadcast_to([128, 2, 4, 128])
                        P.op("dve", lambda e: e.tensor_tensor(out=MT[:].rearrange("p (g j) l -> p g j l", g=2), in0=cbt, in1=dtmp[:].rearrange("p (g j) l -> p g j l", g=2), op=ALU.mult),
                             reads=["PC0", k_dt], writes=[k_mt])
                    ST(_stage)
                    def _stage(hh=hh, hs=hs, rhs_cs=rhs_cs, dtmp=dtmp, MT=MT, PH=PH, nph=nph, k_rc=k_rc, k_dt=k_dt, k_mt=k_mt, **kw):
                        PB3 = PH[:].rearrange("p (a b) -> p a b", a=8); kph = [nph + "0", nph + "1"]
                        cbt = PC[:, hh * 256:hh * 256 + 256].rearrange("p (g l) -> p g l", g=2).unsqueeze(2).broadcast_to([128, 2, 4, 128])
                        for j in range(8):
                            h = hh * 8 + j
                            P.op("pe", lambda e, j=j, h=h: e.matmul(PD[:, h * 64:(h + 1) * 64], lhsT=MT[:, j, :], rhs=xdt[:, h * 64:(h + 1) * 64], start=True, stop=True),
                                 reads=[k_mt, "xdt"], writes=["PD%d" % hh], inc=(j == 7))
                    ST(_stage)
                for i in range(len(stages[0])):
                    stages[0][i]()
                    stages[1][i]()
                chk(C, "halves")
                P.op("act", lambda e: e.activation(out=sm[:, 96:112], in_=sm[:, 96:112], func=AF.Exp), reads=[("dtea", 0), ("dtea", 1)], writes=[("dtea", 0), ("dtea", 1)])
                P.op("pool", lambda e: e.tensor_tensor(out=sm[:, 128:144], in0=sm[:, 96:112], in1=dt, op=ALU.mult), reads=[("dtea", 0), ("dtea", 1), "dt"], writes=["w2"])
                P.op("dve", lambda e: e.tensor_tensor(out=xdtd[:].rearrange("p (h q) -> p h q", h=16), in0=x_tok[:, c4, :].rearrange("p (h q) -> p h q", h=16),
                                                       in1=sm[:, 128:144].unsqueeze(2).broadcast_to([128, 16, 64]), op=ALU.mult), reads=[("x_tok", c4), "w2"], writes=["xdtd"])
                for g in range(4):
                    P.op("pe", lambda e, g=g: e.matmul(PA[:, g * 256:(g + 1) * 256], lhsT=xbcT[:, 12 + g, cs], rhs=Sbf[:, g * 256:(g + 1) * 256], start=True, stop=True),
                         reads=[("xbcT", 12 + g), "Sbf"], writes=["PA%d" % (g // 2)])
                for g in range(4):
                    P.op("pe", lambda e, g=g: e.matmul(PB[:, g * 256:(g + 1) * 256], lhsT=B_tok[:, c4, g * 128:(g + 1) * 128], rhs=xdtd[:, g * 256:(g + 1) * 256], start=True, stop=True),
                         reads=[("B_tok", c4), "xdtd"], writes=["PB%d" % (g // 2)])
                chk(C, "state_mm")
                h3 = lambda ap: ap.rearrange("p (h q) -> p h q", h=16)
                P.op("dve", lambda e: e.tensor_tensor(out=h3(yA[:]), in0=h3(PA[:]), in1=eacs.unsqueeze(2).broadcast_to([128, 16, 64]), op=ALU.mult), reads=["PA0", "PA1", "eacs"], writes=["yA"])
                P.op("dve", lambda e: e.tensor_tensor(out=yA[:], in0=yA[:], in1=PD[:], op=ALU.add), reads=["yA", "PD0", "PD1"], writes=["yA"])
                P.op("pool", lambda e: e.tensor_tensor(out=h3(yB[:]), in0=h3(x_tok[:, c4, :]), in1=Drep.unsqueeze(2).broadcast_to([128, 16, 64]), op=ALU.mult), reads=[("x_tok", c4), "rp"], writes=["yB"])
                P.op("dve", lambda e: e.tensor_tensor(out=yA[:], in0=yA[:], in1=yB[:], op=ALU.add), reads=["yA", "yB"], writes=["yA"])
                P.op("dve", lambda e: e.tensor_tensor(out=yA[:], in0=yA[:], in1=sz, op=ALU.mult), reads=["yA", ("sz", c4)], writes=["yA"])
                P.op("pool", lambda e: e.memset(ss[:], 0.0), writes=["ss"])
                for g in range(4):
                    P.op("act", lambda e, g=g: e.activation(out=yB[:, g * 256:(g + 1) * 256], in_=yA[:, g * 256:(g + 1) * 256], func=AF.Square, accum_out=ss[:, g:g + 1]), reads=["yA", "ss"], writes=["yB", "ss"])
                P.op("dve", lambda e: e.tensor_scalar(out=ss[:, 4:8], in0=ss[:, 0:4], scalar1=1.0 / 256.0, scalar2=EPS, op0=ALU.mult, op1=ALU.add), reads=["ss"], writes=["ss"])
                P.op("act", lambda e: e.activation(out=ss[:, 4:8], in_=ss[:, 4:8], func=AF.Ln), reads=["ss"], writes=["ss"])
                P.op("act", lambda e: e.activation(out=ss[:, 4:8], in_=ss[:, 4:8], func=AF.Exp, scale=-0.5), reads=["ss"], writes=["ss"])
                g3 = lambda ap: ap.rearrange("p (g q) -> p g q", g=4)
                P.op("dve", lambda e: e.tensor_tensor(out=g3(yA[:]), in0=g3(yA[:]), in1=ss[:, 4:8].unsqueeze(2).broadcast_to([128, 4, 256]), op=ALU.mult), reads=["yA", "ss"], writes=["yA"])
                P.op("dve", lambda e: e.tensor_tensor(out=ybf[:], in0=yA[:], in1=normw, op=ALU.mult), reads=["yA", "rp"], writes=["ybf"])
                for blk in range(8):
                    P.op("pe", lambda e, blk=blk: e.transpose(out=PCb[:, blk * 128:(blk + 1) * 128], in_=ybf[:, blk * 128:(blk + 1) * 128], identity=C.identb),
                         reads=["ybf", "cstb"], writes=["PC0"], inc=(blk == 7))
                ci = (sc * 4 + c4) % 2
                P.op("act", lambda e: e.activation(out=yTst[ci][:].rearrange("p a b -> p (a b)"), in_=PCb[:, 0:1024], func=AF.Copy), reads=["PC0"], writes=["yTst%d" % ci])
                P.dma("sp", lambda q: q.dma_start(out=d["yTs"][:, :, tk:tk + 128], in_=yTst[ci][:]), "yTst%d" % ci, reads=["yTst%d" % ci], writes=["yTs_dram"])
                chk(C, "epi")
                cd = sm[:, 112:128]
                P.op("dve", lambda e: e.tensor_tensor(out=h3(St[:]), in0=h3(St[:]), in1=cd.unsqueeze(2).broadcast_to([128, 16, 64]), op=ALU.mult), reads=["St", ("cd", 0), ("cd", 1)], writes=["St"])
                P.op("dve", lambda e: e.tensor_tensor(out=St[:], in0=St[:], in1=PB[:], op=ALU.add), reads=["St", "PB0", "PB1"], writes=["St"])
                P.op("act", lambda e: e.activation(out=Sbf[:], in_=St[:], func=AF.Copy), reads=["St"], writes=["Sbf"])
                chk(C, "chunk%d" % (sc * 4 + c4r))
        P.barrier()


def lstm_pass(C, b):
    P, nc, d = C.P, C.nc, C.d
    S = C.S
    with ExitStack() as es:
        sb = lambda n, s, dt: es.enter_context(nc.sbuf_tensor("ls%d_" % b + n, s, dt))
        Wq = sb("Wq", [128, 8, 512], BF16)
        Wk = sb("Wk", [128, 8, 512], BF16)
        Wv = sb("Wv", [128, 8, 1024], BF16)
        Wo = sb("Wo", [128, 8, 1024], BF16)
        Wif = sb("Wif", [128, 8, 16], BF16)
        win = d["w_in"].rearrange("(kc p) c -> p kc c", p=128)
        P.dma("pool", lambda q: q.dma_start(out=Wq[:], in_=win[:, :, 3088:3600]), "Wq", writes=["Wq"])
        P.dma("pool", lambda q: q.dma_start(out=Wk[:], in_=win[:, :, 3600:4112]), "Wk", writes=["Wk"])
        P.dma("pool", lambda q: q.dma_start(out=Wv[:], in_=win[:, :, 4112:5136]), "Wv", writes=["Wv"])
        P.dma("pool", lambda q: q.dma_start(out=Wo[:], in_=win[:, :, 5136:6160]), "Wo", writes=["Wo"])
        P.dma("pool", lambda q: q.dma_start(out=Wif[:], in_=win[:, :, 6160:6176]), "Wif", writes=["Wif"])
        rp = sb("rp", [128, 16 + 1024], F32)
        P.dma("sp", lambda q: q.dma_start(out=rp[:], in_=d["rep_lstm"][:, :]), "rp", writes=["rp"])
        ifb, normw = rp[:, 0:16], rp[:, 16:1040]

        xTs = [sb("xTs%d" % i, [128, 8, 512], BF16) for i in range(2)]
        qT = sb("qT", [128, 2, 4, 512], BF16)
        kT = sb("kT", [128, 4, 512], BF16)
        k_tok4 = sb("k_tok4", [128, 4, 512], BF16)
        kw = sb("kw", [128, 512], BF16)
        v_tok4 = sb("v_tok4", [128, 4, 1024], BF16)
        so4 = sb("so4", [128, 4, 1024], F32)
        rhsb = sb("rhsb", [128, 8, 128], F32)
        D1 = sb("D1", [128, 8, 128], F32)
        WT = sb("WT", [128, 8, 128], F32)
        scT = sb("scT", [128, 8, 128], BF16)
        hA = sb("hA", [128, 1024], F32)
        hB = sb("hB", [128, 1024], F32)
        hbf = sb("hbf", [128, 1024], BF16)
        Cst = sb("Cst", [128, 4, 128], F32)
        Cbf = sb("Cbf", [128, 4, 128], BF16)
        Ctmp = sb("Ctmp", [128, 4, 128], F32)
        nst = sb("nst", [128, 4], F32)
        nbf = sb("nbf", [128, 4], BF16)
        ntmp = sb("ntmp", [128, 4], F32)
        mprev = sb("mprev", [128, 8], F32)
        sm = sb("sm", [128, 256], F32)
        ss = sb("ss", [128, 16], F32)
        yTst = [sb("yTst%d" % i, [128, 8, 128], BF16) for i in range(2)]

        PA, PB, PC, PD = C.PA, C.PB, C.PC, C.PD
        PCb = PC[:].bitcast(BF16)
        h3 = lambda ap: ap.rearrange("p (h q) -> p h q", h=8)
        P.op("dve", lambda e: e.memset(Cst[:], 0.0), writes=["Cst"])
        P.op("dve", lambda e: e.memset(Cbf[:], 0.0), writes=["Cbf"])
        P.op("dve", lambda e: e.memset(nst[:], 0.0), writes=["nst"])
        P.op("dve", lambda e: e.memset(nbf[:], 0.0), writes=["nbf"])
        P.op("dve", lambda e: e.memset(mprev[:], NEG), writes=["mprev"])
        P.op("dve", lambda e: e.memset(qT[:].rearrange("p a b c -> p (a b c)"), 0.0), writes=[("qT", i) for i in range(4)])

        def S_(name, lo, n):
            return sm[:, lo:lo + n], ("sm", name)
        nsc = S // 512
        for sc in range(nsc):
            tok0 = b * S + sc * 512
            xs = xTs[sc % 2]
            xk = "xTs%d" % (sc % 2)
            P.dma("sp", lambda q: q.dma_start(out=xs[:], in_=d["xT"][:, :, tok0:tok0 + 512]), "l" + xk, reads=["xT_dram"], writes=[xk])
            for blk in range(4):
                bank = PD[:, (blk % 2) * 512:(blk % 2 + 1) * 512]
                bk = "PD%d" % (blk % 2)
                mmg(P, bank, [(Wq[:, kc, blk * 128:(blk + 1) * 128], xs[:, kc, :]) for kc in range(8)], reads=["Wq", xk], writes=[bk])
                P.op("act", lambda e: e.activation(out=qT[0:64, 0, blk, :], in_=bank[0:64], func=AF.Copy, scale=0.125), reads=[bk], writes=[("qT", blk)])
                P.op("act", lambda e: e.activation(out=qT[64:128, 1, blk, :], in_=bank[64:128], func=AF.Copy, scale=0.125), reads=[bk, ("qT", blk)], writes=[("qT", blk)])
            for blk in range(4):
                bank = PD[:, (blk % 2) * 512:(blk % 2 + 1) * 512]
                bk = "PD%d" % (blk % 2)
                mmg(P, bank, [(Wk[:, kc, blk * 128:(blk + 1) * 128], xs[:, kc, :]) for kc in range(8)], reads=["Wk", xk], writes=[bk])
                P.op("dve", lambda e: e.tensor_copy(out=kT[:, blk, :], in_=bank), reads=[bk], writes=[("kT", blk)])
            for c4 in range(4):
                cs = slice(c4 * 128, (c4 + 1) * 128)
                for nb in range(2):
                    mmg(P, PA[:, nb * 512:(nb + 1) * 512], [(xs[:, kc, cs], Wv[:, kc, nb * 512:(nb + 1) * 512]) for kc in range(8)], reads=[xk, "Wv"], writes=["PA%d" % nb])
                P.op("dve", lambda e: e.tensor_copy(out=v_tok4[:, c4, :], in_=PA[:]), reads=["PA0", "PA1"], writes=[("v_tok", c4)])
                for nb in range(2):
                    mmg(P, PB[:, nb * 512:(nb + 1) * 512], [(xs[:, kc, cs], Wo[:, kc, nb * 512:(nb + 1) * 512]) for kc in range(8)], reads=[xk, "Wo"], writes=["PB%d" % nb])
                P.op("act", lambda e: e.activation(out=so4[:, c4, :], in_=PB[:], func=AF.Sigmoid), reads=["PB0", "PB1"], writes=[("so", c4)])
                mmg(P, PC[:, 0:512], [(xs[:, kc, cs], Wk[:, kc, :]) for kc in range(8)], reads=[xk, "Wk"], writes=["PC0"])
                P.op("dve", lambda e: e.tensor_copy(out=k_tok4[:, c4, :], in_=PC[:, 0:512]), reads=["PC0"], writes=[("k_tok", c4)])
            qTk = [("qT", i) for i in range(4)]
            kTk = [("kT", i) for i in range(4)]
            for c4 in range(4):
                cs = slice(c4 * 128, (c4 + 1) * 128)
                tk = tok0 + c4 * 128
                C.cur = sc * 4 + c4
                v_tok, so, k_tok = v_tok4[:, c4, :], so4[:, c4, :], k_tok4[:, c4, :]
                kv, kso, kk = ("v_tok", c4), ("so", c4), ("k_tok", c4)
                mmg(P, PC[:, 512:528], [(xs[:, kc, cs], Wif[:, kc, :]) for kc in range(8)], reads=[xk, "Wif"], writes=["PC1"])
                chk(C, "proj")
                lif, k_lif = S_("lif", 0, 16)
                P.op("dve", lambda e: e.tensor_tensor(out=lif, in0=PC[:, 512:528], in1=ifb, op=ALU.add), reads=["PC1", "rp"], writes=[k_lif])
                spx, k_spx = S_("spx", 16, 8)
                P.op("act", lambda e: e.activation(out=spx, in_=sm[:, 8:16], func=AF.Exp, scale=-1.0), reads=[k_lif], writes=[k_spx])
                P.op("act", lambda e: e.activation(out=spx, in_=spx, func=AF.Ln, bias=1.0), reads=[k_spx], writes=[k_spx])
                lf, k_lf = S_("lf", 24, 8)
                P.op("dve", lambda e: e.tensor_scalar(out=lf, in0=spx, scalar1=-1.0, scalar2=None, op0=ALU.mult), reads=[k_spx], writes=[k_lf])
                P.op("pe", lambda e: e.matmul(PC[:, 528:536], lhsT=C.U, rhs=lf, start=True, stop=True), reads=["cst", k_lf], writes=["PC1"])
                P.op("pe", lambda e: e.matmul(PC[:, 536:544], lhsT=C.ones, rhs=lf, start=True, stop=True), reads=["cst", k_lf], writes=["PC1"])
                fcum, k_fcum = S_("fcum", 32, 8)
                P.op("act", lambda e: e.activation(out=fcum, in_=PC[:, 528:536], func=AF.Copy), reads=["PC1"], writes=[k_fcum])
                g, k_g = S_("g", 40, 8)
                P.op("dve", lambda e: e.tensor_tensor(out=g, in0=sm[:, 0:8], in1=PC[:, 528:536], op=ALU.subtract), reads=[k_lif, "PC1"], writes=[k_g])
                chk(C, "gates")
                P.op("dve", lambda e: e.tensor_tensor(out=rhsb[:], in0=C.ident.unsqueeze(1).broadcast_to([128, 8, 128]), in1=g.unsqueeze(2).broadcast_to([128, 8, 128]), op=ALU.mult),
                     reads=["cst", k_g], writes=["rhsb"])
                for q4 in range(2):
                    P.op("pe", lambda e, q4=q4: e.matmul(PD[:, q4 * 512:(q4 + 1) * 512], lhsT=C.ones, rhs=rhsb[:, q4 * 4:(q4 + 1) * 4, :].rearrange("p a b -> p (a b)"), start=True, stop=True),
                         reads=["cst", "rhsb"], writes=["PD%d" % q4])
                PD3 = PD[:].rearrange("p (a b) -> p a b", a=8)
                gmax, k_gmax = S_("gmax", 48, 8)
                P.op("dve", lambda e: e.tensor_reduce(out=gmax, in_=PD3, axis=AX.X, op=ALU.max), reads=["PD0", "PD1"], writes=[k_gmax])
                P.op("dve", lambda e: e.tensor_tensor(out=D1[:], in0=PD3, in1=C.nm.unsqueeze(1).broadcast_to([128, 8, 128]), op=ALU.add), reads=["PD0", "PD1", "cst"], writes=["D1"])
                m2, k_m2 = S_("m2", 56, 8)
                P.op("dve", lambda e: e.tensor_reduce(out=m2, in_=D1[:], axis=AX.X, op=ALU.max), reads=["D1"], writes=[k_m2])
                P.op("dve", lambda e: e.tensor_tensor(out=m2, in0=m2, in1=mprev[:], op=ALU.max), reads=[k_m2, "mprev"], writes=[k_m2])
                chk(C, "gb")
                P.op("dve", lambda e: e.tensor_tensor(out=rhsb[:], in0=C.ident.unsqueeze(1).broadcast_to([128, 8, 128]), in1=m2.unsqueeze(2).broadcast_to([128, 8, 128]), op=ALU.mult),
                     reads=["cst", k_m2], writes=["rhsb"])
                for q4 in range(2):
                    P.op("pe", lambda e, q4=q4: e.matmul(PA[:, q4 * 512:(q4 + 1) * 512], lhsT=C.ones, rhs=rhsb[:, q4 * 4:(q4 + 1) * 4, :].rearrange("p a b -> p (a b)"), start=True, stop=True),
                         reads=["cst", "rhsb"], writes=["PA%d" % q4])
                PA3 = PA[:].rearrange("p (a b) -> p a b", a=8)
                chk(C, "m2b")
                P.op("dve", lambda e: e.scalar_tensor_tensor(out=WT[:], in0=PA3, scalar=-1.0, in1=C.nmT.unsqueeze(1).broadcast_to([128, 8, 128]), op0=ALU.mult, op1=ALU.add),
                     reads=["PA0", "PA1", "cst"], writes=["WT"])
                P.op("dve", lambda e: e.tensor_tensor(out=WT[:], in0=WT[:], in1=g.unsqueeze(2).broadcast_to([128, 8, 128]), op=ALU.add), reads=["WT", k_g], writes=["WT"])
                P.op("act", lambda e: e.activation(out=WT[:], in_=WT[:], func=AF.Exp), reads=["WT"], writes=["WT"])
                chk(C, "wt")
                iw, k_iw = S_("iw", 64, 8)
                P.op("dve", lambda e: e.tensor_tensor(out=iw, in0=mprev[:], in1=m2, op=ALU.subtract), reads=["mprev", k_m2], writes=[k_iw])
                P.op("act", lambda e: e.activation(out=iw, in_=iw, func=AF.Exp), reads=[k_iw], writes=[k_iw])
                emt, k_emt = S_("emt", 72, 8)
                P.op("dve", lambda e: e.tensor_tensor(out=emt, in0=fcum, in1=m2, op=ALU.add), reads=[k_fcum, k_m2], writes=[k_emt])
                P.op("act", lambda e: e.activation(out=emt, in_=emt, func=AF.Exp, scale=-1.0), reads=[k_emt], writes=[k_emt])
                chk(C, "iw")
                for h in range(8):
                    ps = slice((h % 2) * 64, (h % 2) * 64 + 64)
                    P.op("pe", lambda e, h=h, ps=ps: e.matmul(PD[:, h * 128:(h + 1) * 128], lhsT=kT[:, h // 2, cs], rhs=qT[:, h % 2, h // 2, cs], start=True, stop=True),
                         reads=qTk + kTk, writes=["PD%d" % (h // 4)], inc=(h % 4 == 3))
                P.op("dve", lambda e: e.tensor_tensor(out=scT[:], in0=PD3, in1=WT[:], op=ALU.mult), reads=["PD0", "PD1", "WT"], writes=["scT"])
                chk(C, "qk")
                for h in range(8):
                    P.op("pe", lambda e, h=h: e.matmul(PA[:, h * 128:(h + 1) * 128], lhsT=scT[:, h, :], rhs=v_tok[:, h * 128:(h + 1) * 128], start=True, stop=True),
                         reads=["scT", kv], writes=["PA%d" % (h // 4)], inc=(h % 4 == 3))
                for h in range(8):
                    P.op("pe", lambda e, h=h: e.matmul(PC[:, 544 + h:545 + h], lhsT=scT[:, h, :], rhs=C.onesb[:, 0:1], start=True, stop=True),
                         reads=["scT", "cstb"], writes=["PC1"], inc=(h == 7))
                for h in range(8):
                    ps = slice((h % 2) * 64, (h % 2) * 64 + 64)
                    P.op("pe", lambda e, h=h, ps=ps: e.matmul(PB[:, h * 128:(h + 1) * 128], lhsT=qT[:, h % 2, h // 2, cs], rhs=Cbf[:, h // 2, :], start=True, stop=True),
                         reads=qTk + ["Cbf"], writes=["PB%d" % (h // 4)], inc=(h % 4 == 3))
                for h in range(8):
                    ps = slice((h % 2) * 64, (h % 2) * 64 + 64)
                    P.op("pe", lambda e, h=h, ps=ps: e.matmul(PC[:, 552 + h:553 + h], lhsT=qT[:, h % 2, h // 2, cs], rhs=nbf[:, h // 2:h // 2 + 1], start=True, stop=True),
                         reads=qTk + ["nbf"], writes=["PC1"], inc=(h == 7))
                chk(C, "nums")
                P.op("dve", lambda e: e.tensor_tensor(out=h3(hA[:]), in0=h3(PB[:]), in1=iw.unsqueeze(2).broadcast_to([128, 8, 128]), op=ALU.mult), reads=["PB0", "PB1", k_iw], writes=["hA"])
                P.op("dve", lambda e: e.tensor_tensor(out=hA[:], in0=hA[:], in1=PA[:], op=ALU.add), reads=["hA", "PA0", "PA1"], writes=["hA"])
                den, k_den = S_("den", 80, 8)
                P.op("dve", lambda e: e.tensor_tensor(out=den, in0=PC[:, 552:560], in1=iw, op=ALU.mult), reads=["PC1", k_iw], writes=[k_den])
                P.op("dve", lambda e: e.tensor_tensor(out=den, in0=den, in1=PC[:, 544:552], op=ALU.add), reads=[k_den, "PC1"], writes=[k_den])
                P.op("act", lambda e: e.activation(out=den, in_=den, func=AF.Abs), reads=[k_den], writes=[k_den])
                P.op("dve", lambda e: e.tensor_tensor(out=den, in0=den, in1=emt, op=ALU.max), reads=[k_den, k_emt], writes=[k_den])
                P.op("dve", lambda e: e.reciprocal(out=den, in_=den), reads=[k_den], writes=[k_den])
                P.op("dve", lambda e: e.tensor_tensor(out=h3(hA[:]), in0=h3(hA[:]), in1=den.unsqueeze(2).broadcast_to([128, 8, 128]), op=ALU.mult), reads=["hA", k_den], writes=["hA"])
                chk(C, "den")
                P.op("dve", lambda e: e.memset(ss[:], 0.0), writes=["ss"])
                for h in range(8):
                    P.op("act", lambda e, h=h: e.activation(out=hB[:, h * 128:(h + 1) * 128], in_=hA[:, h * 128:(h + 1) * 128], func=AF.Square, accum_out=ss[:, h:h + 1]), reads=["hA", "ss"], writes=["hB", "ss"])
                P.op("dve", lambda e: e.tensor_scalar(out=ss[:, 8:16], in0=ss[:, 0:8], scalar1=1.0 / 128.0, scalar2=EPS, op0=ALU.mult, op1=ALU.add), reads=["ss"], writes=["ss"])
                P.op("act", lambda e: e.activation(out=ss[:, 8:16], in_=ss[:, 8:16], func=AF.Ln), reads=["ss"], writes=["ss"])
                P.op("act", lambda e: e.activation(out=ss[:, 8:16], in_=ss[:, 8:16], func=AF.Exp, scale=-0.5), reads=["ss"], writes=["ss"])
                P.op("dve", lambda e: e.tensor_tensor(out=h3(hA[:]), in0=h3(hA[:]), in1=ss[:, 8:16].unsqueeze(2).broadcast_to([128, 8, 128]), op=ALU.mult), reads=["hA", "ss"], writes=["hA"])
                P.op("dve", lambda e: e.tensor_tensor(out=hB[:], in0=so, in1=normw, op=ALU.mult), reads=[kso, "rp", "hB"], writes=["hB"])
                P.op("dve", lambda e: e.tensor_tensor(out=hbf[:], in0=hA[:], in1=hB[:], op=ALU.mult), reads=["hA", "hB"], writes=["hbf"])
                for blk in range(8):
                    P.op("pe", lambda e, blk=blk: e.transpose(out=PCb[:, blk * 128:(blk + 1) * 128], in_=hbf[:, blk * 128:(blk + 1) * 128], identity=C.identb),
                         reads=["hbf", "cstb"], writes=["PC0"], inc=(blk == 7))
                ci = (sc * 4 + c4) % 2
                P.op("act", lambda e: e.activation(out=yTst[ci][:].rearrange("p a b -> p (a b)"), in_=PCb[:, 0:1024], func=AF.Copy), reads=["PC0"], writes=["lyTst%d" % ci])
                P.dma("sp", lambda q: q.dma_start(out=d["yTl"][:, :, tk:tk + 128], in_=yTst[ci][:]), "lyTst%d" % ci, reads=["lyTst%d" % ci], writes=["yTl_dram"])
                chk(C, "out")
                wend, k_wend = S_("wend", 88, 8)
                P.op("dve", lambda e: e.tensor_tensor(out=wend, in0=g, in1=gmax, op=ALU.subtract), reads=[k_g, k_gmax], writes=[k_wend])
                P.op("act", lambda e: e.activation(out=wend, in_=wend, func=AF.Exp), reads=[k_wend], writes=[k_wend])
                k3 = lambda ap: ap.rearrange("p (h q) -> p h q", h=8)
                P.op("dve", lambda e: e.tensor_tensor(out=k3(kw[:]), in0=k3(k_tok), in1=wend.unsqueeze(2).broadcast_to([128, 8, 64]), op=ALU.mult), reads=[kk, k_wend], writes=["kw"])
                for pr in range(4):
                    for hp in range(2):
                        h = pr * 2 + hp
                        P.op("pe", lambda e, pr=pr, hp=hp, h=h: e.matmul(PB[:, (hp * 4 + pr) * 128:(hp * 4 + pr + 1) * 128], lhsT=kw[:, pr * 128:(pr + 1) * 128], rhs=v_tok[:, h * 128:(h + 1) * 128], start=True, stop=True),
                             reads=["kw", kv], writes=["PB%d" % hp], inc=(pr == 3))
                for pr in range(4):
                    P.op("pe", lambda e, pr=pr: e.matmul(PC[:, 560 + pr:561 + pr], lhsT=kw[:, pr * 128:(pr + 1) * 128], rhs=C.onesb[:, 0:1], start=True, stop=True),
                         reads=["kw", "cstb"], writes=["PC1"], inc=(pr == 3))
                chk(C, "cloc")
                mloc, k_mloc = S_("mloc", 96, 8)
                P.op("dve", lambda e: e.tensor_tensor(out=mloc, in0=PC[:, 536:544], in1=gmax, op=ALU.add), reads=["PC1", k_gmax], writes=[k_mloc])
                aa, k_aa = S_("aa", 104, 8)
                P.op("dve", lambda e: e.tensor_tensor(out=aa, in0=PC[:, 536:544], in1=mprev[:], op=ALU.add), reads=["PC1", "mprev"], writes=[k_aa])
                P.op("dve", lambda e: e.tensor_tensor(out=mprev[:], in0=aa, in1=mloc, op=ALU.max), reads=[k_aa, k_mloc, "mprev"], writes=["mprev"])
                sps, k_sps = S_("sps", 112, 16)
                P.op("dve", lambda e: e.tensor_tensor(out=sm[:, 112:120], in0=aa, in1=mprev[:], op=ALU.subtract), reads=[k_aa, "mprev"], writes=[k_sps])
                P.op("dve", lambda e: e.tensor_tensor(out=sm[:, 120:128], in0=mloc, in1=mprev[:], op=ALU.subtract), reads=[k_mloc, "mprev", k_sps], writes=[k_sps])
                P.op("act", lambda e: e.activation(out=sps, in_=sps, func=AF.Exp), reads=[k_sps], writes=[k_sps])
                PB4 = PB[:].rearrange("p (a b c) -> p a b c", a=2, b=4)
                for hp in range(2):
                    ps = slice(hp * 64, hp * 64 + 64)
                    sp_h = sm[ps, 112 + hp:120:2]
                    sl_h = sm[ps, 120 + hp:128:2]
                    P.op("dve", lambda e: e.tensor_tensor(out=Cst[ps], in0=Cst[ps], in1=sp_h.unsqueeze(2).broadcast_to([64, 4, 128]), op=ALU.mult), reads=["Cst", k_sps], writes=["Cst"])
                    P.op("dve", lambda e: e.tensor_tensor(out=Ctmp[ps], in0=PB4[ps, hp], in1=sl_h.unsqueeze(2).broadcast_to([64, 4, 128]), op=ALU.mult), reads=["PB%d" % hp, k_sps], writes=["Ctmp"])
                    P.op("dve", lambda e: e.tensor_tensor(out=Cst[ps], in0=Cst[ps], in1=Ctmp[ps], op=ALU.add), reads=["Cst", "Ctmp"], writes=["Cst"])
                    P.op("dve", lambda e: e.tensor_tensor(out=nst[ps], in0=nst[ps], in1=sp_h, op=ALU.mult), reads=["nst", k_sps], writes=["nst"])
                    P.op("dve", lambda e: e.tensor_tensor(out=ntmp[ps], in0=PC[ps, 560:564], in1=sl_h, op=ALU.mult), reads=["PC1", k_sps], writes=["ntmp"])
                    P.op("dve", lambda e: e.tensor_tensor(out=nst[ps], in0=nst[ps], in1=ntmp[ps], op=ALU.add), reads=["nst", "ntmp"], writes=["nst"])
                P.op("act", lambda e: e.activation(out=Cbf[:], in_=Cst[:], func=AF.Copy), reads=["Cst"], writes=["Cbf"])
                P.op("act", lambda e: e.activation(out=nbf[:], in_=nst[:], func=AF.Copy), reads=["nst"], writes=["nbf"])
        P.barrier()


ALPHA = float((2.0 * 1) ** 0.25)
CAP = 256
NE = 64


def phaseB(C):
    P, nc, d = C.P, C.nc, C.d
    TOK = C.TOK
    with ExitStack() as es:
        sb = lambda n, s, dt: es.enter_context(nc.sbuf_tensor("pb_" + n, s, dt))
        win = d["w_in"].rearrange("(kc p) c -> p kc c", p=128)
        Wa = sb("Wa", [128, 8, 1024], BF16)
        Wb = sb("Wb", [128, 8, 1024], BF16)
        Wgs = sb("Wgs", [128, 8, 1024], BF16)
        Wgl = sb("Wgl", [128, 8, 1024], BF16)
        Wout = sb("Wout", [128, 8, 1024], BF16)
        P.dma("pool", lambda q: q.dma_start(out=Wa[:], in_=d["w_a"].rearrange("(kc p) c -> p kc c", p=128)), "Wa", writes=["Wa"])
        P.dma("pool", lambda q: q.dma_start(out=Wgs[:], in_=win[:, :, 6176:7200]), "Wgs", writes=["Wgs"])
        P.dma("pool", lambda q: q.dma_start(out=Wb[:], in_=d["w_b"].rearrange("(kc p) c -> p kc c", p=128)), "Wb", writes=["Wb"])
        P.dma("pool", lambda q: q.dma_start(out=Wgl[:], in_=win[:, :, 7200:8224]), "Wgl", writes=["Wgl"])
        P.dma("pool", lambda q: q.dma_start(out=Wout[:], in_=d["w_out"].rearrange("(kc p) c -> p kc c", p=128)), "Wout", writes=["Wout"])
        Wr = sb("Wr", [128, 8, 72], F32)
        P.dma("sp", lambda q: q.dma_start(out=Wr[:], in_=d["w_router"].rearrange("(kc p) c -> p kc c", p=128)), "Wr", writes=["Wr"])
        rp = sb("rp", [128, 2048 + 72 + 64], F32)
        P.dma("sp", lambda q: q.dma_start(out=rp[:], in_=d["rep_mid"][:, :]), "rp", writes=["rp"])
        lng, lnb, rbias = rp[:, 0:1024], rp[:, 1024:2048], rp[:, 2048:2120]
        cnti = sb("cnti", [128, 64], F32)
        P.op("dve", lambda e: e.tensor_copy(out=cnti[:], in_=rp[:, 2120:2184]), reads=["rp"], writes=["cnti"])

        xTs = [sb("xTs%d" % i, [128, 8, 512], BF16) for i in range(2)]
        yTs = [sb("yTs%d" % i, [128, 8, 512], BF16) for i in range(2)]
        yTl = [sb("yTl%d" % i, [128, 8, 512], BF16) for i in range(2)]
        mT = [sb("mT%d" % i, [128, 8, 512], BF16) for i in range(2)]
        sg = [sb("sg%d" % i, [128, 512], F32) for i in range(2)]
        t1 = [sb("t1%d" % i, [128, 512], F32) for i in range(2)]
        xf = [sb("xf%d" % i, [128, 1024], F32) for i in range(2)]
        r = sb("r", [128, 1024], F32)
        junk = sb("junk", [128, 1024], F32)
        h1 = [sb("h1%d" % i, [128, 1024], F32) for i in range(2)]
        h1b = [sb("h1b%d" % i, [128, 1024], BF16) for i in range(2)]
        h1T = sb("h1T", [128, 8, 128], F32)
        st = sb("st", [128, 16], F32)
        lg = sb("lg", [128, 72], F32)
        rt = sb("rt", [128, 512], F32)
        ohsb = sb("ohsb", [128, 64], BF16)

        PA, PB, PC, PD = C.PA, C.PB, C.PC, C.PD
        NTB = TOK // 512

        def loadsB(tb):
            s2 = tb % 2
            tok0 = tb * 512
            for nm_, buf, src in (("xTs", xTs, "xT"), ("yTs", yTs, "yTs"), ("yTl", yTl, "yTl")):
                P.dma("sp", lambda q, buf=buf, src=src: q.dma_start(out=buf[s2][:], in_=d[src][:, :, tok0:tok0 + 512]), "b%s%d" % (nm_, s2),
                      reads=[src + "_dram"], writes=["%s%d" % (nm_, s2)])

        def mblock(tb, m):
            s2 = tb % 2
            kx, ks, kl = "xTs%d" % s2, "yTs%d" % s2, "yTl%d" % s2
            ms = slice(m * 128, (m + 1) * 128)
            mmg(P, PA[:, 0:512], [(Wa[:, kc, ms], yTs[s2][:, kc, :]) for kc in range(8)], reads=["Wa", ks], writes=["PA0"])
            mmg(P, PA[:, 512:1024], [(Wgs[:, kc, ms], xTs[s2][:, kc, :]) for kc in range(8)], reads=["Wgs", kx], writes=["PA1"])
            mmg(P, PB[:, 0:512], [(Wb[:, kc, ms], yTl[s2][:, kc, :]) for kc in range(8)], reads=["Wb", kl], writes=["PB0"])
            mmg(P, PB[:, 512:1024], [(Wgl[:, kc, ms], xTs[s2][:, kc, :]) for kc in range(8)], reads=["Wgl", kx], writes=["PB1"])
            P.op("act", lambda e: e.activation(out=sg[0][:], in_=PA[:, 512:1024], func=AF.Sigmoid), reads=["PA1"], writes=["sg0"])
            P.op("act", lambda e: e.activation(out=sg[1][:], in_=PB[:, 512:1024], func=AF.Sigmoid), reads=["PB1"], writes=["sg1"])
            P.op("dve", lambda e: e.tensor_tensor(out=t1[0][:], in0=sg[0][:], in1=PA[:, 0:512], op=ALU.mult), reads=["sg0", "PA0"], writes=["t10"])
            P.op("dve", lambda e: e.tensor_tensor(out=t1[1][:], in0=sg[1][:], in1=PB[:, 0:512], op=ALU.mult), reads=["sg1", "PB0"], writes=["t11"])
            P.op("dve", lambda e: e.tensor_tensor(out=mT[s2][:, m, :], in0=t1[0][:], in1=t1[1][:], op=ALU.add), reads=["t10", "t11"], writes=[("mT", s2, m)])

        def out_ln(tb, c4):
            s2 = tb % 2
            nch = tb * 4 + c4
            ci = nch % 2
            cs = slice(c4 * 128, (c4 + 1) * 128)
            tk = tb * 512 + c4 * 128
            mTk = [("mT", s2, m) for m in range(8)]
            P.dma("sp", lambda q: q.dma_start(out=xf[ci][:], in_=d["x"][tk:tk + 128, :]), "bxf%d" % ci, writes=["xf%d" % ci])
            for nb in range(2):
                mmg(P, PC[:, nb * 512:(nb + 1) * 512], [(mT[s2][:, kc, cs], Wout[:, kc, nb * 512:(nb + 1) * 512]) for kc in range(8)], reads=mTk + ["Wout"], writes=["PC%d" % nb])
            P.op("dve", lambda e: e.scalar_tensor_tensor(out=r[:], in0=xf[ci][:], scalar=ALPHA, in1=PC[:], op0=ALU.mult, op1=ALU.add), reads=["xf%d" % ci, "PC0", "PC1"], writes=["r"])
            layer_norm(C, P, r, "r", junk, st, lng, lnb, "rp", h1[ci], "h1%d" % ci)
            P.dma("sp", lambda q: q.dma_start(out=d["h1"][tk:tk + 128, :], in_=h1[ci][:]), "bh1%d" % ci, reads=["h1%d" % ci], writes=["h1_dram"])
            P.op("act", lambda e: e.activation(out=h1b[ci][:], in_=h1[ci][:], func=AF.Copy), reads=["h1%d" % ci], writes=["h1b%d" % ci])

        def logits(tb, c4):
            nch = tb * 4 + c4
            ci = nch % 2
            for kc in range(8):
                P.op("pe", lambda e, kc=kc: e.transpose(out=PD[:, kc * 128:(kc + 1) * 128], in_=h1[ci][:, kc * 128:(kc + 1) * 128], identity=C.ident),
                     reads=["h1%d" % ci, "cst"], writes=["PD%d" % (kc // 4)], inc=(kc % 4 == 3))
            P.op("act", lambda e: e.activation(out=h1T[:].rearrange("p a b -> p (a b)"), in_=PD[:], func=AF.Copy), reads=["PD0", "PD1"], writes=["h1T"])
            mmg(P, PD[:, 0:72], [(h1T[:, kc, :], Wr[:, kc, :]) for kc in range(8)], reads=["h1T", "Wr"], writes=["PD0"])
            P.op("dve", lambda e: e.tensor_tensor(out=lg[:], in0=PD[:, 0:72], in1=rbias, op=ALU.add), reads=["PD0", "rp"], writes=["lg"])

        def route_front(tb, c4):
            route(C, P, lg, rt, cnti, ohsb, PD, "PD", tb * 4 + c4, part="front")

        def dispatch(tb, c4):
            nch = tb * 4 + c4
            ci = nch % 2
            route(C, P, lg, rt, cnti, ohsb, PD, "PD", nch, part="back")
            for k in range(2):
                P.dma("pool", lambda q, k=k: q.indirect_dma_start(out=d["xs"][:, :], out_offset=bass.IndirectOffsetOnAxis(ap=C.slots[:, nch, k:k + 1], axis=0),
                                                                 in_=h1b[ci][:], in_offset=None),
                      "bsc%d" % ci, reads=["h1b%d" % ci, ("slots", nch)], writes=[("xs_dram", nch, k)])

        pend = None
        loadsB(0)
        for m in range(8):
            mblock(0, m)
        for tb in range(NTB):
            nxt = tb + 1 < NTB
            if nxt:
                loadsB(tb + 1)
            for c4 in range(4):
                out_ln(tb, c4)
                if pend is not None:
                    dispatch(*pend)
                if nxt:
                    mblock(tb + 1, 2 * c4)
                logits(tb, c4)
                route_front(tb, c4)
                if nxt:
                    mblock(tb + 1, 2 * c4 + 1)
                pend = (tb, c4)
        dispatch(*pend)
        P.barrier()


def layer_norm(C, P, x, kx, junk, st, g, b, kgb, out, kout):
    P.op("dve", lambda e: e.memset(st[:, 0:2], 0.0), writes=["st"])
    P.op("act", lambda e: e.activation(out=junk[:], in_=x[:], func=AF.Copy, accum_out=st[:, 0:1]), reads=[kx, "st"], writes=["junk", "st"])
    P.op("act", lambda e: e.activation(out=junk[:], in_=x[:], func=AF.Square, accum_out=st[:, 1:2]), reads=[kx, "st", "junk"], writes=["junk", "st"])
    P.op("dve", lambda e: e.tensor_scalar(out=st[:, 2:4], in0=st[:, 0:2], scalar1=1.0 / 1024.0, scalar2=None, op0=ALU.mult), reads=["st"], writes=["st"])
    P.op("dve", lambda e: e.tensor_tensor(out=st[:, 4:5], in0=st[:, 2:3], in1=st[:, 2:3], op=ALU.mult), reads=["st"], writes=["st"])
    P.op("dve", lambda e: e.tensor_tensor(out=st[:, 5:6], in0=st[:, 3:4], in1=st[:, 4:5], op=ALU.subtract), reads=["st"], writes=["st"])
    P.op("dve", lambda e: e.tensor_scalar(out=st[:, 5:6], in0=st[:, 5:6], scalar1=EPS, scalar2=None, op0=ALU.add), reads=["st"], writes=["st"])
    P.op("act", lambda e: e.activation(out=st[:, 5:6], in_=st[:, 5:6], func=AF.Ln), reads=["st"], writes=["st"])
    P.op("act", lambda e: e.activation(out=st[:, 6:7], in_=st[:, 5:6], func=AF.Exp, scale=-0.5), reads=["st"], writes=["st"])
    P.op("dve", lambda e: e.tensor_scalar(out=x[:], in0=x[:], scalar1=st[:, 2:3], scalar2=st[:, 6:7], op0=ALU.subtract, op1=ALU.mult), reads=[kx, "st"], writes=[kx])
    P.op("dve", lambda e: e.tensor_tensor(out=x[:], in0=x[:], in1=g, op=ALU.mult), reads=[kx, kgb], writes=[kx])
    P.op("dve", lambda e: e.tensor_tensor(out=out[:], in0=x[:], in1=b, op=ALU.add), reads=[kx, kgb], writes=[kout])


def route(C, P, lg, rt, cnti, ohsb, PY, ny, nch, part=None):
    R = lambda lo, n: rt[:, lo:lo + n]
    k = "rt"
    gl = lg[:, 0:8]
    el = lg[:, 8:72].rearrange("p (g e) -> p g e", g=8)
    gmx, ge, gsum, ohg = R(0, 1), R(8, 8), R(1, 1), R(16, 8)
    emit = [part != "back"]
    D = lambda f, reads=(), writes=(k,): (P.op("dve", f, reads=list(reads) + [k, "lg"], writes=list(writes)) if emit[0] else None)
    A = lambda f, reads=(), writes=(k,): (P.op("act", f, reads=list(reads) + [k, "lg"], writes=list(writes)) if emit[0] else None)
    D(lambda e: e.tensor_reduce(out=gmx, in_=gl, axis=AX.X, op=ALU.max))
    D(lambda e: e.tensor_scalar(out=R(2, 1), in0=gmx, scalar1=-1.0, scalar2=None, op0=ALU.mult))
    D(lambda e: e.memset(gsum, 0.0))
    A(lambda e: e.activation(out=ge, in_=gl, func=AF.Exp, bias=R(2, 1), accum_out=gsum))
    D(lambda e: e.reciprocal(out=R(3, 1), in_=gsum))
    D(lambda e: e.tensor_scalar(out=ohg, in0=gl, scalar1=gmx, scalar2=None, op0=ALU.is_equal))
    msk = R(64, 64).rearrange("p (g e) -> p g e", g=8)
    D(lambda e: e.tensor_tensor(out=msk, in0=el, in1=ohg.unsqueeze(2).broadcast_to([128, 8, 8]), op=ALU.mult))
    ing = R(24, 8)
    D(lambda e: e.tensor_reduce(out=ing, in_=R(64, 64).rearrange("p (g e) -> p e g", g=8), axis=AX.X, op=ALU.add))
    m1, oh1, ing2, m2v, oh2 = R(4, 1), R(32, 8), R(40, 8), R(5, 1), R(48, 8)
    D(lambda e: e.tensor_reduce(out=m1, in_=ing, axis=AX.X, op=ALU.max))
    D(lambda e: e.tensor_scalar(out=oh1, in0=ing, scalar1=m1, scalar2=None, op0=ALU.is_equal))
    D(lambda e: e.scalar_tensor_tensor(out=ing2, in0=oh1, scalar=NEG, in1=ing, op0=ALU.mult, op1=ALU.add))
    D(lambda e: e.tensor_reduce(out=m2v, in_=ing2, axis=AX.X, op=ALU.max))
    D(lambda e: e.tensor_scalar(out=oh2, in0=ing2, scalar1=m2v, scalar2=None, op0=ALU.is_equal))
    D(lambda e: e.tensor_tensor(out=R(6, 1), in0=m2v, in1=m1, op=ALU.subtract))
    A(lambda e: e.activation(out=R(6, 1), in_=R(6, 1), func=AF.Exp))
    D(lambda e: e.tensor_scalar(out=R(7, 1), in0=R(6, 1), scalar1=1.0, scalar2=None, op0=ALU.add))
    D(lambda e: e.reciprocal(out=R(7, 1), in_=R(7, 1)))
    D(lambda e: e.tensor_tensor(out=R(56, 1), in0=R(7, 1), in1=R(3, 1), op=ALU.mult))
    D(lambda e: e.tensor_tensor(out=R(57, 1), in0=R(56, 1), in1=R(6, 1), op=ALU.mult))
    D(lambda e: e.tensor_copy(out=C.gates[:, nch, :], in_=R(56, 2)), writes=(("gates", nch),))
    OH1 = R(128, 64).rearrange("p (g e) -> p g e", g=8)
    OH2 = R(192, 64).rearrange("p (g e) -> p g e", g=8)
    D(lambda e: e.tensor_tensor(out=OH1, in0=ohg.unsqueeze(2).broadcast_to([128, 8, 8]), in1=oh1.unsqueeze(1).broadcast_to([128, 8, 8]), op=ALU.mult))
    D(lambda e: e.tensor_tensor(out=OH2, in0=ohg.unsqueeze(2).broadcast_to([128, 8, 8]), in1=oh2.unsqueeze(1).broadcast_to([128, 8, 8]), op=ALU.mult))
    D(lambda e: e.tensor_tensor(out=ohsb[:], in0=R(128, 64), in1=R(192, 64), op=ALU.add), writes=("ohsb",))
    if part == "front":
        return
    emit[0] = True
    P.op("pe", lambda e: e.matmul(PY[:, 128:192], lhsT=C.Ustrb, rhs=ohsb[:], start=True, stop=True), reads=["cstb", "ohsb"], writes=[ny + "0"])
    P.op("pe", lambda e: e.matmul(PY[:, 192:256], lhsT=C.onesb, rhs=ohsb[:], start=True, stop=True), reads=["cstb", "ohsb"], writes=[ny + "0"])
    base = R(256, 64)
    D(lambda e: e.tensor_tensor(out=base, in0=PY[:, 128:192], in1=cnti[:], op=ALU.add), reads=[ny + "0", "cnti"])
    D(lambda e: e.tensor_tensor(out=R(320, 64), in0=base, in1=R(128, 64), op=ALU.mult))
    D(lambda e: e.tensor_reduce(out=R(60, 1), in_=R(320, 64), axis=AX.X, op=ALU.add))
    D(lambda e: e.tensor_tensor(out=R(384, 64), in0=base, in1=R(192, 64), op=ALU.mult))
    D(lambda e: e.tensor_reduce(out=R(61, 1), in_=R(384, 64), axis=AX.X, op=ALU.add))
    D(lambda e: e.tensor_scalar(out=R(60, 2), in0=R(60, 2), scalar1=float(NE * CAP - 1), scalar2=None, op0=ALU.min))
    D(lambda e: e.tensor_copy(out=C.slots[:, nch, :], in_=R(60, 2)), writes=(("slots", nch),))
    D(lambda e: e.tensor_tensor(out=cnti[:], in0=cnti[:], in1=PY[:, 192:256], op=ALU.add), reads=[ny + "0", "cnti"], writes=("cnti",))


def zero_xs(C, es):
    P, nc, d = C.P, C.nc, C.d
    z = es.enter_context(nc.sbuf_tensor("zx", [128, 2, 1024], BF16))
    P.op("dve", lambda e: e.memset(z[:].rearrange("p a b -> p (a b)"), 0.0), writes=["zx"])
    xs3 = d["xs"].rearrange("(r t p) f -> r p t f", p=128, t=2)
    for r in range((NE + 1) * CAP // 256):
        P.dma("pool", lambda q, r=r: q.dma_start(out=xs3[r], in_=z[:]), "zx", reads=["zx"], writes=[("xs0", r)])


def phaseC(C):
    P, nc, d = C.P, C.nc, C.d
    with ExitStack() as es:
        sb = lambda n, s, dt: es.enter_context(nc.sbuf_tensor("pc_" + n, s, dt))
        Wg = [sb("Wg%d" % i, [128, 8, 512], BF16) for i in range(2)]
        Wu = [sb("Wu%d" % i, [128, 8, 512], BF16) for i in range(2)]
        Wd = [sb("Wd%d" % i, [128, 4, 1024], BF16) for i in range(2)]
        NS = 2
        Fg = [sb("Fg%d" % i, [128, 8, 512], F32) for i in range(NS)]
        Fu = [sb("Fu%d" % i, [128, 8, 512], F32) for i in range(NS)]
        Fd = [sb("Fd%d" % i, [128, 4, 1024], F32) for i in range(NS)]

        def load_w(e_):
            f = e_ % NS
            P.dma("sp", lambda q: q.dma_start(out=Fg[f][:], in_=d["moe_wg"][e_].rearrange("(kc p) f -> p kc f", p=128)), "cfg%d" % f, writes=["Fg%d" % f])
            P.dma("sp", lambda q: q.dma_start(out=Fu[f][:], in_=d["moe_wu"][e_].rearrange("(kc p) f -> p kc f", p=128)), "cfu%d" % f, writes=["Fu%d" % f])
            P.dma("sp", lambda q: q.dma_start(out=Fd[f][:], in_=d["moe_wd"][e_].rearrange("(fc p) n -> p fc n", p=128)), "cfd%d" % f, writes=["Fd%d" % f])

        def cast_w(e_):
            s, f = e_ % 2, e_ % NS
            f2 = lambda t: t[:].rearrange("p a b -> p (a b)")
            P.op("dve", lambda e: e.tensor_copy(out=f2(Wg[s]), in_=f2(Fg[f])), reads=["Fg%d" % f], writes=["Wg%d" % s])
            P.op("act", lambda e: e.activation(out=f2(Wu[s]), in_=f2(Fu[f]), func=AF.Copy), reads=["Fu%d" % f], writes=["Wu%d" % s])
            P.op("dve", lambda e: e.tensor_copy(out=f2(Wd[s]), in_=f2(Fd[f])), reads=["Fd%d" % f], writes=["Wd%d" % s])
        for e0 in range(NS):
            load_w(e0)
        cast_w(0)
        xse = [sb("xse%d" % i, [128, 2, 1024], BF16) for i in range(2)]
        xsT = sb("xsT", [128, 8, 256], BF16)
        hs = sb("hs", [128, 1024], F32)
        hidT = sb("hidT", [128, 4, 256], BF16)
        yo = [sb("yo%d" % i, [128, 2, 1024], F32) for i in range(2)]
        PA, PB, PC, PD = C.PA, C.PB, C.PC, C.PD
        PCb = PC[:].bitcast(BF16)
        def load_xs(e_):
            s = e_ % 2
            P.dma("pool", lambda q: q.dma_start(out=xse[s][:], in_=d["xs"][e_ * CAP:(e_ + 1) * CAP, :].rearrange("(t p) f -> p t f", p=128)), "cxs%d" % s, writes=["xse%d" % s])
        load_xs(0)
        for e_ in range(NE):
            s = e_ % 2
            for t in range(2):
                for kc in range(8):
                    P.op("pe", lambda e, t=t, kc=kc: e.transpose(out=PCb[:, kc * 256 + t * 128:kc * 256 + (t + 1) * 128], in_=xse[s][:, t, kc * 128:(kc + 1) * 128], identity=C.identb),
                         reads=["xse%d" % s, "cstb"], writes=["PC%d" % (kc // 4)], inc=(t == 1 and kc % 4 == 3))
            if e_ + 1 < NE:
                load_xs(e_ + 1)
            xsT2 = xsT[:].rearrange("p a b -> p (a b)")
            P.op("act", lambda e: e.activation(out=xsT2[:, 0:1024], in_=PCb[:, 0:1024], func=AF.Copy), reads=["PC0"], writes=["xsT0"])
            P.op("dve", lambda e: e.tensor_copy(out=xsT2[:, 1024:2048], in_=PCb[:, 1024:2048]), reads=["PC1"], writes=["xsT1"])
            for fb in range(4):
                fs = slice(fb * 128, (fb + 1) * 128)
                mmg(P, PA[:, fb * 256:(fb + 1) * 256], [(Wg[s][:, kc, fs], xsT[:, kc, :]) for kc in range(8)], reads=["Wg%d" % s, "xsT0", "xsT1"], writes=["PA%d" % (fb // 2)])
            for fb in range(4):
                fs = slice(fb * 128, (fb + 1) * 128)
                mmg(P, PB[:, fb * 256:(fb + 1) * 256], [(Wu[s][:, kc, fs], xsT[:, kc, :]) for kc in range(8)], reads=["Wu%d" % s, "xsT0", "xsT1"], writes=["PB%d" % (fb // 2)])
            if e_ + 1 < NE:
                cast_w(e_ + 1)
            if e_ + NS < NE:
                load_w(e_ + NS)
            P.op("act", lambda e: e.activation(out=hs[:], in_=PA[:], func=AF.Silu), reads=["PA0", "PA1"], writes=["hs"])
            P.op("dve", lambda e: e.tensor_tensor(out=hidT[:].rearrange("p a b -> p (a b)"), in0=hs[:], in1=PB[:], op=ALU.mult), reads=["hs", "PB0", "PB1"], writes=["hidT"])
            for t in range(2):
                PY, ny = (PD, "PD") if t == 0 else (PA, "PA")
                for nb in range(2):
                    mmg(P, PY[:, nb * 512:(nb + 1) * 512], [(hidT[:, fb, t * 128:(t + 1) * 128], Wd[s][:, fb, nb * 512:(nb + 1) * 512]) for fb in range(4)],
                        reads=["hidT", "Wd%d" % s], writes=[ny + "%d" % nb])
                if t == 0:
                    P.op("act", lambda e: e.activation(out=yo[s][:, 0, :], in_=PD[:], func=AF.Copy), reads=["PD0", "PD1"], writes=[("yo", s, 0)])
                else:
                    P.op("dve", lambda e: e.tensor_copy(out=yo[s][:, 1, :], in_=PA[:]), reads=["PA0", "PA1"], writes=[("yo", s, 1)])
            P.dma("pool", lambda q: q.dma_start(out=d["ys"][e_ * CAP:(e_ + 1) * CAP, :].rearrange("(t p) f -> p t f", p=128), in_=yo[s][:]), "cyo%d" % s,
                  reads=[("yo", s, 0), ("yo", s, 1)], writes=[("ys_dram", e_)])
        P.barrier()


def phaseD(C):
    P, nc, d = C.P, C.nc, C.d
    with ExitStack() as es:
        sb = lambda n, s, dt: es.enter_context(nc.sbuf_tensor("pd_" + n, s, dt))
        Wpg = sb("Wpg", [128, 8, 1024], BF16)
        Wpp = sb("Wpp", [128, 2, 1024], BF16)
        P.dma("pool", lambda q: q.dma_start(out=Wpg[:], in_=d["w_pg"].rearrange("(kc p) c -> p kc c", p=128)), "Wpg", writes=["Wpg"])
        P.dma("pool", lambda q: q.dma_start(out=Wpp[:], in_=d["w_pp"].rearrange("(kc p) c -> p kc c", p=128)), "Wpp", writes=["Wpp"])
        rp = sb("rp", [128, 2048], F32)
        P.dma("sp", lambda q: q.dma_start(out=rp[:], in_=d["rep_ln2"][:, :]), "rp", writes=["rp"])
        lng, lnb = rp[:, 0:1024], rp[:, 1024:2048]
        y1 = [sb("y1%d" % i, [128, 1024], F32) for i in range(2)]
        y2 = [sb("y2%d" % i, [128, 1024], F32) for i in range(2)]
        h1 = [sb("h1%d" % i, [128, 1024], F32) for i in range(2)]
        pf = [sb("pf%d" % i, [128, 256], F32) for i in range(2)]
        m = sb("m", [128, 1024], F32)
        junk = sb("junk", [128, 1024], F32)
        st = sb("st", [128, 16], F32)
        x2 = [sb("x2%d" % i, [128, 1024], F32) for i in range(2)]
        x2b = [sb("x2b%d" % i, [128, 1280], BF16) for i in range(2)]
        x2T = sb("x2T", [128, 10, 128], BF16)
        sgt = sb("sgt", [128, 1024], F32)
        ot = [sb("ot%d" % i, [128, 1024], F32) for i in range(2)]
        PA, PB, PC, PD = C.PA, C.PB, C.PC, C.PD
        PCb = PC[:].bitcast(BF16)
        NCH = C.TOK // 128

        def loads(ch):
            s = ch % 2
            tk = ch * 128
            P.dma("pool", lambda q: q.indirect_dma_start(out=y1[s][:], out_offset=None, in_=d["ys"][:, :], in_offset=bass.IndirectOffsetOnAxis(ap=C.slots[:, ch, 0:1], axis=0)),
                  "dy1%d" % s, writes=["y1%d" % s])
            P.dma("pool", lambda q: q.indirect_dma_start(out=y2[s][:], out_offset=None, in_=d["ys"][:, :], in_offset=bass.IndirectOffsetOnAxis(ap=C.slots[:, ch, 1:2], axis=0)),
                  "dy2%d" % s, writes=["y2%d" % s])
            P.dma("sp", lambda q: q.dma_start(out=h1[s][:], in_=d["h1"][tk:tk + 128, :]), "dh1%d" % s, writes=["h1%d" % s])
            P.dma("sp", lambda q: q.dma_start(out=pf[s][:], in_=d["p"][tk:tk + 128, :]), "dpf%d" % s, writes=["pf%d" % s])

        def combine(ch):
            s = ch % 2
            P.op("dve", lambda e: e.tensor_scalar(out=m[:], in0=y1[s][:], scalar1=C.gates[:, ch, 0:1], scalar2=None, op0=ALU.mult), reads=["y1%d" % s], writes=["m"])
            P.op("dve", lambda e: e.scalar_tensor_tensor(out=m[:], in0=y2[s][:], scalar=C.gates[:, ch, 1:2], in1=m[:], op0=ALU.mult, op1=ALU.add), reads=["y2%d" % s, "m"], writes=["m"])
            P.op("dve", lambda e: e.scalar_tensor_tensor(out=m[:], in0=h1[s][:], scalar=ALPHA, in1=m[:], op0=ALU.mult, op1=ALU.add), reads=["h1%d" % s, "m"], writes=["m"])

        def norm(ch):
            s = ch % 2
            layer_norm(C, P, m, "m", junk, st, lng, lnb, "rp", x2[s], "x2%d" % s)
            P.op("act", lambda e: e.activation(out=x2b[s][:, 0:1024], in_=x2[s][:], func=AF.Copy), reads=["x2%d" % s], writes=["x2b%d" % s])
            P.op("act", lambda e: e.activation(out=x2b[s][:, 1024:1280], in_=pf[s][:], func=AF.Copy), reads=["pf%d" % s, "x2b%d" % s], writes=["x2b%d" % s])

        def transp(ch):
            s = ch % 2
            for kc in range(10):
                P.op("pe", lambda e, kc=kc: e.transpose(out=PCb[:, kc * 128:(kc + 1) * 128], in_=x2b[s][:, kc * 128:(kc + 1) * 128], identity=C.identb),
                     reads=["x2b%d" % s, "cstb"], writes=["PC0", "PC1"], inc=(kc == 9))

        def xcopy(ch):
            P.op("dve", lambda e: e.tensor_copy(out=x2T[:].rearrange("p a b -> p (a b)"), in_=PCb[:, 0:1280]), reads=["PC0", "PC1"], writes=["x2T"])

        def mms(ch):
            for nb in range(2):
                mmg(P, PA[:, nb * 512:(nb + 1) * 512], [(x2T[:, kc, :], Wpg[:, kc, nb * 512:(nb + 1) * 512]) for kc in range(8)], reads=["x2T", "Wpg"], writes=["PA%d" % nb])
                mmg(P, PB[:, nb * 512:(nb + 1) * 512], [(x2T[:, 8 + kc, :], Wpp[:, kc, nb * 512:(nb + 1) * 512]) for kc in range(2)], reads=["x2T", "Wpp"], writes=["PB%d" % nb])

        def fin(ch):
            s = ch % 2
            tk = ch * 128
            P.op("act", lambda e: e.activation(out=sgt[:], in_=PA[:], func=AF.Sigmoid), reads=["PA0", "PA1"], writes=["sgt"])
            P.op("dve", lambda e: e.tensor_tensor(out=sgt[:], in0=sgt[:], in1=PB[:], op=ALU.mult), reads=["sgt", "PB0", "PB1"], writes=["sgt"])
            P.op("dve", lambda e: e.tensor_tensor(out=ot[s][:], in0=sgt[:], in1=x2[s][:], op=ALU.add), reads=["sgt", "x2%d" % s], writes=["ot%d" % s])
            P.dma("sp", lambda q: q.dma_start(out=d["out"][tk:tk + 128, :], in_=ot[s][:]), "dot%d" % s, reads=["ot%d" % s], writes=[("out_dram", ch)])

        loads(0)
        if NCH > 1:
            loads(1)
        combine(0)
        norm(0)
        for ch in range(NCH):
            transp(ch)
            if ch + 1 < NCH:
                combine(ch + 1)
            xcopy(ch)
            mms(ch)
            if ch + 1 < NCH:
                norm(ch + 1)
            if ch + 2 < NCH:
                loads(ch + 2)
            fin(ch)
        P.barrier()


def make_consts():
    c = np.zeros((128, 6, 128), np.float32)
    i = np.arange(128)
    c[:, 0, :] = np.eye(128)
    c[:, 1, :] = (i[:, None] <= i[None, :])
    c[:, 2, :] = 1.0
    c[:, 3, :] = np.where(i[None, :] >= i[:, None], 0.0, NEG)
    c[:, 4, :] = np.where(i[None, :] <= i[:, None], 0.0, NEG)
    c[:, 5, :] = (i[:, None] < i[None, :])
    return c.reshape(128, 768)


def rep(v, n=128):
    v = np.asarray(v, np.float32).reshape(1, -1)
    return np.ascontiguousarray(np.broadcast_to(v, (n, v.shape[1])))


def build_nc(S, NSEQ, stages="0slBCD"):
    TOK = S * NSEQ
    nc = bass.Bass("TRN2", target_bir_lowering=False)
    C = Ctx(); C.nc = nc; C.S = S; C.TOK = TOK
    dt = lambda n, s, d, k: nc.dram_tensor(n, s, d, kind=k).ap()
    EI, IN = "ExternalInput", "Internal"
    C.d = dict(
        x=dt("x", [TOK, 1024], F32, EI), p=dt("p", [TOK, 256], F32, EI),
        w_in=dt("w_in", [1024, 8224], F32, EI), consts=dt("consts", [128, 768], F32, EI),
        convw_fm=dt("convw_fm", [128, 64], F32, EI), convb_fm=dt("convb_fm", [128, 16], F32, EI),
        rep_ssd=dt("rep_ssd", [128, 1072], F32, EI), rep_lstm=dt("rep_lstm", [128, 1040], F32, EI),
        rep_mid=dt("rep_mid", [128, 2184], F32, EI), rep_ln2=dt("rep_ln2", [128, 2048], F32, EI),
        w_a=dt("w_a", [1024, 1024], F32, EI), w_b=dt("w_b", [1024, 1024], F32, EI), w_out=dt("w_out", [1024, 1024], F32, EI),
        w_router=dt("w_router", [1024, 72], F32, EI),
        moe_wg=dt("moe_wg", [NE, 1024, 512], F32, EI), moe_wu=dt("moe_wu", [NE, 1024, 512], F32, EI), moe_wd=dt("moe_wd", [NE, 512, 1024], F32, EI),
        w_pg=dt("w_pg", [1024, 1024], F32, EI), w_pp=dt("w_pp", [256, 1024], F32, EI),
        xT=dt("xT", [128, 8, TOK], BF16, IN), yTs=dt("yTs", [128, 8, TOK], BF16, IN), yTl=dt("yTl", [128, 8, TOK], BF16, IN),
        h1=dt("h1", [TOK, 1024], F32, IN), xs=dt("xs", [(NE + 1) * CAP, 1024], BF16, IN), ys=dt("ys", [NE * CAP, 1024], F32, IN),
        out=dt("out", [TOK, 1024], F32, "ExternalOutput"),
    )
    with ExitStack() as es:
        C.es = es
        C.P = Prog(nc, es)
        C.sb = lambda n, s, d: es.enter_context(nc.sbuf_tensor(n, s, d))
        C.PA = es.enter_context(nc.psum_tensor("PA", [128, 1024], F32))
        C.PB = es.enter_context(nc.psum_tensor("PB", [128, 1024], F32))
        C.PC = es.enter_context(nc.psum_tensor("PC", [128, 1024], F32))
        C.PD = es.enter_context(nc.psum_tensor("PD", [128, 1024], F32))
        setup_consts(C)
        C.slots = C.sb("slots", [128, TOK // 128, 2], I32)
        C.gates = C.sb("gates", [128, TOK // 128, 2], F32)
        C.zero_xs = zero_xs
        phase0(C)
        for b in range(NSEQ):
            ssd_pass(C, b)
            lstm_pass(C, b)
        phaseB(C)
        phaseC(C)
        phaseD(C)
        C.P.finish()
        C.stats = (C.P.nops, C.P.nwaits, C.P.nsem)
    return nc, C


def shared_inputs(I):
    g = lambda k: np.asarray(I[k], np.float32)[0]
    cw = g("ssm_conv_w")
    return dict(
        w_in=g("w_in"), consts=make_consts(),
        convw_fm=np.ascontiguousarray(cw.reshape(4, 16, 128).transpose(2, 1, 0)).reshape(128, 64),
        convb_fm=np.ascontiguousarray(g("ssm_conv_b").reshape(16, 128).T),
        rep_ssd=np.concatenate([rep(g("ssm_dt_bias")), rep(g("ssm_a_log")), rep(g("ssm_d")), rep(g("ssm_norm_w"))], 1),
        rep_lstm=np.concatenate([rep(g("lstm_i_bias")), rep(g("lstm_f_bias")), rep(g("lstm_norm_w"))], 1),
        rep_mid=np.concatenate([rep(g("ln1_g")), rep(g("ln1_b")), rep(g("moe_b_group")), rep(g("moe_b_expert")), rep(np.arange(NE, dtype=np.float32) * CAP)], 1),
        rep_ln2=np.concatenate([rep(g("ln2_g")), rep(g("ln2_b"))], 1),
        w_a=g("w_branch_ssm"), w_b=g("w_branch_lstm"), w_out=g("w_out"),
        w_router=np.ascontiguousarray(np.concatenate([g("moe_w_group"), g("moe_w_expert")], 1)),
        moe_wg=g("moe_w_gate"), moe_wu=g("moe_w_up"), moe_wd=g("moe_w_down"),
        w_pg=g("ple_w_gate"), w_pp=g("ple_w_proj"),
    )


_CACHE = {}


def kernel(**inputs):
    I = {k: np.asarray(v) for k, v in inputs.items()}
    if "nc" not in _CACHE:
        _CACHE["nc"] = build_nc(2048, 2)[0]
    nc = _CACHE["nc"]
    sh = shared_inputs(I)
    x = np.asarray(I["x"], np.float32)
    p = np.asarray(I["p"], np.float32)[0]
    in_maps = []
    for c in range(8):
        in_maps.append(dict(sh, x=np.ascontiguousarray(x[2 * c:2 * c + 2].reshape(4096, 1024)),
                            p=np.ascontiguousarray(p[2 * c:2 * c + 2].reshape(4096, 256))))
    res = run_bass_kernel_spmd(nc, in_maps, core_ids=list(range(8)))
    out = np.concatenate([np.asarray(r["out"]).reshape(2, 2048, 1024) for r in res.results], 0)
    return np.ascontiguousarray(out.astype(np.float32))
```

```python
import numpy as np
from contextlib import ExitStack
import concourse.bass as bass
import concourse.mybir as mybir
from concourse.bass_utils import run_bass_kernel_spmd


F32 = mybir.dt.float32
BF16 = mybir.dt.bfloat16
I32 = mybir.dt.int32
AF = mybir.ActivationFunctionType
ALU = mybir.AluOpType
AX = mybir.AxisListType

EPOCH = 30000


class _Rec:
    def __init__(self):
        self.call = None

    def __getattr__(self, name):
        def f(*a, **k):
            assert self.call is None
            self.call = (name, a, k)
            return None
        return f


def _record(fn):
    r = _Rec()
    fn(r)
    assert r.call is not None
    return r.call


class Prog:
    def __init__(self, nc, es):
        self.nc = nc
        self.es = es
        self.eng = {"pe": nc.tensor, "dve": nc.vector, "act": nc.scalar,
                    "pool": nc.gpsimd, "sp": nc.sync}
        self.cnt = {e: 0 for e in self.eng}
        self.esem = {}
        self.nsem = 0
        for e in self.eng:
            self.esem[e] = self._newsem("c_" + e)
        self.waited = {e: {} for e in self.eng}
        self.res = {}
        self.dsem = {}
        self.nops = 0
        self.q = {e: [] for e in self.eng}
        self.pend = {e: False for e in self.eng}
        self.nwaits = 0

    def _newsem(self, name):
        self.nsem += 1
        return self.es.enter_context(self.nc.semaphore(name + "_%d" % self.nsem))

    def _wait(self, e, toks):
        w = self.waited[e]
        need = {}
        for t in toks:
            if t is None:
                continue
            sem, val, src = t
            if src == e and e == "pe":
                continue
            k = id(sem)
            if w.get(k, 0) >= val:
                continue
            if k not in need or need[k][1] < val:
                need[k] = (sem, val)
        for k, (sem, val) in need.items():
            self.q[e].append(("w", sem, val))
            w[k] = val
            self.nwaits += 1

    def _deps(self, reads, writes):
        toks = []
        for r in reads:
            st = self.res.get(r)
            if st is not None:
                toks.append(st[0])
        for wkey in writes:
            st = self.res.get(wkey)
            if st is not None:
                toks.append(st[0])
                toks.extend(st[1].values())
        return toks

    def _commit(self, tok, reads, writes):
        for r in reads:
            st = self.res.setdefault(r, [None, {}])
            k = tok[1] if tok[0] == "dma" else id(tok[0])
            old = st[1].get(k)
            if old is None or tok[0] == "dma" or old[1] < tok[1]:
                st[1][k] = tok
        for wkey in writes:
            self.res[wkey] = [tok, {}]

    skip = False

    def op(self, e, fn, reads=(), writes=(), inc=True):
        if self.skip:
            return None
        toks = self._deps(reads, writes)
        self._wait(e, toks)
        self.nops += 1
        if inc:
            if self.cnt[e] >= EPOCH and not self.pend[e]:
                self.esem[e] = self._newsem("c_" + e)
                self.cnt[e] = 0
            self.cnt[e] += 1
            self.q[e].append(("o", _record(fn), self.esem[e], 1))
            tok = (self.esem[e], self.cnt[e], e)
            self.pend[e] = False
        else:
            self.q[e].append(("o", _record(fn), None, 0))
            self.pend[e] = True
            tok = (self.esem[e], self.cnt[e] + 1, e)
        self._commit(tok, reads, writes)
        return None

    def dma(self, q, fn, semkey, reads=(), writes=()):
        if self.skip:
            return None
        toks = self._deps(reads, writes)
        self._wait(q, toks)
        if semkey not in self.dsem:
            self.dsem[semkey] = [self._newsem("d"), 0]
        ds = self.dsem[semkey]
        ds[1] += 16
        self.q[q].append(("o", _record(fn), ds[0], 16))
        self.nops += 1
        tok = ("dma", semkey)
        self._commit(tok, reads, writes)
        return None

    def barrier(self):
        self.skip = False
        toks = []
        for e in self.eng:
            if self.pend[e]:
                self.op(e, lambda q: q.nop(), inc=True)
            if self.cnt[e] > 0:
                toks.append((self.esem[e], self.cnt[e], None))
        for k, (sem, cnt) in self.dsem.items():
            if cnt > 0:
                toks.append((sem, cnt, None))
        for e in self.eng:
            self._wait(e, toks)
        self.res = {}
        self.flush()

    def flush(self):
        if not any(self.q.values()):
            return
        qs = self.q
        self.q = {e: [] for e in self.eng}
        deco = {"pe": "tensor", "dve": "vector", "act": "scalar", "pool": "gpsimd", "sp": "sync"}

        def replay(lst):
            def f(eng):
                for it in lst:
                    if it[0] == "w":
                        eng.wait_ge(it[1], it[2])
                    else:
                        name, a, k = it[1]
                        ins = getattr(eng, name)(*a, **k)
                        if it[2] is not None:
                            ins.then_inc(it[2], it[3])
            return f
        with self.nc.Block() as block:
            for e, lst in qs.items():
                if lst:
                    getattr(block, deco[e])(replay(lst))

    def finish(self, e="sp"):
        for k, (sem, cnt) in self.dsem.items():
            if cnt > 0:
                self.q[e].append(("w", sem, cnt))
        self.flush()


_orig_wait = Prog._wait


def _wait2(self, e, toks):
    out = []
    for t in toks:
        if t is None:
            continue
        if t[0] == "dma":
            ds = self.dsem[t[1]]
            out.append((ds[0], ds[1], None))
        else:
            out.append(t)
    _orig_wait(self, e, out)


Prog._wait = _wait2


NEG = -1.0e30
EPS = 1e-5


class Ctx:
    stop = None


class Stop(Exception):
    pass


def chk(C, name):
    if C.stop == name or C.stop == "%s@%d" % (name, getattr(C, "cur", -1)):
        C.P.skip = True


def mmg(P, out, pairs, reads, writes):
    n = len(pairs)
    for i, (l, r) in enumerate(pairs):
        P.op("pe", lambda e, l=l, r=r, i=i: e.matmul(out, lhsT=l, rhs=r, start=(i == 0), stop=(i == n - 1)),
             reads=reads, writes=writes, inc=(i == n - 1))


def setup_consts(C):
    P, nc, es = C.P, C.nc, C.es
    sb = C.sb
    C.cst = sb("cst", [128, 6, 128], F32)
    P.dma("sp", lambda q: q.dma_start(out=C.cst[:].rearrange("p a b -> p (a b)"), in_=C.d["consts"][:, :]), "cst", writes=["cst"])
    C.ident = C.cst[:, 0, :]
    C.U = C.cst[:, 1, :]
    C.ones = C.cst[:, 2, :]
    C.nmT = C.cst[:, 3, :]
    C.nm = C.cst[:, 4, :]
    C.cstb = sb("cstb", [128, 6, 128], BF16)
    P.op("dve", lambda e: e.tensor_copy(out=C.cstb[:], in_=C.cst[:]), reads=["cst"], writes=["cstb"])
    C.identb = C.cstb[:, 0, :]
    C.Ustrb = C.cstb[:, 5, :]
    C.onesb = C.cstb[:, 2, :]


def phase0(C):
    P, nc = C.P, C.nc
    with ExitStack() as es:
        sb = lambda n, s, d: es.enter_context(nc.sbuf_tensor(n, s, d))
        xf = [sb("p0_xf%d" % i, [128, 1024], F32) for i in range(4)]
        xb = [sb("p0_xb%d" % i, [128, 1024], BF16) for i in range(4)]
        xt = [sb("p0_xt%d" % i, [128, 8, 128], BF16) for i in range(4)]
        x = C.d["x"]
        if getattr(C, "zero_xs", None):
            C.zero_xs(C, es)
        for c in range(C.TOK // 128):
            s = c % 4
            t0 = c * 128
            P.dma("sp", lambda q: q.dma_start(out=xf[s][:], in_=x[t0:t0 + 128, :]), "p0xf%d" % s, writes=["p0xf%d" % s])
            P.op("act", lambda e: e.activation(out=xb[s][:], in_=xf[s][:], func=AF.Copy), reads=["p0xf%d" % s], writes=["p0xb%d" % s])
            pb = (C.PC if s < 2 else C.PD)[:].bitcast(BF16)[:, (s % 2) * 1024:(s % 2 + 1) * 1024]
            bk = ("PC%d" if s < 2 else "PD%d") % (s % 2)
            for kc in range(8):
                P.op("pe", lambda e, kc=kc: e.transpose(out=pb[:, kc * 128:(kc + 1) * 128], in_=xb[s][:, kc * 128:(kc + 1) * 128], identity=C.identb),
                     reads=["p0xb%d" % s, "cstb"], writes=[bk], inc=(kc == 7))
            P.op("dve" if s % 2 == 0 else "act", lambda e: (e.tensor_copy(out=xt[s][:].rearrange("p a b -> p (a b)"), in_=pb) if s % 2 == 0 else e.activation(out=xt[s][:].rearrange("p a b -> p (a b)"), in_=pb, func=AF.Copy)),
                 reads=[bk], writes=["p0xt%d" % s])
            P.dma("sp", lambda q: q.dma_start(out=C.d["xT"][:, :, t0:t0 + 128], in_=xt[s][:]), "p0xt%d" % s, reads=["p0xt%d" % s], writes=["xT_dram"])
        P.barrier()


def ssd_pass(C, b):
    P, nc, d = C.P, C.nc, C.d
    S = C.S
    with ExitStack() as es:
        sb = lambda n, s, dt: es.enter_context(nc.sbuf_tensor("ssd%d_" % b + n, s, dt))
        Wz = sb("Wz", [128, 8, 1024], BF16)
        Wx = sb("Wx", [128, 8, 2048], BF16)
        Wdt = sb("Wdt", [128, 8, 16], BF16)
        win = d["w_in"].rearrange("(kc p) c -> p kc c", p=128)
        P.dma("pool", lambda q: q.dma_start(out=Wx[:, :, 0:1024], in_=win[:, :, 1024:2048]), "Wx", writes=["Wx"])
        P.dma("pool", lambda q: q.dma_start(out=Wx[:, :, 1024:2048], in_=win[:, :, 2048:3072]), "Wx", writes=["Wx"])
        P.dma("pool", lambda q: q.dma_start(out=Wz[:], in_=win[:, :, 0:1024]), "Wz", writes=["Wz"])
        P.dma("pool", lambda q: q.dma_start(out=Wdt[:], in_=win[:, :, 3072:3088]), "Wdt", writes=["Wdt"])
        cw = sb("cw", [128, 16, 4], F32)
        cb = sb("cb", [128, 16], F32)
        P.dma("sp", lambda q: q.dma_start(out=cw[:].rearrange("p a b -> p (a b)"), in_=d["convw_fm"][:, :]), "cw", writes=["cw"])
        P.dma("sp", lambda q: q.dma_start(out=cb[:], in_=d["convb_fm"][:, :]), "cb", writes=["cb"])
        rp = sb("rp", [128, 48 + 1024], F32)
        P.dma("sp", lambda q: q.dma_start(out=rp[:], in_=d["rep_ssd"][:, :]), "rp", writes=["rp"])
        dtb, alog, Drep, normw = rp[:, 0:16], rp[:, 16:32], rp[:, 32:48], rp[:, 48:1072]
        arep = sb("arep", [128, 16], F32)
        P.op("act", lambda e: e.activation(out=arep[:], in_=alog, func=AF.Exp), reads=["rp"], writes=["arep"])
        P.op("dve", lambda e: e.tensor_scalar(out=arep[:], in0=arep[:], scalar1=-1.0, scalar2=None, op0=ALU.mult), reads=["arep"], writes=["arep"])
        diagw = sb("diagw", [128, 16, 4, 128], BF16)
        for blk in range(16):
            for k in range(4):
                P.op("dve" if (blk + k) % 2 else "pool", lambda e, blk=blk, k=k: e.tensor_scalar(out=diagw[:, blk, k, :], in0=C.ident, scalar1=cw[:, blk, k:k + 1], scalar2=None, op0=ALU.mult),
                     reads=["cst", "cw"], writes=[("diagw", blk, k)])
        chk(C, "setup")
        ubuf = sb("ubuf", [128, 16, 516], BF16)
        xbcT = sb("xbcT", [128, 16, 512], BF16)
        xTs = [sb("xTs%d" % i, [128, 8, 512], BF16) for i in range(2)]
        x_tok = sb("x_tok", [128, 4, 1024], BF16)
        B_tok = sb("B_tok", [128, 4, 512], BF16)
        szb = sb("szb", [128, 4, 1024], F32)
        rhs_cs2 = [sb("rhs_cs%d" % i, [128, 8, 128], F32) for i in range(2)]
        dtmp2 = [sb("dtmp%d" % i, [128, 8, 128], F32) for i in range(2)]
        MT2 = [sb("MT%d" % i, [128, 8, 128], BF16) for i in range(2)]
        xdt = sb("xdt", [128, 1024], BF16)
        xdtd = sb("xdtd", [128, 1024], BF16)
        yA = sb("yA", [128, 1024], F32)
        yB = sb("yB", [128, 1024], F32)
        ybf = sb("ybf", [128, 1024], BF16)
        St = sb("St", [128, 1024], F32)
        Sbf = sb("Sbf", [128, 1024], BF16)
        sm = sb("sm", [128, 160], F32)
        ss = sb("ss", [128, 8], F32)
        yTst = [sb("yTst%d" % i, [128, 8, 128], BF16) for i in range(2)]

        PA, PB, PC, PD = C.PA, C.PB, C.PC, C.PD
        PCb = PC[:].bitcast(BF16)
        P.op("pool", lambda e: e.memset(St[:], 0.0), writes=["St"])
        P.op("pool", lambda e: e.memset(Sbf[:], 0.0), writes=["Sbf"])
        P.op("pool", lambda e: e.memset(ubuf[:, :, 0:4], 0.0), writes=[("ubuf", i) for i in range(16)])

        nsc = S // 512
        for sc in range(nsc):
            tok0 = b * S + sc * 512
            xs = xTs[sc % 2]
            xk = "xTs%d" % (sc % 2)
            P.dma("sp", lambda q: q.dma_start(out=xs[:], in_=d["xT"][:, :, tok0:tok0 + 512]), xk, reads=["xT_dram"], writes=[xk])
            if sc > 0:
                P.op("pool", lambda e: e.tensor_copy(out=ubuf[:, :, 1:4], in_=ubuf[:, :, 513:516]),
                     reads=[("ubuf", i) for i in range(16)], writes=[("ubuf", i) for i in range(16)])
            for blk in range(16):
                bank = PD[:, (blk % 2) * 512:(blk % 2 + 1) * 512]
                bk = "PD%d" % (blk % 2)
                mmg(P, bank, [(Wx[:, kc, blk * 128:(blk + 1) * 128], xs[:, kc, :]) for kc in range(8)], reads=["Wx", xk], writes=[bk])
                P.op("act" if blk % 2 else "dve", lambda e, blk=blk, bank=bank: (e.activation(out=ubuf[:, blk, 4:516], in_=bank, func=AF.Copy) if blk % 2 else e.tensor_copy(out=ubuf[:, blk, 4:516], in_=bank)),
                     reads=[bk], writes=[("ubuf", blk)])
            chk(C, "uT")
            for blk in range(16):
                bank = PD[:, (blk % 2) * 512:(blk % 2 + 1) * 512]
                bk = "PD%d" % (blk % 2)
                mmg(P, bank, [(diagw[:, blk, k, :], ubuf[:, blk, 1 + k:1 + k + 512]) for k in range(4)],
                    reads=[("ubuf", blk)] + [("diagw", blk, k) for k in range(4)], writes=[bk])
                P.op("act", lambda e, blk=blk, bank=bank: e.activation(out=xbcT[:, blk, :], in_=bank, func=AF.Silu, bias=cb[:, blk:blk + 1]),
                     reads=[bk, "cb"], writes=[("xbcT", blk)])
            chk(C, "conv")
            for c4 in range(4):
                cs = slice(c4 * 128, (c4 + 1) * 128)
                for blk in range(8):
                    P.op("pe", lambda e, blk=blk: e.transpose(out=PCb[:, blk * 128:(blk + 1) * 128], in_=xbcT[:, blk, cs], identity=C.identb),
                         reads=[("xbcT", blk), "cstb"], writes=["PC0"], inc=(blk == 7))
                P.op("dve", lambda e: e.tensor_copy(out=x_tok[:, c4, :], in_=PCb[:, 0:1024]), reads=["PC0"], writes=[("x_tok", c4)])
                for g in range(4):
                    P.op("pe", lambda e, g=g: e.transpose(out=PCb[:, 1024 + g * 128:1024 + (g + 1) * 128], in_=xbcT[:, 8 + g, cs], identity=C.identb),
                         reads=[("xbcT", 8 + g), "cstb"], writes=["PC1"], inc=(g == 3))
                P.op("act", lambda e: e.activation(out=B_tok[:, c4, :], in_=PCb[:, 1024:1536], func=AF.Copy), reads=["PC1"], writes=[("B_tok", c4)])

            for c4 in range(4):
                cs = slice(c4 * 128, (c4 + 1) * 128)
                for nb in range(2):
                    mmg(P, PA[:, nb * 512:(nb + 1) * 512], [(xs[:, kc, cs], Wz[:, kc, nb * 512:(nb + 1) * 512]) for kc in range(8)], reads=[xk, "Wz"], writes=["PA%d" % nb])
                P.op("act", lambda e: e.activation(out=szb[:, c4, :], in_=PA[:], func=AF.Silu), reads=["PA0", "PA1"], writes=[("sz", c4)])
            chk(C, "tok")
            for c4r in range(4):
                c4 = c4r
                cs = slice(c4 * 128, (c4 + 1) * 128)
                tk = tok0 + c4 * 128
                C.cur = sc * 4 + c4r
                sz = szb[:, c4, :]
                chk(C, "z")
                mmg(P, PC[:, 512:528], [(xs[:, kc, cs], Wdt[:, kc, :]) for kc in range(8)], reads=[xk, "Wdt"], writes=["PC1"])
                P.op("dve", lambda e: e.tensor_tensor(out=sm[:, 0:16], in0=PC[:, 512:528], in1=dtb, op=ALU.add), reads=["PC1", "rp"], writes=["sm0"])
                P.op("act", lambda e: e.activation(out=sm[:, 16:32], in_=sm[:, 0:16], func=AF.Exp), reads=["sm0"], writes=["sm1"])
                P.op("act", lambda e: e.activation(out=sm[:, 32:48], in_=sm[:, 16:32], func=AF.Ln, bias=1.0), reads=["sm1"], writes=["dt"])
                dt = sm[:, 32:48]
                P.op("dve", lambda e: e.tensor_tensor(out=sm[:, 48:64], in0=dt, in1=arep[:], op=ALU.mult), reads=["dt", "arep"], writes=["dta"])
                dta = sm[:, 48:64]
                P.op("pe", lambda e: e.matmul(PC[:, 528:544], lhsT=C.U, rhs=dta, start=True, stop=True), reads=["cst", "dta"], writes=["PC1"])
                P.op("act", lambda e: e.activation(out=sm[:, 64:80], in_=PC[:, 528:544], func=AF.Exp), reads=["PC1"], writes=["eacs"])
                P.op("dve", lambda e: e.tensor_scalar(out=sm[:, 80:96], in0=PC[:, 528:544], scalar1=-1.0, scalar2=None, op0=ALU.mult), reads=["PC1"], writes=["nacs"])
                eacs, nacs = sm[:, 64:80], sm[:, 80:96]
                chk(C, "dt")
                P.op("dve", lambda e: e.tensor_tensor(out=xdt[:].rearrange("p (h q) -> p h q", h=16), in0=x_tok[:, c4, :].rearrange("p (h q) -> p h q", h=16),
                                                       in1=dt.unsqueeze(2).broadcast_to([128, 16, 64]), op=ALU.mult), reads=[("x_tok", c4), "dt"], writes=["xdt"])
                stages = [[], []]
                for hh in range(2):
                    hs = slice(hh * 8, hh * 8 + 8)
                    rhs_cs, dtmp, MT = rhs_cs2[hh], dtmp2[hh], MT2[hh]
                    PH, nph = (PB, "PB") if hh == 0 else (PA, "PA")
                    k_rc, k_dt, k_mt = "rhs_cs%d" % hh, "dtmp%d" % hh, "MT%d" % hh
                    ST = stages[hh].append
                    def _stage(hh=hh, hs=hs, rhs_cs=rhs_cs, dtmp=dtmp, MT=MT, PH=PH, nph=nph, k_rc=k_rc, k_dt=k_dt, k_mt=k_mt, **kw):
                        PB3 = PH[:].rearrange("p (a b) -> p a b", a=8); kph = [nph + "0", nph + "1"]
                        cbt = PC[:, hh * 256:hh * 256 + 256].rearrange("p (g l) -> p g l", g=2).unsqueeze(2).broadcast_to([128, 2, 4, 128])
                        P.op("dve", lambda e: e.tensor_tensor(out=rhs_cs[:], in0=C.U.unsqueeze(1).broadcast_to([128, 8, 128]),
                                                               in1=dta[:, hs].unsqueeze(2).broadcast_to([128, 8, 128]), op=ALU.mult), reads=["cst", "dta"], writes=[k_rc])
                    ST(_stage)
                    def _stage(hh=hh, hs=hs, rhs_cs=rhs_cs, dtmp=dtmp, MT=MT, PH=PH, nph=nph, k_rc=k_rc, k_dt=k_dt, k_mt=k_mt, **kw):
                        PB3 = PH[:].rearrange("p (a b) -> p a b", a=8); kph = [nph + "0", nph + "1"]
                        cbt = PC[:, hh * 256:hh * 256 + 256].rearrange("p (g l) -> p g l", g=2).unsqueeze(2).broadcast_to([128, 2, 4, 128])
                        for q4 in range(2):
                            P.op("pe", lambda e, q4=q4: e.matmul(PH[:, q4 * 512:(q4 + 1) * 512], lhsT=C.ones, rhs=rhs_cs[:, q4 * 4:(q4 + 1) * 4, :].rearrange("p a b -> p (a b)"), start=True, stop=True),
                                 reads=["cst", k_rc], writes=[nph + "%d" % q4])
                    ST(_stage)
                    PB3 = PH[:].rearrange("p (a b) -> p a b", a=8)
                    kph = [nph + "0", nph + "1"]
                    def _stage(hh=hh, hs=hs, rhs_cs=rhs_cs, dtmp=dtmp, MT=MT, PH=PH, nph=nph, k_rc=k_rc, k_dt=k_dt, k_mt=k_mt, **kw):
                        PB3 = PH[:].rearrange("p (a b) -> p a b", a=8); kph = [nph + "0", nph + "1"]
                        cbt = PC[:, hh * 256:hh * 256 + 256].rearrange("p (g l) -> p g l", g=2).unsqueeze(2).broadcast_to([128, 2, 4, 128])
                        P.op("dve", lambda e: e.tensor_tensor(out=dtmp[:], in0=PB3, in1=C.nmT.unsqueeze(1).broadcast_to([128, 8, 128]), op=ALU.add), reads=kph + ["cst"], writes=[k_dt])
                    ST(_stage)
                    def _stage(hh=hh, hs=hs, rhs_cs=rhs_cs, dtmp=dtmp, MT=MT, PH=PH, nph=nph, k_rc=k_rc, k_dt=k_dt, k_mt=k_mt, **kw):
                        PB3 = PH[:].rearrange("p (a b) -> p a b", a=8); kph = [nph + "0", nph + "1"]
                        cbt = PC[:, hh * 256:hh * 256 + 256].rearrange("p (g l) -> p g l", g=2).unsqueeze(2).broadcast_to([128, 2, 4, 128])
                    ST(_stage)
                    def _stage(hh=hh, hs=hs, rhs_cs=rhs_cs, dtmp=dtmp, MT=MT, PH=PH, nph=nph, k_rc=k_rc, k_dt=k_dt, k_mt=k_mt, **kw):
                        PB3 = PH[:].rearrange("p (a b) -> p a b", a=8); kph = [nph + "0", nph + "1"]
                        cbt = PC[:, hh * 256:hh * 256 + 256].rearrange("p (g l) -> p g l", g=2).unsqueeze(2).broadcast_to([128, 2, 4, 128])
                        P.op("dve", lambda e: e.tensor_tensor(out=sm[:, 96 + hh * 8:96 + hh * 8 + 8], in0=PB3[:, :, 127], in1=nacs[:, hs], op=ALU.add), reads=kph + ["nacs"], writes=[("dtea", hh)])
                    ST(_stage)
                    def _stage(hh=hh, hs=hs, rhs_cs=rhs_cs, dtmp=dtmp, MT=MT, PH=PH, nph=nph, k_rc=k_rc, k_dt=k_dt, k_mt=k_mt, **kw):
                        PB3 = PH[:].rearrange("p (a b) -> p a b", a=8); kph = [nph + "0", nph + "1"]
                        cbt = PC[:, hh * 256:hh * 256 + 256].rearrange("p (g l) -> p g l", g=2).unsqueeze(2).broadcast_to([128, 2, 4, 128])
                        P.op("act", lambda e: e.activation(out=sm[:, 112 + hh * 8:112 + hh * 8 + 8], in_=PB3[:, :, 127], func=AF.Exp), reads=kph, writes=[("cd", hh)])
                    ST(_stage)
                    def _stage(hh=hh, hs=hs, rhs_cs=rhs_cs, dtmp=dtmp, MT=MT, PH=PH, nph=nph, k_rc=k_rc, k_dt=k_dt, k_mt=k_mt, **kw):
                        PB3 = PH[:].rearrange("p (a b) -> p a b", a=8); kph = [nph + "0", nph + "1"]
                        cbt = PC[:, hh * 256:hh * 256 + 256].rearrange("p (g l) -> p g l", g=2).unsqueeze(2).broadcast_to([128, 2, 4, 128])
                        P.op("dve", lambda e: e.tensor_tensor(out=dtmp[:], in0=dtmp[:], in1=nacs[:, hs].unsqueeze(2).broadcast_to([128, 8, 128]), op=ALU.add), reads=[k_dt, "nacs"], writes=[k_dt])
                    ST(_stage)
                    def _stage(hh=hh, hs=hs, rhs_cs=rhs_cs, dtmp=dtmp, MT=MT, PH=PH, nph=nph, k_rc=k_rc, k_dt=k_dt, k_mt=k_mt, **kw):
                        PB3 = PH[:].rearrange("p (a b) -> p a b", a=8); kph = [nph + "0", nph + "1"]
                        cbt = PC[:, hh * 256:hh * 256 + 256].rearrange("p (g l) -> p g l", g=2).unsqueeze(2).broadcast_to([128, 2, 4, 128])
                        P.op("act", lambda e: e.activation(out=dtmp[:], in_=dtmp[:], func=AF.Exp), reads=[k_dt], writes=[k_dt])
                    ST(_stage)
                    def _stage(hh=hh, hs=hs, rhs_cs=rhs_cs, dtmp=dtmp, MT=MT, PH=PH, nph=nph, k_rc=k_rc, k_dt=k_dt, k_mt=k_mt, **kw):
                        PB3 = PH[:].rearrange("p (a b) -> p a b", a=8); kph = [nph + "0", nph + "1"]
                        cbt = PC[:, hh * 256:hh * 256 + 256].rearrange("p (g l) -> p g l", g=2).unsqueeze(2).broadcast_to([128, 2, 4, 128])
                    ST(_stage)
                    def _stage(hh=hh, hs=hs, rhs_cs=rhs_cs, dtmp=dtmp, MT=MT, PH=PH, nph=nph, k_rc=k_rc, k_dt=k_dt, k_mt=k_mt, **kw):
                        PB3 = PH[:].rearrange("p (a b) -> p a b", a=8); kph = [nph + "0", nph + "1"]
                        cbt = PC[:, hh * 256:hh * 256 + 256].rearrange("p (g l) -> p g l", g=2).unsqueeze(2).broadcast_to([128, 2, 4, 128])
                        for gi in range(2):
                            g = hh * 2 + gi
                            P.op("pe", lambda e, g=g, gi=gi: e.matmul(PC[:, hh * 256 + gi * 128:hh * 256 + (gi + 1) * 128], lhsT=xbcT[:, 8 + g, cs], rhs=xbcT[:, 12 + g, cs], start=True, stop=True),
                                 reads=[("xbcT", 8 + g), ("xbcT", 12 + g)], writes=["PC0"])
                    ST(_stage)
                    cbt = PC[:, hh * 256:hh * 256 + 256].rearrange("p (g l) -> p g l", g=2).unsqueeze(2).broadcast_to([128, 2, 4, 128])
                    def _stage(hh=hh, hs=hs, rhs_cs=rhs_cs, dtmp=dtmp, MT=MT, PH=PH, nph=nph, k_rc=k_rc, k_dt=k_dt, k_mt=k_mt, **kw):
                        PB3 = PH[:].rearrange("p (a b) -> p a b", a=8); kph = [nph + "0", nph + "1"]
                        cbt = PC[:, hh * 256:hh * 256 + 256].rearrange("p (g l) -> p g l", g=2).unsqueeze(2).broadcast_to([128, 2, 4, 128])
                        P.op("dve", lambda e: e.tensor_tensor(out=MT[:].rearrange("p (g j) l -> p g j l", g=2), in0=cbt, in1=dtmp[:].rearrange("p (g j) l -> p g j l", g=2), op=ALU.mult),
                             reads=["PC0", k_dt], writes=[k_mt])
                    ST(_stage)
                    def _stage(hh=hh, hs=hs, rhs_cs=rhs_cs, dtmp=dtmp, MT=MT, PH=PH, nph=nph, k_rc=k_rc, k_dt=k_dt, k_mt=k_mt, **kw):
                        PB3 = PH[:].rearrange("p (a b) -> p a b", a=8); kph = [nph + "0", nph + "1"]
                        cbt = PC[:, hh * 256:hh * 256 + 256].rearrange("p (g l) -> p g l", g=2).unsqueeze(2).broadcast_to([128, 2, 4, 128])
                        for j in range(8):
                            h = hh * 8 + j
                            P.op("pe", lambda e, j=j, h=h: e.matmul(PD[:, h * 64:(h + 1) * 64], lhsT=MT[:, j, :], rhs=xdt[:, h * 64:(h + 1) * 64], start=True, stop=True),
                                 reads=[k_mt, "xdt"], writes=["PD%d" % hh], inc=(j == 7))
                    ST(_stage)
                for i in range(len(stages[0])):
                    stages[0][i]()
                    stages[1][i]()
                chk(C, "halves")
                P.op("act", lambda e: e.activation(out=sm[:, 96:112], in_=sm[:, 96:112], func=AF.Exp), reads=[("dtea", 0), ("dtea", 1)], writes=[("dtea", 0), ("dtea", 1)])
                P.op("pool", lambda e: e.tensor_tensor(out=sm[:, 128:144], in0=sm[:, 96:112], in1=dt, op=ALU.mult), reads=[("dtea", 0), ("dtea", 1), "dt"], writes=["w2"])
                P.op("dve", lambda e: e.tensor_tensor(out=xdtd[:].rearrange("p (h q) -> p h q", h=16), in0=x_tok[:, c4, :].rearrange("p (h q) -> p h q", h=16),
                                                       in1=sm[:, 128:144].unsqueeze(2).broadcast_to([128, 16, 64]), op=ALU.mult), reads=[("x_tok", c4), "w2"], writes=["xdtd"])
                for g in range(4):
                    P.op("pe", lambda e, g=g: e.matmul(PA[:, g * 256:(g + 1) * 256], lhsT=xbcT[:, 12 + g, cs], rhs=Sbf[:, g * 256:(g + 1) * 256], start=True, stop=True),
                         reads=[("xbcT", 12 + g), "Sbf"], writes=["PA%d" % (g // 2)])
                for g in range(4):
                    P.op("pe", lambda e, g=g: e.matmul(PB[:, g * 256:(g + 1) * 256], lhsT=B_tok[:, c4, g * 128:(g + 1) * 128], rhs=xdtd[:, g * 256:(g + 1) * 256], start=True, stop=True),
                         reads=[("B_tok", c4), "xdtd"], writes=["PB%d" % (g // 2)])
                chk(C, "state_mm")
                h3 = lambda ap: ap.rearrange("p (h q) -> p h q", h=16)
                P.op("dve", lambda e: e.tensor_tensor(out=h3(yA[:]), in0=h3(PA[:]), in1=eacs.unsqueeze(2).broadcast_to([128, 16, 64]), op=ALU.mult), reads=["PA0", "PA1", "eacs"], writes=["yA"])
                P.op("dve", lambda e: e.tensor_tensor(out=yA[:], in0=yA[:], in1=PD[:], op=ALU.add), reads=["yA", "PD0", "PD1"], writes=["yA"])
                P.op("pool", lambda e: e.tensor_tensor(out=h3(yB[:]), in0=h3(x_tok[:, c4, :]), in1=Drep.unsqueeze(2).broadcast_to([128, 16, 64]), op=ALU.mult), reads=[("x_tok", c4), "rp"], writes=["yB"])
                P.op("dve", lambda e: e.tensor_tensor(out=yA[:], in0=yA[:], in1=yB[:], op=ALU.add), reads=["yA", "yB"], writes=["yA"])
                P.op("dve", lambda e: e.tensor_tensor(out=yA[:], in0=yA[:], in1=sz, op=ALU.mult), reads=["yA", ("sz", c4)], writes=["yA"])
                P.op("pool", lambda e: e.memset(ss[:], 0.0), writes=["ss"])
                for g in range(4):
                    P.op("act", lambda e, g=g: e.activation(out=yB[:, g * 256:(g + 1) * 256], in_=yA[:, g * 256:(g + 1) * 256], func=AF.Square, accum_out=ss[:, g:g + 1]), reads=["yA", "ss"], writes=["yB", "ss"])
                P.op("dve", lambda e: e.tensor_scalar(out=ss[:, 4:8], in0=ss[:, 0:4], scalar1=1.0 / 256.0, scalar2=EPS, op0=ALU.mult, op1=ALU.add), reads=["ss"], writes=["ss"])
                P.op("act", lambda e: e.activation(out=ss[:, 4:8], in_=ss[:, 4:8], func=AF.Ln), reads=["ss"], writes=["ss"])
                P.op("act", lambda e: e.activation(out=ss[:, 4:8], in_=ss[:, 4:8], func=AF.Exp, scale=-0.5), reads=["ss"], writes=["ss"])
                g3 = lambda ap: ap.rearrange("p (g q) -> p g q", g=4)
                P.op("dve", lambda e: e.tensor_tensor(out=g3(yA[:]), in0=g3(yA[:]), in1=ss[:, 4:8].unsqueeze(2).broadcast_to([128, 4, 256]), op=ALU.mult), reads=["yA", "ss"], writes=["yA"])
                P.op("dve", lambda e: e.tensor_tensor(out=ybf[:], in0=yA[:], in1=normw, op=ALU.mult), reads=["yA", "rp"], writes=["ybf"])
                for blk in range(8):
                    P.op("pe", lambda e, blk=blk: e.transpose(out=PCb[:, blk * 128:(blk + 1) * 128], in_=ybf[:, blk * 128:(blk + 1) * 128], identity=C.identb),
                         reads=["ybf", "cstb"], writes=["PC0"], inc=(blk == 7))
                ci = (sc * 4 + c4) % 2
                P.op("act", lambda e: e.activation(out=yTst[ci][:].rearrange("p a b -> p (a b)"), in_=PCb[:, 0:1024], func=AF.Copy), reads=["PC0"], writes=["yTst%d" % ci])
                P.dma("sp", lambda q: q.dma_start(out=d["yTs"][:, :, tk:tk + 128], in_=yTst[ci][:]), "yTst%d" % ci, reads=["yTst%d" % ci], writes=["yTs_dram"])
                chk(C, "epi")
                cd = sm[:, 112:128]
                P.op("dve", lambda e: e.tensor_tensor(out=h3(St[:]), in0=h3(St[:]), in1=cd.unsqueeze(2).broadcast_to([128, 16, 64]), op=ALU.mult), reads=["St", ("cd", 0), ("cd", 1)], writes=["St"])
                P.op("dve", lambda e: e.tensor_tensor(out=St[:], in0=St[:], in1=PB[:], op=ALU.add), reads=["St", "PB0", "PB1"], writes=["St"])
                P.op("act", lambda e: e.activation(out=Sbf[:], in_=St[:], func=AF.Copy), reads=["St"], writes=["Sbf"])
                chk(C, "chunk%d" % (sc * 4 + c4r))
        P.barrier()


def lstm_pass(C, b):
    P, nc, d = C.P, C.nc, C.d
    S = C.S
    with ExitStack() as es:
        sb = lambda n, s, dt: es.enter_context(nc.sbuf_tensor("ls%d_" % b + n, s, dt))
        Wq = sb("Wq", [128, 8, 512], BF16)
        Wk = sb("Wk", [128, 8, 512], BF16)
        Wv = sb("Wv", [128, 8, 1024], BF16)
        Wo = sb("Wo", [128, 8, 1024], BF16)
        Wif = sb("Wif", [128, 8, 16], BF16)
        win = d["w_in"].rearrange("(kc p) c -> p kc c", p=128)
        P.dma("pool", lambda q: q.dma_start(out=Wq[:], in_=win[:, :, 3088:3600]), "Wq", writes=["Wq"])
        P.dma("pool", lambda q: q.dma_start(out=Wk[:], in_=win[:, :, 3600:4112]), "Wk", writes=["Wk"])
        P.dma("pool", lambda q: q.dma_start(out=Wv[:], in_=win[:, :, 4112:5136]), "Wv", writes=["Wv"])
        P.dma("pool", lambda q: q.dma_start(out=Wo[:], in_=win[:, :, 5136:6160]), "Wo", writes=["Wo"])
        P.dma("pool", lambda q: q.dma_start(out=Wif[:], in_=win[:, :, 6160:6176]), "Wif", writes=["Wif"])
        rp = sb("rp", [128, 16 + 1024], F32)
        P.dma("sp", lambda q: q.dma_start(out=rp[:], in_=d["rep_lstm"][:, :]), "rp", writes=["rp"])
        ifb, normw = rp[:, 0:16], rp[:, 16:1040]

        xTs = [sb("xTs%d" % i, [128, 8, 512], BF16) for i in range(2)]
        qT = sb("qT", [128, 2, 4, 512], BF16)
        kT = sb("kT", [128, 4, 512], BF16)
        k_tok4 = sb("k_tok4", [128, 4, 512], BF16)
        kw = sb("kw", [128, 512], BF16)
        v_tok4 = sb("v_tok4", [128, 4, 1024], BF16)
        so4 = sb("so4", [128, 4, 1024], F32)
        rhsb = sb("rhsb", [128, 8, 128], F32)
        D1 = sb("D1", [128, 8, 128], F32)
        WT = sb("WT", [128, 8, 128], F32)
        scT = sb("scT", [128, 8, 128], BF16)
        hA = sb("hA", [128, 1024], F32)
        hB = sb("hB", [128, 1024], F32)
        hbf = sb("hbf", [128, 1024], BF16)
        Cst = sb("Cst", [128, 4, 128], F32)
        Cbf = sb("Cbf", [128, 4, 128], BF16)
        Ctmp = sb("Ctmp", [128, 4, 128], F32)
        nst = sb("nst", [128, 4], F32)
        nbf = sb("nbf", [128, 4], BF16)
        ntmp = sb("ntmp", [128, 4], F32)
        mprev = sb("mprev", [128, 8], F32)
        sm = sb("sm", [128, 256], F32)
        ss = sb("ss", [128, 16], F32)
        yTst = [sb("yTst%d" % i, [128, 8, 128], BF16) for i in range(2)]

        PA, PB, PC, PD = C.PA, C.PB, C.PC, C.PD
        PCb = PC[:].bitcast(BF16)
        h3 = lambda ap: ap.rearrange("p (h q) -> p h q", h=8)
        P.op("dve", lambda e: e.memset(Cst[:], 0.0), writes=["Cst"])
        P.op("dve", lambda e: e.memset(Cbf[:], 0.0), writes=["Cbf"])
        P.op("dve", lambda e: e.memset(nst[:], 0.0), writes=["nst"])
        P.op("dve", lambda e: e.memset(nbf[:], 0.0), writes=["nbf"])
        P.op("dve", lambda e: e.memset(mprev[:], NEG), writes=["mprev"])
        P.op("dve", lambda e: e.memset(qT[:].rearrange("p a b c -> p (a b c)"), 0.0), writes=[("qT", i) for i in range(4)])

        def S_(name, lo, n):
            return sm[:, lo:lo + n], ("sm", name)
        nsc = S // 512
        for sc in range(nsc):
            tok0 = b * S + sc * 512
            xs = xTs[sc % 2]
            xk = "xTs%d" % (sc % 2)
            P.dma("sp", lambda q: q.dma_start(out=xs[:], in_=d["xT"][:, :, tok0:tok0 + 512]), "l" + xk, reads=["xT_dram"], writes=[xk])
            for blk in range(4):
                bank = PD[:, (blk % 2) * 512:(blk % 2 + 1) * 512]
                bk = "PD%d" % (blk % 2)
                mmg(P, bank, [(Wq[:, kc, blk * 128:(blk + 1) * 128], xs[:, kc, :]) for kc in range(8)], reads=["Wq", xk], writes=[bk])
                P.op("act", lambda e: e.activation(out=qT[0:64, 0, blk, :], in_=bank[0:64], func=AF.Copy, scale=0.125), reads=[bk], writes=[("qT", blk)])
                P.op("act", lambda e: e.activation(out=qT[64:128, 1, blk, :], in_=bank[64:128], func=AF.Copy, scale=0.125), reads=[bk, ("qT", blk)], writes=[("qT", blk)])
            for blk in range(4):
                bank = PD[:, (blk % 2) * 512:(blk % 2 + 1) * 512]
                bk = "PD%d" % (blk % 2)
                mmg(P, bank, [(Wk[:, kc, blk * 128:(blk + 1) * 128], xs[:, kc, :]) for kc in range(8)], reads=["Wk", xk], writes=[bk])
                P.op("dve", lambda e: e.tensor_copy(out=kT[:, blk, :], in_=bank), reads=[bk], writes=[("kT", blk)])
            for c4 in range(4):
                cs = slice(c4 * 128, (c4 + 1) * 128)
                for nb in range(2):
                    mmg(P, PA[:, nb * 512:(nb + 1) * 512], [(xs[:, kc, cs], Wv[:, kc, nb * 512:(nb + 1) * 512]) for kc in range(8)], reads=[xk, "Wv"], writes=["PA%d" % nb])
                P.op("dve", lambda e: e.tensor_copy(out=v_tok4[:, c4, :], in_=PA[:]), reads=["PA0", "PA1"], writes=[("v_tok", c4)])
                for nb in range(2):
                    mmg(P, PB[:, nb * 512:(nb + 1) * 512], [(xs[:, kc, cs], Wo[:, kc, nb * 512:(nb + 1) * 512]) for kc in range(8)], reads=[xk, "Wo"], writes=["PB%d" % nb])
                P.op("act", lambda e: e.activation(out=so4[:, c4, :], in_=PB[:], func=AF.Sigmoid), reads=["PB0", "PB1"], writes=[("so", c4)])
                mmg(P, PC[:, 0:512], [(xs[:, kc, cs], Wk[:, kc, :]) for kc in range(8)], reads=[xk, "Wk"], writes=["PC0"])
                P.op("dve", lambda e: e.tensor_copy(out=k_tok4[:, c4, :], in_=PC[:, 0:512]), reads=["PC0"], writes=[("k_tok", c4)])
            qTk = [("qT", i) for i in range(4)]
            kTk = [("kT", i) for i in range(4)]
            for c4 in range(4):
                cs = slice(c4 * 128, (c4 + 1) * 128)
                tk = tok0 + c4 * 128
                C.cur = sc * 4 + c4
                v_tok, so, k_tok = v_tok4[:, c4, :], so4[:, c4, :], k_tok4[:, c4, :]
                kv, kso, kk = ("v_tok", c4), ("so", c4), ("k_tok", c4)
                mmg(P, PC[:, 512:528], [(xs[:, kc, cs], Wif[:, kc, :]) for kc in range(8)], reads=[xk, "Wif"], writes=["PC1"])
                chk(C, "proj")
                lif, k_lif = S_("lif", 0, 16)
                P.op("dve", lambda e: e.tensor_tensor(out=lif, in0=PC[:, 512:528], in1=ifb, op=ALU.add), reads=["PC1", "rp"], writes=[k_lif])
                spx, k_spx = S_("spx", 16, 8)
                P.op("act", lambda e: e.activation(out=spx, in_=sm[:, 8:16], func=AF.Exp, scale=-1.0), reads=[k_lif], writes=[k_spx])
                P.op("act", lambda e: e.activation(out=spx, in_=spx, func=AF.Ln, bias=1.0), reads=[k_spx], writes=[k_spx])
                lf, k_lf = S_("lf", 24, 8)
                P.op("dve", lambda e: e.tensor_scalar(out=lf, in0=spx, scalar1=-1.0, scalar2=None, op0=ALU.mult), reads=[k_spx], writes=[k_lf])
                P.op("pe", lambda e: e.matmul(PC[:, 528:536], lhsT=C.U, rhs=lf, start=True, stop=True), reads=["cst", k_lf], writes=["PC1"])
                P.op("pe", lambda e: e.matmul(PC[:, 536:544], lhsT=C.ones, rhs=lf, start=True, stop=True), reads=["cst", k_lf], writes=["PC1"])
                fcum, k_fcum = S_("fcum", 32, 8)
                P.op("act", lambda e: e.activation(out=fcum, in_=PC[:, 528:536], func=AF.Copy), reads=["PC1"], writes=[k_fcum])
                g, k_g = S_("g", 40, 8)
                P.op("dve", lambda e: e.tensor_tensor(out=g, in0=sm[:, 0:8], in1=PC[:, 528:536], op=ALU.subtract), reads=[k_lif, "PC1"], writes=[k_g])
                chk(C, "gates")
                P.op("dve", lambda e: e.tensor_tensor(out=rhsb[:], in0=C.ident.unsqueeze(1).broadcast_to([128, 8, 128]), in1=g.unsqueeze(2).broadcast_to([128, 8, 128]), op=ALU.mult),
                     reads=["cst", k_g], writes=["rhsb"])
                for q4 in range(2):
                    P.op("pe", lambda e, q4=q4: e.matmul(PD[:, q4 * 512:(q4 + 1) * 512], lhsT=C.ones, rhs=rhsb[:, q4 * 4:(q4 + 1) * 4, :].rearrange("p a b -> p (a b)"), start=True, stop=True),
                         reads=["cst", "rhsb"], writes=["PD%d" % q4])
                PD3 = PD[:].rearrange("p (a b) -> p a b", a=8)
                gmax, k_gmax = S_("gmax", 48, 8)
                P.op("dve", lambda e: e.tensor_reduce(out=gmax, in_=PD3, axis=AX.X, op=ALU.max), reads=["PD0", "PD1"], writes=[k_gmax])
                P.op("dve", lambda e: e.tensor_tensor(out=D1[:], in0=PD3, in1=C.nm.unsqueeze(1).broadcast_to([128, 8, 128]), op=ALU.add), reads=["PD0", "PD1", "cst"], writes=["D1"])
                m2, k_m2 = S_("m2", 56, 8)
                P.op("dve", lambda e: e.tensor_reduce(out=m2, in_=D1[:], axis=AX.X, op=ALU.max), reads=["D1"], writes=[k_m2])
                P.op("dve", lambda e: e.tensor_tensor(out=m2, in0=m2, in1=mprev[:], op=ALU.max), reads=[k_m2, "mprev"], writes=[k_m2])
                chk(C, "gb")
                P.op("dve", lambda e: e.tensor_tensor(out=rhsb[:], in0=C.ident.unsqueeze(1).broadcast_to([128, 8, 128]), in1=m2.unsqueeze(2).broadcast_to([128, 8, 128]), op=ALU.mult),
                     reads=["cst", k_m2], writes=["rhsb"])
                for q4 in range(2):
                    P.op("pe", lambda e, q4=q4: e.matmul(PA[:, q4 * 512:(q4 + 1) * 512], lhsT=C.ones, rhs=rhsb[:, q4 * 4:(q4 + 1) * 4, :].rearrange("p a b -> p (a b)"), start=True, stop=True),
                         reads=["cst", "rhsb"], writes=["PA%d" % q4])
                PA3 = PA[:].rearrange("p (a b) -> p a b", a=8)
                chk(C, "m2b")
                P.op("dve", lambda e: e.scalar_tensor_tensor(out=WT[:], in0=PA3, scalar=-1.0, in1=C.nmT.unsqueeze(1).broadcast_to([128, 8, 128]), op0=ALU.mult, op1=ALU.add),
                     reads=["PA0", "PA1", "cst"], writes=["WT"])
                P.op("dve", lambda e: e.tensor_tensor(out=WT[:], in0=WT[:], in1=g.unsqueeze(2).broadcast_to([128, 8, 128]), op=ALU.add), reads=["WT", k_g], writes=["WT"])
                P.op("act", lambda e: e.activation(out=WT[:], in_=WT[:], func=AF.Exp), reads=["WT"], writes=["WT"])
                chk(C, "wt")
                iw, k_iw = S_("iw", 64, 8)
                P.op("dve", lambda e: e.tensor_tensor(out=iw, in0=mprev[:], in1=m2, op=ALU.subtract), reads=["mprev", k_m2], writes=[k_iw])
                P.op("act", lambda e: e.activation(out=iw, in_=iw, func=AF.Exp), reads=[k_iw], writes=[k_iw])
                emt, k_emt = S_("emt", 72, 8)
                P.op("dve", lambda e: e.tensor_tensor(out=emt, in0=fcum, in1=m2, op=ALU.add), reads=[k_fcum, k_m2], writes=[k_emt])
                P.op("act", lambda e: e.activation(out=emt, in_=emt, func=AF.Exp, scale=-1.0), reads=[k_emt], writes=[k_emt])
                chk(C, "iw")
                for h in range(8):
                    ps = slice((h % 2) * 64, (h % 2) * 64 + 64)
                    P.op("pe", lambda e, h=h, ps=ps: e.matmul(PD[:, h * 128:(h + 1) * 128], lhsT=kT[:, h // 2, cs], rhs=qT[:, h % 2, h // 2, cs], start=True, stop=True),
                         reads=qTk + kTk, writes=["PD%d" % (h // 4)], inc=(h % 4 == 3))
                P.op("dve", lambda e: e.tensor_tensor(out=scT[:], in0=PD3, in1=WT[:], op=ALU.mult), reads=["PD0", "PD1", "WT"], writes=["scT"])
                chk(C, "qk")
                for h in range(8):
                    P.op("pe", lambda e, h=h: e.matmul(PA[:, h * 128:(h + 1) * 128], lhsT=scT[:, h, :], rhs=v_tok[:, h * 128:(h + 1) * 128], start=True, stop=True),
                         reads=["scT", kv], writes=["PA%d" % (h // 4)], inc=(h % 4 == 3))
                for h in range(8):
                    P.op("pe", lambda e, h=h: e.matmul(PC[:, 544 + h:545 + h], lhsT=scT[:, h, :], rhs=C.onesb[:, 0:1], start=True, stop=True),
                         reads=["scT", "cstb"], writes=["PC1"], inc=(h == 7))
                for h in range(8):
                    ps = slice((h % 2) * 64, (h % 2) * 64 + 64)
                    P.op("pe", lambda e, h=h, ps=ps: e.matmul(PB[:, h * 128:(h + 1) * 128], lhsT=qT[:, h % 2, h // 2, cs], rhs=Cbf[:, h // 2, :], start=True, stop=True),
                         reads=qTk + ["Cbf"], writes=["PB%d" % (h // 4)], inc=(h % 4 == 3))
                for h in range(8):
                    ps = slice((h % 2) * 64, (h % 2) * 64 + 64)
                    P.op("pe", lambda e, h=h, ps=ps: e.matmul(PC[:, 552 + h:553 + h], lhsT=qT[:, h % 2, h // 2, cs], rhs=nbf[:, h // 2:h // 2 + 1], start=True, stop=True),
                         reads=qTk + ["nbf"], writes=["PC1"], inc=(h == 7))
                chk(C, "nums")
                P.op("dve", lambda e: e.tensor_tensor(out=h3(hA[:]), in0=h3(PB[:]), in1=iw.unsqueeze(2).broadcast_to([128, 8, 128]), op=ALU.mult), reads=["PB0", "PB1", k_iw], writes=["hA"])
                P.op("dve", lambda e: e.tensor_tensor(out=hA[:], in0=hA[:], in1=PA[:], op=ALU.add), reads=["hA", "PA0", "PA1"], writes=["hA"])
                den, k_den = S_("den", 80, 8)
                P.op("dve", lambda e: e.tensor_tensor(out=den, in0=PC[:, 552:560], in1=iw, op=ALU.mult), reads=["PC1", k_iw], writes=[k_den])
                P.op("dve", lambda e: e.tensor_tensor(out=den, in0=den, in1=PC[:, 544:552], op=ALU.add), reads=[k_den, "PC1"], writes=[k_den])
                P.op("act", lambda e: e.activation(out=den, in_=den, func=AF.Abs), reads=[k_den], writes=[k_den])
                P.op("dve", lambda e: e.tensor_tensor(out=den, in0=den, in1=emt, op=ALU.max), reads=[k_den, k_emt], writes=[k_den])
                P.op("dve", lambda e: e.reciprocal(out=den, in_=den), reads=[k_den], writes=[k_den])
                P.op("dve", lambda e: e.tensor_tensor(out=h3(hA[:]), in0=h3(hA[:]), in1=den.unsqueeze(2).broadcast_to([128, 8, 128]), op=ALU.mult), reads=["hA", k_den], writes=["hA"])
                chk(C, "den")
                P.op("dve", lambda e: e.memset(ss[:], 0.0), writes=["ss"])
                for h in range(8):
                    P.op("act", lambda e, h=h: e.activation(out=hB[:, h * 128:(h + 1) * 128], in_=hA[:, h * 128:(h + 1) * 128], func=AF.Square, accum_out=ss[:, h:h + 1]), reads=["hA", "ss"], writes=["hB", "ss"])
                P.op("dve", lambda e: e.tensor_scalar(out=ss[:, 8:16], in0=ss[:, 0:8], scalar1=1.0 / 128.0, scalar2=EPS, op0=ALU.mult, op1=ALU.add), reads=["ss"], writes=["ss"])
                P.op("act", lambda e: e.activation(out=ss[:, 8:16], in_=ss[:, 8:16], func=AF.Ln), reads=["ss"], writes=["ss"])
                P.op("act", lambda e: e.activation(out=ss[:, 8:16], in_=ss[:, 8:16], func=AF.Exp, scale=-0.5), reads=["ss"], writes=["ss"])
                P.op("dve", lambda e: e.tensor_tensor(out=h3(hA[:]), in0=h3(hA[:]), in1=ss[:, 8:16].unsqueeze(2).broadcast_to([128, 8, 128]), op=ALU.mult), reads=["hA", "ss"], writes=["hA"])
                P.op("dve", lambda e: e.tensor_tensor(out=hB[:], in0=so, in1=normw, op=ALU.mult), reads=[kso, "rp", "hB"], writes=["hB"])
                P.op("dve", lambda e: e.tensor_tensor(out=hbf[:], in0=hA[:], in1=hB[:], op=ALU.mult), reads=["hA", "hB"], writes=["hbf"])
                for blk in range(8):
                    P.op("pe", lambda e, blk=blk: e.transpose(out=PCb[:, blk * 128:(blk + 1) * 128], in_=hbf[:, blk * 128:(blk + 1) * 128], identity=C.identb),
                         reads=["hbf", "cstb"], writes=["PC0"], inc=(blk == 7))
                ci = (sc * 4 + c4) % 2
                P.op("act", lambda e: e.activation(out=yTst[ci][:].rearrange("p a b -> p (a b)"), in_=PCb[:, 0:1024], func=AF.Copy), reads=["PC0"], writes=["lyTst%d" % ci])
                P.dma("sp", lambda q: q.dma_start(out=d["yTl"][:, :, tk:tk + 128], in_=yTst[ci][:]), "lyTst%d" % ci, reads=["lyTst%d" % ci], writes=["yTl_dram"])
                chk(C, "out")
                wend, k_wend = S_("wend", 88, 8)
                P.op("dve", lambda e: e.tensor_tensor(out=wend, in0=g, in1=gmax, op=ALU.subtract), reads=[k_g, k_gmax], writes=[k_wend])
                P.op("act", lambda e: e.activation(out=wend, in_=wend, func=AF.Exp), reads=[k_wend], writes=[k_wend])
                k3 = lambda ap: ap.rearrange("p (h q) -> p h q", h=8)
                P.op("dve", lambda e: e.tensor_tensor(out=k3(kw[:]), in0=k3(k_tok), in1=wend.unsqueeze(2).broadcast_to([128, 8, 64]), op=ALU.mult), reads=[kk, k_wend], writes=["kw"])
                for pr in range(4):
                    for hp in range(2):
                        h = pr * 2 + hp
                        P.op("pe", lambda e, pr=pr, hp=hp, h=h: e.matmul(PB[:, (hp * 4 + pr) * 128:(hp * 4 + pr + 1) * 128], lhsT=kw[:, pr * 128:(pr + 1) * 128], rhs=v_tok[:, h * 128:(h + 1) * 128], start=True, stop=True),
                             reads=["kw", kv], writes=["PB%d" % hp], inc=(pr == 3))
                for pr in range(4):
                    P.op("pe", lambda e, pr=pr: e.matmul(PC[:, 560 + pr:561 + pr], lhsT=kw[:, pr * 128:(pr + 1) * 128], rhs=C.onesb[:, 0:1], start=True, stop=True),
                         reads=["kw", "cstb"], writes=["PC1"], inc=(pr == 3))
                chk(C, "cloc")
                mloc, k_mloc = S_("mloc", 96, 8)
                P.op("dve", lambda e: e.tensor_tensor(out=mloc, in0=PC[:, 536:544], in1=gmax, op=ALU.add), reads=["PC1", k_gmax], writes=[k_mloc])
                aa, k_aa = S_("aa", 104, 8)
                P.op("dve", lambda e: e.tensor_tensor(out=aa, in0=PC[:, 536:544], in1=mprev[:], op=ALU.add), reads=["PC1", "mprev"], writes=[k_aa])
                P.op("dve", lambda e: e.tensor_tensor(out=mprev[:], in0=aa, in1=mloc, op=ALU.max), reads=[k_aa, k_mloc, "mprev"], writes=["mprev"])
                sps, k_sps = S_("sps", 112, 16)
                P.op("dve", lambda e: e.tensor_tensor(out=sm[:, 112:120], in0=aa, in1=mprev[:], op=ALU.subtract), reads=[k_aa, "mprev"], writes=[k_sps])
                P.op("dve", lambda e: e.tensor_tensor(out=sm[:, 120:128], in0=mloc, in1=mprev[:], op=ALU.subtract), reads=[k_mloc, "mprev", k_sps], writes=[k_sps])
                P.op("act", lambda e: e.activation(out=sps, in_=sps, func=AF.Exp), reads=[k_sps], writes=[k_sps])
                PB4 = PB[:].rearrange("p (a b c) -> p a b c", a=2, b=4)
                for hp in range(2):
                    ps = slice(hp * 64, hp * 64 + 64)
                    sp_h = sm[ps, 112 + hp:120:2]
                    sl_h = sm[ps, 120 + hp:128:2]
                    P.op("dve", lambda e: e.tensor_tensor(out=Cst[ps], in0=Cst[ps], in1=sp_h.unsqueeze(2).broadcast_to([64, 4, 128]), op=ALU.mult), reads=["Cst", k_sps], writes=["Cst"])
                    P.op("dve", lambda e: e.tensor_tensor(out=Ctmp[ps], in0=PB4[ps, hp], in1=sl_h.unsqueeze(2).broadcast_to([64, 4, 128]), op=ALU.mult), reads=["PB%d" % hp, k_sps], writes=["Ctmp"])
                    P.op("dve", lambda e: e.tensor_tensor(out=Cst[ps], in0=Cst[ps], in1=Ctmp[ps], op=ALU.add), reads=["Cst", "Ctmp"], writes=["Cst"])
                    P.op("dve", lambda e: e.tensor_tensor(out=nst[ps], in0=nst[ps], in1=sp_h, op=ALU.mult), reads=["nst", k_sps], writes=["nst"])
                    P.op("dve", lambda e: e.tensor_tensor(out=ntmp[ps], in0=PC[ps, 560:564], in1=sl_h, op=ALU.mult), reads=["PC1", k_sps], writes=["ntmp"])
                    P.op("dve", lambda e: e.tensor_tensor(out=nst[ps], in0=nst[ps], in1=ntmp[ps], op=ALU.add), reads=["nst", "ntmp"], writes=["nst"])
                P.op("act", lambda e: e.activation(out=Cbf[:], in_=Cst[:], func=AF.Copy), reads=["Cst"], writes=["Cbf"])
                P.op("act", lambda e: e.activation(out=nbf[:], in_=nst[:], func=AF.Copy), reads=["nst"], writes=["nbf"])
        P.barrier()


ALPHA = float((2.0 * 1) ** 0.25)
CAP = 256
NE = 64


def phaseB(C):
    P, nc, d = C.P, C.nc, C.d
    TOK = C.TOK
    with ExitStack() as es:
        sb = lambda n, s, dt: es.enter_context(nc.sbuf_tensor("pb_" + n, s, dt))
        win = d["w_in"].rearrange("(kc p) c -> p kc c", p=128)
        Wa = sb("Wa", [128, 8, 1024], BF16)
        Wb = sb("Wb", [128, 8, 1024], BF16)
        Wgs = sb("Wgs", [128, 8, 1024], BF16)
        Wgl = sb("Wgl", [128, 8, 1024], BF16)
        Wout = sb("Wout", [128, 8, 1024], BF16)
        P.dma("pool", lambda q: q.dma_start(out=Wa[:], in_=d["w_a"].rearrange("(kc p) c -> p kc c", p=128)), "Wa", writes=["Wa"])
        P.dma("pool", lambda q: q.dma_start(out=Wgs[:], in_=win[:, :, 6176:7200]), "Wgs", writes=["Wgs"])
        P.dma("pool", lambda q: q.dma_start(out=Wb[:], in_=d["w_b"].rearrange("(kc p) c -> p kc c", p=128)), "Wb", writes=["Wb"])
        P.dma("pool", lambda q: q.dma_start(out=Wgl[:], in_=win[:, :, 7200:8224]), "Wgl", writes=["Wgl"])
        P.dma("pool", lambda q: q.dma_start(out=Wout[:], in_=d["w_out"].rearrange("(kc p) c -> p kc c", p=128)), "Wout", writes=["Wout"])
        Wr = sb("Wr", [128, 8, 72], F32)
        P.dma("sp", lambda q: q.dma_start(out=Wr[:], in_=d["w_router"].rearrange("(kc p) c -> p kc c", p=128)), "Wr", writes=["Wr"])
        rp = sb("rp", [128, 2048 + 72 + 64], F32)
        P.dma("sp", lambda q: q.dma_start(out=rp[:], in_=d["rep_mid"][:, :]), "rp", writes=["rp"])
        lng, lnb, rbias = rp[:, 0:1024], rp[:, 1024:2048], rp[:, 2048:2120]
        cnti = sb("cnti", [128, 64], F32)
        P.op("dve", lambda e: e.tensor_copy(out=cnti[:], in_=rp[:, 2120:2184]), reads=["rp"], writes=["cnti"])

        xTs = [sb("xTs%d" % i, [128, 8, 512], BF16) for i in range(2)]
        yTs = [sb("yTs%d" % i, [128, 8, 512], BF16) for i in range(2)]
        yTl = [sb("yTl%d" % i, [128, 8, 512], BF16) for i in range(2)]
        mT = [sb("mT%d" % i, [128, 8, 512], BF16) for i in range(2)]
        sg = [sb("sg%d" % i, [128, 512], F32) for i in range(2)]
        t1 = [sb("t1%d" % i, [128, 512], F32) for i in range(2)]
        xf = [sb("xf%d" % i, [128, 1024], F32) for i in range(2)]
        r = sb("r", [128, 1024], F32)
        junk = sb("junk", [128, 1024], F32)
        h1 = [sb("h1%d" % i, [128, 1024], F32) for i in range(2)]
        h1b = [sb("h1b%d" % i, [128, 1024], BF16) for i in range(2)]
        h1T = sb("h1T", [128, 8, 128], F32)
        st = sb("st", [128, 16], F32)
        lg = sb("lg", [128, 72], F32)
        rt = sb("rt", [128, 512], F32)
        ohsb = sb("ohsb", [128, 64], BF16)

        PA, PB, PC, PD = C.PA, C.PB, C.PC, C.PD
        NTB = TOK // 512

        def loadsB(tb):
            s2 = tb % 2
            tok0 = tb * 512
            for nm_, buf, src in (("xTs", xTs, "xT"), ("yTs", yTs, "yTs"), ("yTl", yTl, "yTl")):
                P.dma("sp", lambda q, buf=buf, src=src: q.dma_start(out=buf[s2][:], in_=d[src][:, :, tok0:tok0 + 512]), "b%s%d" % (nm_, s2),
                      reads=[src + "_dram"], writes=["%s%d" % (nm_, s2)])

        def mblock(tb, m):
            s2 = tb % 2
            kx, ks, kl = "xTs%d" % s2, "yTs%d" % s2, "yTl%d" % s2
            ms = slice(m * 128, (m + 1) * 128)
            mmg(P, PA[:, 0:512], [(Wa[:, kc, ms], yTs[s2][:, kc, :]) for kc in range(8)], reads=["Wa", ks], writes=["PA0"])
            mmg(P, PA[:, 512:1024], [(Wgs[:, kc, ms], xTs[s2][:, kc, :]) for kc in range(8)], reads=["Wgs", kx], writes=["PA1"])
            mmg(P, PB[:, 0:512], [(Wb[:, kc, ms], yTl[s2][:, kc, :]) for kc in range(8)], reads=["Wb", kl], writes=["PB0"])
            mmg(P, PB[:, 512:1024], [(Wgl[:, kc, ms], xTs[s2][:, kc, :]) for kc in range(8)], reads=["Wgl", kx], writes=["PB1"])
            P.op("act", lambda e: e.activation(out=sg[0][:], in_=PA[:, 512:1024], func=AF.Sigmoid), reads=["PA1"], writes=["sg0"])
            P.op("act", lambda e: e.activation(out=sg[1][:], in_=PB[:, 512:1024], func=AF.Sigmoid), reads=["PB1"], writes=["sg1"])
            P.op("dve", lambda e: e.tensor_tensor(out=t1[0][:], in0=sg[0][:], in1=PA[:, 0:512], op=ALU.mult), reads=["sg0", "PA0"], writes=["t10"])
            P.op("dve", lambda e: e.tensor_tensor(out=t1[1][:], in0=sg[1][:], in1=PB[:, 0:512], op=ALU.mult), reads=["sg1", "PB0"], writes=["t11"])
            P.op("dve", lambda e: e.tensor_tensor(out=mT[s2][:, m, :], in0=t1[0][:], in1=t1[1][:], op=ALU.add), reads=["t10", "t11"], writes=[("mT", s2, m)])

        def out_ln(tb, c4):
            s2 = tb % 2
            nch = tb * 4 + c4
            ci = nch % 2
            cs = slice(c4 * 128, (c4 + 1) * 128)
            tk = tb * 512 + c4 * 128
            mTk = [("mT", s2, m) for m in range(8)]
            P.dma("sp", lambda q: q.dma_start(out=xf[ci][:], in_=d["x"][tk:tk + 128, :]), "bxf%d" % ci, writes=["xf%d" % ci])
            for nb in range(2):
                mmg(P, PC[:, nb * 512:(nb + 1) * 512], [(mT[s2][:, kc, cs], Wout[:, kc, nb * 512:(nb + 1) * 512]) for kc in range(8)], reads=mTk + ["Wout"], writes=["PC%d" % nb])
            P.op("dve", lambda e: e.scalar_tensor_tensor(out=r[:], in0=xf[ci][:], scalar=ALPHA, in1=PC[:], op0=ALU.mult, op1=ALU.add), reads=["xf%d" % ci, "PC0", "PC1"], writes=["r"])
            layer_norm(C, P, r, "r", junk, st, lng, lnb, "rp", h1[ci], "h1%d" % ci)
            P.dma("sp", lambda q: q.dma_start(out=d["h1"][tk:tk + 128, :], in_=h1[ci][:]), "bh1%d" % ci, reads=["h1%d" % ci], writes=["h1_dram"])
            P.op("act", lambda e: e.activation(out=h1b[ci][:], in_=h1[ci][:], func=AF.Copy), reads=["h1%d" % ci], writes=["h1b%d" % ci])

        def logits(tb, c4):
            nch = tb * 4 + c4
            ci = nch % 2
            for kc in range(8):
                P.op("pe", lambda e, kc=kc: e.transpose(out=PD[:, kc * 128:(kc + 1) * 128], in_=h1[ci][:, kc * 128:(kc + 1) * 128], identity=C.ident),
                     reads=["h1%d" % ci, "cst"], writes=["PD%d" % (kc // 4)], inc=(kc % 4 == 3))
            P.op("act", lambda e: e.activation(out=h1T[:].rearrange("p a b -> p (a b)"), in_=PD[:], func=AF.Copy), reads=["PD0", "PD1"], writes=["h1T"])
            mmg(P, PD[:, 0:72], [(h1T[:, kc, :], Wr[:, kc, :]) for kc in range(8)], reads=["h1T", "Wr"], writes=["PD0"])
            P.op("dve", lambda e: e.tensor_tensor(out=lg[:], in0=PD[:, 0:72], in1=rbias, op=ALU.add), reads=["PD0", "rp"], writes=["lg"])

        def route_front(tb, c4):
            route(C, P, lg, rt, cnti, ohsb, PD, "PD", tb * 4 + c4, part="front")

        def dispatch(tb, c4):
            nch = tb * 4 + c4
            ci = nch % 2
            route(C, P, lg, rt, cnti, ohsb, PD, "PD", nch, part="back")
            for k in range(2):
                P.dma("pool", lambda q, k=k: q.indirect_dma_start(out=d["xs"][:, :], out_offset=bass.IndirectOffsetOnAxis(ap=C.slots[:, nch, k:k + 1], axis=0),
                                                                 in_=h1b[ci][:], in_offset=None),
                      "bsc%d" % ci, reads=["h1b%d" % ci, ("slots", nch)], writes=[("xs_dram", nch, k)])

        pend = None
        loadsB(0)
        for m in range(8):
            mblock(0, m)
        for tb in range(NTB):
            nxt = tb + 1 < NTB
            if nxt:
                loadsB(tb + 1)
            for c4 in range(4):
                out_ln(tb, c4)
                if pend is not None:
                    dispatch(*pend)
                if nxt:
                    mblock(tb + 1, 2 * c4)
                logits(tb, c4)
                route_front(tb, c4)
                if nxt:
                    mblock(tb + 1, 2 * c4 + 1)
                pend = (tb, c4)
        dispatch(*pend)
        P.barrier()


def layer_norm(C, P, x, kx, junk, st, g, b, kgb, out, kout):
    P.op("dve", lambda e: e.memset(st[:, 0:2], 0.0), writes=["st"])
    P.op("act", lambda e: e.activation(out=junk[:], in_=x[:], func=AF.Copy, accum_out=st[:, 0:1]), reads=[kx, "st"], writes=["junk", "st"])
    P.op("act", lambda e: e.activation(out=junk[:], in_=x[:], func=AF.Square, accum_out=st[:, 1:2]), reads=[kx, "st", "junk"], writes=["junk", "st"])
    P.op("dve", lambda e: e.tensor_scalar(out=st[:, 2:4], in0=st[:, 0:2], scalar1=1.0 / 1024.0, scalar2=None, op0=ALU.mult), reads=["st"], writes=["st"])
    P.op("dve", lambda e: e.tensor_tensor(out=st[:, 4:5], in0=st[:, 2:3], in1=st[:, 2:3], op=ALU.mult), reads=["st"], writes=["st"])
    P.op("dve", lambda e: e.tensor_tensor(out=st[:, 5:6], in0=st[:, 3:4], in1=st[:, 4:5], op=ALU.subtract), reads=["st"], writes=["st"])
    P.op("dve", lambda e: e.tensor_scalar(out=st[:, 5:6], in0=st[:, 5:6], scalar1=EPS, scalar2=None, op0=ALU.add), reads=["st"], writes=["st"])
    P.op("act", lambda e: e.activation(out=st[:, 5:6], in_=st[:, 5:6], func=AF.Ln), reads=["st"], writes=["st"])
    P.op("act", lambda e: e.activation(out=st[:, 6:7], in_=st[:, 5:6], func=AF.Exp, scale=-0.5), reads=["st"], writes=["st"])
    P.op("dve", lambda e: e.scalar_tensor_tensor(out=x[:], in0=x[:], scalar=st[:, 2:3], in1=g, op0=ALU.subtract, op1=ALU.mult), reads=[kx, "st", kgb], writes=[kx])
    P.op("dve", lambda e: e.scalar_tensor_tensor(out=out[:], in0=x[:], scalar=st[:, 6:7], in1=b, op0=ALU.mult, op1=ALU.add), reads=[kx, "st", kgb], writes=[kout])


def route(C, P, lg, rt, cnti, ohsb, PY, ny, nch, part=None):
    R = lambda lo, n: rt[:, lo:lo + n]
    k = "rt"
    gl = lg[:, 0:8]
    el = lg[:, 8:72].rearrange("p (g e) -> p g e", g=8)
    gmx, ge, gsum, ohg = R(0, 1), R(8, 8), R(1, 1), R(16, 8)
    emit = [part != "back"]
    D = lambda f, reads=(), writes=(k,): (P.op("dve", f, reads=list(reads) + [k, "lg"], writes=list(writes)) if emit[0] else None)
    A = lambda f, reads=(), writes=(k,): (P.op("act", f, reads=list(reads) + [k, "lg"], writes=list(writes)) if emit[0] else None)
    D(lambda e: e.tensor_reduce(out=gmx, in_=gl, axis=AX.X, op=ALU.max))
    D(lambda e: e.tensor_scalar(out=R(2, 1), in0=gmx, scalar1=-1.0, scalar2=None, op0=ALU.mult))
    D(lambda e: e.memset(gsum, 0.0))
    A(lambda e: e.activation(out=ge, in_=gl, func=AF.Exp, bias=R(2, 1), accum_out=gsum))
    D(lambda e: e.reciprocal(out=R(3, 1), in_=gsum))
    D(lambda e: e.tensor_scalar(out=ohg, in0=gl, scalar1=gmx, scalar2=None, op0=ALU.is_equal))
    msk = R(64, 64).rearrange("p (g e) -> p g e", g=8)
    D(lambda e: e.tensor_tensor(out=msk, in0=el, in1=ohg.unsqueeze(2).broadcast_to([128, 8, 8]), op=ALU.mult))
    ing = R(24, 8)
    D(lambda e: e.tensor_reduce(out=ing, in_=R(64, 64).rearrange("p (g e) -> p e g", g=8), axis=AX.X, op=ALU.add))
    m1, oh1, ing2, m2v, oh2 = R(4, 1), R(32, 8), R(40, 8), R(5, 1), R(48, 8)
    D(lambda e: e.tensor_reduce(out=m1, in_=ing, axis=AX.X, op=ALU.max))
    D(lambda e: e.tensor_scalar(out=oh1, in0=ing, scalar1=m1, scalar2=None, op0=ALU.is_equal))
    D(lambda e: e.scalar_tensor_tensor(out=ing2, in0=oh1, scalar=NEG, in1=ing, op0=ALU.mult, op1=ALU.add))
    D(lambda e: e.tensor_reduce(out=m2v, in_=ing2, axis=AX.X, op=ALU.max))
    D(lambda e: e.tensor_scalar(out=oh2, in0=ing2, scalar1=m2v, scalar2=None, op0=ALU.is_equal))
    D(lambda e: e.tensor_tensor(out=R(6, 1), in0=m2v, in1=m1, op=ALU.subtract))
    A(lambda e: e.activation(out=R(6, 1), in_=R(6, 1), func=AF.Exp))
    D(lambda e: e.tensor_scalar(out=R(7, 1), in0=R(6, 1), scalar1=1.0, scalar2=None, op0=ALU.add))
    D(lambda e: e.reciprocal(out=R(7, 1), in_=R(7, 1)))
    D(lambda e: e.tensor_tensor(out=R(56, 1), in0=R(7, 1), in1=R(3, 1), op=ALU.mult))
    D(lambda e: e.tensor_tensor(out=R(57, 1), in0=R(56, 1), in1=R(6, 1), op=ALU.mult))
    D(lambda e: e.tensor_copy(out=C.gates[:, nch, :], in_=R(56, 2)), writes=(("gates", nch),))
    OH1 = R(128, 64).rearrange("p (g e) -> p g e", g=8)
    OH2 = R(192, 64).rearrange("p (g e) -> p g e", g=8)
    D(lambda e: e.tensor_tensor(out=OH1, in0=ohg.unsqueeze(2).broadcast_to([128, 8, 8]), in1=oh1.unsqueeze(1).broadcast_to([128, 8, 8]), op=ALU.mult))
    D(lambda e: e.tensor_tensor(out=OH2, in0=ohg.unsqueeze(2).broadcast_to([128, 8, 8]), in1=oh2.unsqueeze(1).broadcast_to([128, 8, 8]), op=ALU.mult))
    D(lambda e: e.tensor_tensor(out=ohsb[:], in0=R(128, 64), in1=R(192, 64), op=ALU.add), writes=("ohsb",))
    if part == "front":
        return
    emit[0] = True
    P.op("pe", lambda e: e.matmul(PY[:, 128:192], lhsT=C.Ustrb, rhs=ohsb[:], start=True, stop=True), reads=["cstb", "ohsb"], writes=[ny + "0"])
    P.op("pe", lambda e: e.matmul(PY[:, 192:256], lhsT=C.onesb, rhs=ohsb[:], start=True, stop=True), reads=["cstb", "ohsb"], writes=[ny + "0"])
    base = R(256, 64)
    D(lambda e: e.tensor_tensor(out=base, in0=PY[:, 128:192], in1=cnti[:], op=ALU.add), reads=[ny + "0", "cnti"])
    D(lambda e: e.tensor_tensor(out=R(320, 64), in0=base, in1=R(128, 64), op=ALU.mult))
    D(lambda e: e.tensor_reduce(out=R(60, 1), in_=R(320, 64), axis=AX.X, op=ALU.add))
    D(lambda e: e.tensor_tensor(out=R(384, 64), in0=base, in1=R(192, 64), op=ALU.mult))
    D(lambda e: e.tensor_reduce(out=R(61, 1), in_=R(384, 64), axis=AX.X, op=ALU.add))
    D(lambda e: e.tensor_scalar(out=R(60, 2), in0=R(60, 2), scalar1=float(NE * CAP - 1), scalar2=None, op0=ALU.min))
    D(lambda e: e.tensor_copy(out=C.slots[:, nch, :], in_=R(60, 2)), writes=(("slots", nch),))
    D(lambda e: e.tensor_tensor(out=cnti[:], in0=cnti[:], in1=PY[:, 192:256], op=ALU.add), reads=[ny + "0", "cnti"], writes=("cnti",))


def zero_xs(C, es):
    P, nc, d = C.P, C.nc, C.d
    z = es.enter_context(nc.sbuf_tensor("zx", [128, 2, 1024], BF16))
    P.op("dve", lambda e: e.memset(z[:].rearrange("p a b -> p (a b)"), 0.0), writes=["zx"])
    xs3 = d["xs"].rearrange("(r t p) f -> r p t f", p=128, t=2)
    for r in range((NE + 1) * CAP // 256):
        P.dma("pool", lambda q, r=r: q.dma_start(out=xs3[r], in_=z[:]), "zx", reads=["zx"], writes=[("xs0", r)])


def phaseC(C):
    P, nc, d = C.P, C.nc, C.d
    with ExitStack() as es:
        sb = lambda n, s, dt: es.enter_context(nc.sbuf_tensor("pc_" + n, s, dt))
        Wg = [sb("Wg%d" % i, [128, 8, 512], BF16) for i in range(2)]
        Wu = [sb("Wu%d" % i, [128, 8, 512], BF16) for i in range(2)]
        Wd = [sb("Wd%d" % i, [128, 4, 1024], BF16) for i in range(2)]
        NS = 2
        Fg = [sb("Fg%d" % i, [128, 8, 512], F32) for i in range(NS)]
        Fu = [sb("Fu%d" % i, [128, 8, 512], F32) for i in range(NS)]
        Fd = [sb("Fd%d" % i, [128, 4, 1024], F32) for i in range(NS)]

        def load_w(e_):
            f = e_ % NS
            P.dma("sp", lambda q: q.dma_start(out=Fg[f][:], in_=d["moe_wg"][e_].rearrange("(kc p) f -> p kc f", p=128)), "cfg%d" % f, writes=["Fg%d" % f])
            P.dma("sp", lambda q: q.dma_start(out=Fu[f][:], in_=d["moe_wu"][e_].rearrange("(kc p) f -> p kc f", p=128)), "cfu%d" % f, writes=["Fu%d" % f])
            P.dma("sp", lambda q: q.dma_start(out=Fd[f][:], in_=d["moe_wd"][e_].rearrange("(fc p) n -> p fc n", p=128)), "cfd%d" % f, writes=["Fd%d" % f])

        def cast_w(e_):
            s, f = e_ % 2, e_ % NS
            f2 = lambda t: t[:].rearrange("p a b -> p (a b)")
            P.op("dve", lambda e: e.tensor_copy(out=f2(Wg[s]), in_=f2(Fg[f])), reads=["Fg%d" % f], writes=["Wg%d" % s])
            P.op("act", lambda e: e.activation(out=f2(Wu[s]), in_=f2(Fu[f]), func=AF.Copy), reads=["Fu%d" % f], writes=["Wu%d" % s])
            P.op("dve", lambda e: e.tensor_copy(out=f2(Wd[s]), in_=f2(Fd[f])), reads=["Fd%d" % f], writes=["Wd%d" % s])
        for e0 in range(NS):
            load_w(e0)
        cast_w(0)
        xse = [sb("xse%d" % i, [128, 2, 1024], BF16) for i in range(2)]
        xsT = sb("xsT", [128, 8, 256], BF16)
        hs = sb("hs", [128, 1024], F32)
        hidT = sb("hidT", [128, 4, 256], BF16)
        yo = [sb("yo%d" % i, [128, 2, 1024], F32) for i in range(2)]
        PA, PB, PC, PD = C.PA, C.PB, C.PC, C.PD
        PCb = PC[:].bitcast(BF16)
        def load_xs(e_):
            s = e_ % 2
            P.dma("pool", lambda q: q.dma_start(out=xse[s][:], in_=d["xs"][e_ * CAP:(e_ + 1) * CAP, :].rearrange("(t p) f -> p t f", p=128)), "cxs%d" % s, writes=["xse%d" % s])
        load_xs(0)
        for e_ in range(NE):
            s = e_ % 2
            for t in range(2):
                for kc in range(8):
                    P.op("pe", lambda e, t=t, kc=kc: e.transpose(out=PCb[:, kc * 256 + t * 128:kc * 256 + (t + 1) * 128], in_=xse[s][:, t, kc * 128:(kc + 1) * 128], identity=C.identb),
                         reads=["xse%d" % s, "cstb"], writes=["PC%d" % (kc // 4)], inc=(t == 1 and kc % 4 == 3))
            if e_ + 1 < NE:
                load_xs(e_ + 1)
            xsT2 = xsT[:].rearrange("p a b -> p (a b)")
            P.op("act", lambda e: e.activation(out=xsT2[:, 0:1024], in_=PCb[:, 0:1024], func=AF.Copy), reads=["PC0"], writes=["xsT0"])
            P.op("dve", lambda e: e.tensor_copy(out=xsT2[:, 1024:2048], in_=PCb[:, 1024:2048]), reads=["PC1"], writes=["xsT1"])
            for fb in range(4):
                fs = slice(fb * 128, (fb + 1) * 128)
                mmg(P, PA[:, fb * 256:(fb + 1) * 256], [(Wg[s][:, kc, fs], xsT[:, kc, :]) for kc in range(8)], reads=["Wg%d" % s, "xsT0", "xsT1"], writes=["PA%d" % (fb // 2)])
            for fb in range(4):
                fs = slice(fb * 128, (fb + 1) * 128)
                mmg(P, PB[:, fb * 256:(fb + 1) * 256], [(Wu[s][:, kc, fs], xsT[:, kc, :]) for kc in range(8)], reads=["Wu%d" % s, "xsT0", "xsT1"], writes=["PB%d" % (fb // 2)])
            if e_ + 1 < NE:
                cast_w(e_ + 1)
            if e_ + NS < NE:
                load_w(e_ + NS)
            P.op("act", lambda e: e.activation(out=hs[:], in_=PA[:], func=AF.Silu), reads=["PA0", "PA1"], writes=["hs"])
            P.op("dve", lambda e: e.tensor_tensor(out=hidT[:].rearrange("p a b -> p (a b)"), in0=hs[:], in1=PB[:], op=ALU.mult), reads=["hs", "PB0", "PB1"], writes=["hidT"])
            for t in range(2):
                PY, ny = (PD, "PD") if t == 0 else (PA, "PA")
                for nb in range(2):
                    mmg(P, PY[:, nb * 512:(nb + 1) * 512], [(hidT[:, fb, t * 128:(t + 1) * 128], Wd[s][:, fb, nb * 512:(nb + 1) * 512]) for fb in range(4)],
                        reads=["hidT", "Wd%d" % s], writes=[ny + "%d" % nb])
                if t == 0:
                    P.op("act", lambda e: e.activation(out=yo[s][:, 0, :], in_=PD[:], func=AF.Copy), reads=["PD0", "PD1"], writes=[("yo", s, 0)])
                else:
                    P.op("dve", lambda e: e.tensor_copy(out=yo[s][:, 1, :], in_=PA[:]), reads=["PA0", "PA1"], writes=[("yo", s, 1)])
            P.dma("pool", lambda q: q.dma_start(out=d["ys"][e_ * CAP:(e_ + 1) * CAP, :].rearrange("(t p) f -> p t f", p=128), in_=yo[s][:]), "cyo%d" % s,
                  reads=[("yo", s, 0), ("yo", s, 1)], writes=[("ys_dram", e_)])
        P.barrier()


def phaseD(C):
    P, nc, d = C.P, C.nc, C.d
    with ExitStack() as es:
        sb = lambda n, s, dt: es.enter_context(nc.sbuf_tensor("pd_" + n, s, dt))
        Wpg = sb("Wpg", [128, 8, 1024], BF16)
        Wpp = sb("Wpp", [128, 2, 1024], BF16)
        P.dma("pool", lambda q: q.dma_start(out=Wpg[:], in_=d["w_pg"].rearrange("(kc p) c -> p kc c", p=128)), "Wpg", writes=["Wpg"])
        P.dma("pool", lambda q: q.dma_start(out=Wpp[:], in_=d["w_pp"].rearrange("(kc p) c -> p kc c", p=128)), "Wpp", writes=["Wpp"])
        rp = sb("rp", [128, 2048], F32)
        P.dma("sp", lambda q: q.dma_start(out=rp[:], in_=d["rep_ln2"][:, :]), "rp", writes=["rp"])
        lng, lnb = rp[:, 0:1024], rp[:, 1024:2048]
        y1 = [sb("y1%d" % i, [128, 1024], F32) for i in range(2)]
        y2 = [sb("y2%d" % i, [128, 1024], F32) for i in range(2)]
        h1 = [sb("h1%d" % i, [128, 1024], F32) for i in range(2)]
        pf = [sb("pf%d" % i, [128, 256], F32) for i in range(2)]
        m = sb("m", [128, 1024], F32)
        junk = sb("junk", [128, 1024], F32)
        st = sb("st", [128, 16], F32)
        x2 = [sb("x2%d" % i, [128, 1024], F32) for i in range(2)]
        x2b = [sb("x2b%d" % i, [128, 1280], BF16) for i in range(2)]
        x2T = sb("x2T", [128, 10, 128], BF16)
        sgt = sb("sgt", [128, 1024], F32)
        ot = [sb("ot%d" % i, [128, 1024], F32) for i in range(2)]
        PA, PB, PC, PD = C.PA, C.PB, C.PC, C.PD
        PCb = PC[:].bitcast(BF16)
        NCH = C.TOK // 128

        def loads(ch):
            s = ch % 2
            tk = ch * 128
            P.dma("pool", lambda q: q.indirect_dma_start(out=y1[s][:], out_offset=None, in_=d["ys"][:, :], in_offset=bass.IndirectOffsetOnAxis(ap=C.slots[:, ch, 0:1], axis=0)),
                  "dy1%d" % s, writes=["y1%d" % s])
            P.dma("pool", lambda q: q.indirect_dma_start(out=y2[s][:], out_offset=None, in_=d["ys"][:, :], in_offset=bass.IndirectOffsetOnAxis(ap=C.slots[:, ch, 1:2], axis=0)),
                  "dy2%d" % s, writes=["y2%d" % s])
            P.dma("sp", lambda q: q.dma_start(out=h1[s][:], in_=d["h1"][tk:tk + 128, :]), "dh1%d" % s, writes=["h1%d" % s])
            P.dma("sp", lambda q: q.dma_start(out=pf[s][:], in_=d["p"][tk:tk + 128, :]), "dpf%d" % s, writes=["pf%d" % s])

        def combine(ch):
            s = ch % 2
            P.op("dve", lambda e: e.tensor_scalar(out=m[:], in0=y1[s][:], scalar1=C.gates[:, ch, 0:1], scalar2=None, op0=ALU.mult), reads=["y1%d" % s], writes=["m"])
            P.op("dve", lambda e: e.scalar_tensor_tensor(out=m[:], in0=y2[s][:], scalar=C.gates[:, ch, 1:2], in1=m[:], op0=ALU.mult, op1=ALU.add), reads=["y2%d" % s, "m"], writes=["m"])
            P.op("dve", lambda e: e.scalar_tensor_tensor(out=m[:], in0=h1[s][:], scalar=ALPHA, in1=m[:], op0=ALU.mult, op1=ALU.add), reads=["h1%d" % s, "m"], writes=["m"])

        def norm(ch):
            s = ch % 2
            layer_norm(C, P, m, "m", junk, st, lng, lnb, "rp", x2[s], "x2%d" % s)
            P.op("act", lambda e: e.activation(out=x2b[s][:, 0:1024], in_=x2[s][:], func=AF.Copy), reads=["x2%d" % s], writes=["x2b%d" % s])
            P.op("act", lambda e: e.activation(out=x2b[s][:, 1024:1280], in_=pf[s][:], func=AF.Copy), reads=["pf%d" % s, "x2b%d" % s], writes=["x2b%d" % s])

        def transp(ch):
            s = ch % 2
            for kc in range(10):
                P.op("pe", lambda e, kc=kc: e.transpose(out=PCb[:, kc * 128:(kc + 1) * 128], in_=x2b[s][:, kc * 128:(kc + 1) * 128], identity=C.identb),
                     reads=["x2b%d" % s, "cstb"], writes=["PC0", "PC1"], inc=(kc == 9))

        def xcopy(ch):
            P.op("dve", lambda e: e.tensor_copy(out=x2T[:].rearrange("p a b -> p (a b)"), in_=PCb[:, 0:1280]), reads=["PC0", "PC1"], writes=["x2T"])

        def mms(ch):
            for nb in range(2):
                mmg(P, PA[:, nb * 512:(nb + 1) * 512], [(x2T[:, kc, :], Wpg[:, kc, nb * 512:(nb + 1) * 512]) for kc in range(8)], reads=["x2T", "Wpg"], writes=["PA%d" % nb])
                mmg(P, PB[:, nb * 512:(nb + 1) * 512], [(x2T[:, 8 + kc, :], Wpp[:, kc, nb * 512:(nb + 1) * 512]) for kc in range(2)], reads=["x2T", "Wpp"], writes=["PB%d" % nb])

        def fin(ch):
            s = ch % 2
            tk = ch * 128
            P.op("act", lambda e: e.activation(out=sgt[:], in_=PA[:], func=AF.Sigmoid), reads=["PA0", "PA1"], writes=["sgt"])
            P.op("dve", lambda e: e.tensor_tensor(out=sgt[:], in0=sgt[:], in1=PB[:], op=ALU.mult), reads=["sgt", "PB0", "PB1"], writes=["sgt"])
            P.op("dve", lambda e: e.tensor_tensor(out=ot[s][:], in0=sgt[:], in1=x2[s][:], op=ALU.add), reads=["sgt", "x2%d" % s], writes=["ot%d" % s])
            P.dma("sp", lambda q: q.dma_start(out=d["out"][tk:tk + 128, :], in_=ot[s][:]), "dot%d" % s, reads=["ot%d" % s], writes=[("out_dram", ch)])

        loads(0)
        if NCH > 1:
            loads(1)
        combine(0)
        norm(0)
        for ch in range(NCH):
            transp(ch)
            if ch + 1 < NCH:
                combine(ch + 1)
            xcopy(ch)
            mms(ch)
            if ch + 1 < NCH:
                norm(ch + 1)
            if ch + 2 < NCH:
                loads(ch + 2)
            fin(ch)
        P.barrier()


def make_consts():
    c = np.zeros((128, 6, 128), np.float32)
    i = np.arange(128)
    c[:, 0, :] = np.eye(128)
    c[:, 1, :] = (i[:, None] <= i[None, :])
    c[:, 2, :] = 1.0
    c[:, 3, :] = np.where(i[None, :] >= i[:, None], 0.0, NEG)
    c[:, 4, :] = np.where(i[None, :] <= i[:, None], 0.0, NEG)
    c[:, 5, :] = (i[:, None] < i[None, :])
    return c.reshape(128, 768)


def rep(v, n=128):
    v = np.asarray(v, np.float32).reshape(1, -1)
    return np.ascontiguousarray(np.broadcast_to(v, (n, v.shape[1])))


def build_nc(S, NSEQ, stages="0slBCD"):
    TOK = S * NSEQ
    nc = bass.Bass("TRN2", target_bir_lowering=False)
    C = Ctx(); C.nc = nc; C.S = S; C.TOK = TOK
    dt = lambda n, s, d, k: nc.dram_tensor(n, s, d, kind=k).ap()
    EI, IN = "ExternalInput", "Internal"
    C.d = dict(
        x=dt("x", [TOK, 1024], F32, EI), p=dt("p", [TOK, 256], F32, EI),
        w_in=dt("w_in", [1024, 8224], F32, EI), consts=dt("consts", [128, 768], F32, EI),
        convw_fm=dt("convw_fm", [128, 64], F32, EI), convb_fm=dt("convb_fm", [128, 16], F32, EI),
        rep_ssd=dt("rep_ssd", [128, 1072], F32, EI), rep_lstm=dt("rep_lstm", [128, 1040], F32, EI),
        rep_mid=dt("rep_mid", [128, 2184], F32, EI), rep_ln2=dt("rep_ln2", [128, 2048], F32, EI),
        w_a=dt("w_a", [1024, 1024], F32, EI), w_b=dt("w_b", [1024, 1024], F32, EI), w_out=dt("w_out", [1024, 1024], F32, EI),
        w_router=dt("w_router", [1024, 72], F32, EI),
        moe_wg=dt("moe_wg", [NE, 1024, 512], F32, EI), moe_wu=dt("moe_wu", [NE, 1024, 512], F32, EI), moe_wd=dt("moe_wd", [NE, 512, 1024], F32, EI),
        w_pg=dt("w_pg", [1024, 1024], F32, EI), w_pp=dt("w_pp", [256, 1024], F32, EI),
        xT=dt("xT", [128, 8, TOK], BF16, IN), yTs=dt("yTs", [128, 8, TOK], BF16, IN), yTl=dt("yTl", [128, 8, TOK], BF16, IN),
        h1=dt("h1", [TOK, 1024], F32, IN), xs=dt("xs", [(NE + 1) * CAP, 1024], BF16, IN), ys=dt("ys", [NE * CAP, 1024], F32, IN),
        out=dt("out", [TOK, 1024], F32, "ExternalOutput"),
    )
    with ExitStack() as es:
        C.es = es
        C.P = Prog(nc, es)
        C.sb = lambda n, s, d: es.enter_context(nc.sbuf_tensor(n, s, d))
        C.PA = es.enter_context(nc.psum_tensor("PA", [128, 1024], F32))
        C.PB = es.enter_context(nc.psum_tensor("PB", [128, 1024], F32))
        C.PC = es.enter_context(nc.psum_tensor("PC", [128, 1024], F32))
        C.PD = es.enter_context(nc.psum_tensor("PD", [128, 1024], F32))
        setup_consts(C)
        C.slots = C.sb("slots", [128, TOK // 128, 2], I32)
        C.gates = C.sb("gates", [128, TOK // 128, 2], F32)
        C.zero_xs = zero_xs
        phase0(C)
        for b in range(NSEQ):
            ssd_pass(C, b)
            lstm_pass(C, b)
        phaseB(C)
        phaseC(C)
        phaseD(C)
        C.P.finish()
        C.stats = (C.P.nops, C.P.nwaits, C.P.nsem)
    return nc, C


def shared_inputs(I):
    g = lambda k: np.asarray(I[k], np.float32)[0]
    cw = g("ssm_conv_w")
    return dict(
        w_in=g("w_in"), consts=make_consts(),
        convw_fm=np.ascontiguousarray(cw.reshape(4, 16, 128).transpose(2, 1, 0)).reshape(128, 64),
        convb_fm=np.ascontiguousarray(g("ssm_conv_b").reshape(16, 128).T),
        rep_ssd=np.concatenate([rep(g("ssm_dt_bias")), rep(g("ssm_a_log")), rep(g("ssm_d")), rep(g("ssm_norm_w"))], 1),
        rep_lstm=np.concatenate([rep(g("lstm_i_bias")), rep(g("lstm_f_bias")), rep(g("lstm_norm_w"))], 1),
        rep_mid=np.concatenate([rep(g("ln1_g")), rep(g("ln1_b")), rep(g("moe_b_group")), rep(g("moe_b_expert")), rep(np.arange(NE, dtype=np.float32) * CAP)], 1),
        rep_ln2=np.concatenate([rep(g("ln2_g")), rep(g("ln2_b"))], 1),
        w_a=g("w_branch_ssm"), w_b=g("w_branch_lstm"), w_out=g("w_out"),
        w_router=np.ascontiguousarray(np.concatenate([g("moe_w_group"), g("moe_w_expert")], 1)),
        moe_wg=g("moe_w_gate"), moe_wu=g("moe_w_up"), moe_wd=g("moe_w_down"),
        w_pg=g("ple_w_gate"), w_pp=g("ple_w_proj"),
    )


_CACHE = {}


def kernel(**inputs):
    I = {k: np.asarray(v) for k, v in inputs.items()}
    if "nc" not in _CACHE:
        _CACHE["nc"] = build_nc(2048, 2)[0]
    nc = _CACHE["nc"]
    sh = shared_inputs(I)
    x = np.asarray(I["x"], np.float32)
    p = np.asarray(I["p"], np.float32)[0]
    in_maps = []
    for c in range(8):
        in_maps.append(dict(sh, x=np.ascontiguousarray(x[2 * c:2 * c + 2].reshape(4096, 1024)),
                            p=np.ascontiguousarray(p[2 * c:2 * c + 2].reshape(4096, 256))))
    res = run_bass_kernel_spmd(nc, in_maps, core_ids=list(range(8)))
    out = np.concatenate([np.asarray(r["out"]).reshape(2, 2048, 1024) for r in res.results], 0)
    return np.ascontiguousarray(out.astype(np.float32))
```
